# Optimizing a Trainium2 kernel written in Bass

```python
import jax, jax.numpy as jnp
from jax import lax
import numpy as np

D_MODEL = 1024
BATCH = 8
SEQ = 4096
DEPTH = 4

CHUNK = 64
Q_BLOCK = 128
PLE_DIM = 256

MLA_HEADS = 16
MLA_Q_LORA = D_MODEL // 2
MLA_KV_LORA = D_MODEL // 4
MLA_NOPE = 128
MLA_ROPE = 64
MLA_V = 128
ROPE_THETA = 10000.0

GLA_HEADS = 4
GLA_QK = D_MODEL // 2
GLA_VD = D_MODEL
GLA_DK = GLA_QK // GLA_HEADS
GLA_DV = GLA_VD // GLA_HEADS
GLA_GATE_RANK = 16
GLA_TAU = 16.0

D_FF = 2816
CONV_W = 3

N_MLA = (DEPTH + 1) // 2
N_GLA = DEPTH // 2
DN_ALPHA = (2 * DEPTH) ** 0.25
DN_BETA = (8 * DEPTH) ** -0.25
EPS = 1e-5
NEG_INF = -1e30

kernel_name = "hybrid_mla_gla_deepnorm_convffn_ple"


def layer_norm(x, g, b):
    xf = x.astype(jnp.float32)
    mu = jnp.mean(xf, -1, keepdims=True)
    var = jnp.mean(jnp.square(xf - mu), -1, keepdims=True)
    return ((xf - mu) * lax.rsqrt(var + EPS) * g + b).astype(x.dtype)


def rms_norm(x, g):
    xf = x.astype(jnp.float32)
    return (xf * lax.rsqrt(jnp.mean(jnp.square(xf), -1, keepdims=True) + EPS) * g).astype(x.dtype)


def rope_tables(positions):
    inv = 1.0 / (ROPE_THETA ** (jnp.arange(0, MLA_ROPE, 2, dtype=jnp.float32) / MLA_ROPE))
    ang = positions.astype(jnp.float32)[..., None] * inv
    return jnp.cos(ang), jnp.sin(ang)


def apply_rope(x, cos, sin):
    x1, x2 = jnp.split(x, 2, axis=-1)
    c = cos.astype(x.dtype)
    s = sin.astype(x.dtype)
    return jnp.concatenate([x1 * c - x2 * s, x1 * s + x2 * c], axis=-1)


def mla_mixer(x, positions, cos, sin, w_in, q_norm, kv_norm, w_uq, w_uk, w_uv, w_o):
    B, S, _ = x.shape
    h = x @ w_in
    c_q, c_kv, k_r = jnp.split(h, [MLA_Q_LORA, MLA_Q_LORA + MLA_KV_LORA], axis=-1)
    c_q = rms_norm(c_q, q_norm)
    c_kv = rms_norm(c_kv, kv_norm)
    q = (c_q @ w_uq).reshape(B, S, MLA_HEADS, MLA_NOPE + MLA_ROPE)
    q_nope = q[..., :MLA_NOPE]
    q_rope = apply_rope(q[..., MLA_NOPE:], cos[:, :, None], sin[:, :, None])
    k_rope = apply_rope(k_r, cos, sin)
    k_nope = (c_kv @ w_uk).reshape(B, S, MLA_HEADS, MLA_NOPE)
    v = (c_kv @ w_uv).reshape(B, S, MLA_HEADS, MLA_V)
    scale = (MLA_NOPE + MLA_ROPE) ** -0.5
    chunk_id = positions // CHUNK
    outs = []
    for blk in range(S // Q_BLOCK):
        q0 = blk * Q_BLOCK
        q1 = q0 + Q_BLOCK
        s = (jnp.einsum('bqhd,bkhd->bhqk', q_nope[:, q0:q1], k_nope[:, :q1])
             + jnp.einsum('bqhr,bkr->bhqk', q_rope[:, q0:q1], k_rope[:, :q1])).astype(jnp.float32) * scale
        mask = chunk_id[:, None, :q1] <= chunk_id[:, q0:q1, None]
        s = jnp.where(mask[:, None], s, NEG_INF)
        pr = jax.nn.softmax(s, axis=-1).astype(v.dtype)
        outs.append(jnp.einsum('bhqk,bkhd->bqhd', pr, v[:, :q1]))
    o = jnp.concatenate(outs, axis=1).reshape(B, S, MLA_HEADS * MLA_V)
    return (o @ w_o).astype(x.dtype)


def gla_mixer(x, w_in, w_a2, b_a, o_norm, w_o):
    B, S, _ = x.shape
    N = S // CHUNK
    h = x @ w_in
    q, k, v, r, a = jnp.split(h, [GLA_QK, 2 * GLA_QK, 2 * GLA_QK + GLA_VD, 2 * GLA_QK + 2 * GLA_VD], axis=-1)
    log_a = jax.nn.log_sigmoid((a @ w_a2 + b_a).astype(jnp.float32)) / GLA_TAU
    q = q.reshape(B, N, CHUNK, GLA_HEADS, GLA_DK) * (GLA_DK ** -0.5)
    k = k.reshape(B, N, CHUNK, GLA_HEADS, GLA_DK)
    v = v.reshape(B, N, CHUNK, GLA_HEADS, GLA_DV)
    log_a = log_a.reshape(B, N, CHUNK, GLA_HEADS, GLA_DK)
    cum = jnp.cumsum(log_a, axis=2)
    tot = cum[:, :, -1]
    k_dec = k * jnp.exp(tot[:, :, None] - cum).astype(k.dtype)
    upd = jnp.einsum('bnchk,bnchv->nbhkv', k_dec, v).astype(jnp.float32)
    decay = jnp.exp(jnp.moveaxis(tot, 1, 0))
    q_n = jnp.moveaxis(q, 1, 0)

    def step(state, inp):
        g, u, qc = inp
        state = state * g[..., None] + u
        return state, jnp.einsum('bchk,bhkv->bchv', qc, state)

    state0 = jnp.zeros((B, GLA_HEADS, GLA_DK, GLA_DV), jnp.float32)
    _, o = lax.scan(step, state0, (decay, upd, q_n))
    o = jnp.moveaxis(o, 0, 1).reshape(B, S, GLA_HEADS, GLA_DV)
    mu = jnp.mean(o, -1, keepdims=True)
    var = jnp.mean(jnp.square(o - mu), -1, keepdims=True)
    o = (o - mu) * lax.rsqrt(var + EPS) * o_norm.reshape(GLA_HEADS, GLA_DV)
    o = o.reshape(B, S, GLA_VD).astype(x.dtype) * jax.nn.silu(r)
    return (o @ w_o).astype(x.dtype)


def conv_ffn(x, w_up, conv_w, conv_b, w_down):
    S = x.shape[1]
    h = x @ w_up
    hp = jnp.pad(h, ((0, 0), (CONV_W - 1, 0), (0, 0)))
    h = hp[:, 0:S] * conv_w[0] + hp[:, 1:S + 1] * conv_w[1] + hp[:, 2:S + 2] * conv_w[2] + conv_b
    u, g = jnp.split(h, 2, axis=-1)
    return ((u * jax.nn.gelu(g)) @ w_down).astype(x.dtype)


def setup_inputs(seed: int = 0) -> dict:
    key = jax.random.key(seed)
    ks = iter(jax.random.split(key, 40))
    f32 = jnp.float32

    def nrm(shape, scale):
        return jax.random.normal(next(ks), shape, f32) * scale

    def gain(shape):
        return 1.0 + nrm(shape, 0.01)

    x = jax.random.normal(next(ks), (BATCH, SEQ, D_MODEL), f32)
    p = jax.random.normal(next(ks), (DEPTH, BATCH, SEQ, PLE_DIM), f32)
    offsets = jax.random.randint(next(ks), (BATCH, 1), 0, 16, dtype=jnp.int32) * CHUNK
    positions = (jnp.arange(SEQ, dtype=jnp.int32)[None, :] + offsets).astype(jnp.int32)

    mla_in = MLA_Q_LORA + MLA_KV_LORA + MLA_ROPE
    gla_in = 2 * GLA_QK + 2 * GLA_VD + GLA_GATE_RANK
    return {
        "x": x,
        "p": p,
        "positions": positions,
        "mla_w_in": nrm((N_MLA, D_MODEL, mla_in), D_MODEL ** -0.5),
        "mla_q_norm": gain((N_MLA, MLA_Q_LORA)),
        "mla_kv_norm": gain((N_MLA, MLA_KV_LORA)),
        "mla_w_uq": nrm((N_MLA, MLA_Q_LORA, MLA_HEADS * (MLA_NOPE + MLA_ROPE)), MLA_Q_LORA ** -0.5),
        "mla_w_uk": nrm((N_MLA, MLA_KV_LORA, MLA_HEADS * MLA_NOPE), MLA_KV_LORA ** -0.5),
        "mla_w_uv": nrm((N_MLA, MLA_KV_LORA, MLA_HEADS * MLA_V), DN_BETA * MLA_KV_LORA ** -0.5),
        "mla_w_o": nrm((N_MLA, MLA_HEADS * MLA_V, D_MODEL), DN_BETA * (MLA_HEADS * MLA_V) ** -0.5),
        "gla_w_in": nrm((N_GLA, D_MODEL, gla_in), D_MODEL ** -0.5),
        "gla_w_a2": nrm((N_GLA, GLA_GATE_RANK, GLA_QK), GLA_GATE_RANK ** -0.5),
        "gla_b_a": nrm((N_GLA, GLA_QK), 0.1),
        "gla_o_norm": gain((N_GLA, GLA_VD)),
        "gla_w_o": nrm((N_GLA, GLA_VD, D_MODEL), DN_BETA * GLA_VD ** -0.5),
        "ln1_g": gain((DEPTH, D_MODEL)),
        "ln1_b": nrm((DEPTH, D_MODEL), 0.01),
        "ln2_g": gain((DEPTH, D_MODEL)),
        "ln2_b": nrm((DEPTH, D_MODEL), 0.01),
        "ffn_w_up": nrm((DEPTH, D_MODEL, 2 * D_FF), D_MODEL ** -0.5),
        "ffn_conv_w": nrm((DEPTH, CONV_W, 2 * D_FF), CONV_W ** -0.5),
        "ffn_conv_b": nrm((DEPTH, 2 * D_FF), 0.01),
        "ffn_w_down": nrm((DEPTH, D_FF, D_MODEL), DN_BETA * D_FF ** -0.5),
        "ple_w_proj": nrm((DEPTH, PLE_DIM, D_MODEL), PLE_DIM ** -0.5),
        "ple_w_gate": nrm((DEPTH, D_MODEL, D_MODEL), D_MODEL ** -0.5),
        "ple_b_gate": nrm((DEPTH, D_MODEL), 0.01),
    }


def reference(x, p, positions, mla_w_in, mla_q_norm, mla_kv_norm, mla_w_uq, mla_w_uk, mla_w_uv, mla_w_o,
              gla_w_in, gla_w_a2, gla_b_a, gla_o_norm, gla_w_o, ln1_g, ln1_b, ln2_g, ln2_b,
              ffn_w_up, ffn_conv_w, ffn_conv_b, ffn_w_down, ple_w_proj, ple_w_gate, ple_b_gate):
    cos, sin = rope_tables(positions)
    for i in range(DEPTH):
        j = i // 2
        if i % 2 == 0:
            m = mla_mixer(x, positions, cos, sin, mla_w_in[j], mla_q_norm[j], mla_kv_norm[j],
                          mla_w_uq[j], mla_w_uk[j], mla_w_uv[j], mla_w_o[j])
        else:
            m = gla_mixer(x, gla_w_in[j], gla_w_a2[j], gla_b_a[j], gla_o_norm[j], gla_w_o[j])
        x = layer_norm(DN_ALPHA * x + m, ln1_g[i], ln1_b[i])
        x = layer_norm(DN_ALPHA * x + conv_ffn(x, ffn_w_up[i], ffn_conv_w[i], ffn_conv_b[i], ffn_w_down[i]),
                       ln2_g[i], ln2_b[i])
        gate = jax.nn.sigmoid(x @ ple_w_gate[i] + ple_b_gate[i])
        x = x + gate * (p[i] @ ple_w_proj[i])
    return x
```

```python
import contextlib
import math
import numpy as np
import concourse.bass as bass
import concourse.mybir as mybir
from concourse.bass_utils import run_bass_kernel_spmd

F32 = mybir.dt.float32
BF16 = mybir.dt.bfloat16
I32 = mybir.dt.int32
AF = mybir.ActivationFunctionType
ALU = mybir.AluOpType

ENGS = ("pe", "act", "dve", "pool", "sp")

S = 4096
D = 1024
NT = 32
TB = 512
NBLK = 8
DFF = 2816
NFC = 22
ALPHA = 8.0 ** 0.25
EPS = 1e-5
ATT_SCALE = 192.0 ** -0.5
GLA_QS = 128.0 ** -0.5
NW = 20
TWO_PI = 2.0 * math.pi


class Buf:
    __slots__ = ("name", "lw", "rd", "sem", "cnt")

    def __init__(self, name=""):
        self.name = name
        self.lw = None
        self.rd = {}
        self.sem = None
        self.cnt = 0


class Op:
    __slots__ = ("fn", "waits", "signal", "dma", "clock")

    def __init__(self, fn):
        self.fn = fn
        self.waits = []
        self.signal = False
        self.dma = None
        self.clock = None


class Prog:
    def __init__(self, nc):
        self.nc = nc
        self.ops = {e: [] for e in ENGS}
        self.clock = {e: {} for e in ENGS}
        self.dma_bufs = []

    def _need(self, eng, tok, raw):
        ck = self.clock[eng]
        if tok[0] == 'e':
            _, E, i = tok
            if E == eng and not raw:
                return None
            if ck.get(E, -1) >= i:
                return None
            ck[E] = i
            src = self.ops[E][i]
            src.signal = True
            for k, v in src.clock.items():
                if ck.get(k, -1) < v:
                    ck[k] = v
            return tok
        _, b, c = tok
        if ck.get(b, -1) >= c:
            return None
        ck[b] = c
        return tok

    def op(self, eng, fn, reads=(), writes=(), dma=None):
        rec = Op(fn)
        lst = self.ops[eng]
        idx = len(lst)
        waits = rec.waits
        for b in reads:
            if b.lw is not None:
                w = self._need(eng, b.lw, True)
                if w:
                    waits.append(w)
        for b in writes:
            if b.lw is not None:
                w = self._need(eng, b.lw, False)
                if w:
                    waits.append(w)
            for t in b.rd.values():
                w = self._need(eng, t, False)
                if w:
                    waits.append(w)
        rec.clock = dict(self.clock[eng])
        if dma is not None:
            if dma.sem is None:
                dma.sem = True
                self.dma_bufs.append(dma)
            dma.cnt += 16
            tok = ('d', dma, dma.cnt)
            key = dma
            rec.dma = dma
        else:
            tok = ('e', eng, idx)
            key = eng
        for b in writes:
            b.lw = tok
            b.rd = {}
        for b in reads:
            b.rd[key] = tok
        lst.append(rec)
        return rec

    def barrier(self):
        toks = []
        for e in ENGS:
            lst = self.ops[e]
            for i in range(len(lst) - 1, -1, -1):
                if lst[i].dma is None and lst[i].fn is not None:
                    toks.append(('e', e, i))
                    break
        dtoks = [('d', b, b.cnt) for b in self.dma_bufs if b.cnt > 0]
        for e in ENGS:
            rec = Op(None)
            for t in toks + dtoks:
                w = self._need(e, t, True)
                if w:
                    rec.waits.append(w)
            rec.clock = dict(self.clock[e])
            self.ops[e].append(rec)

    def emit(self):
        nc = self.nc
        with contextlib.ExitStack() as st:
            esem = {e: st.enter_context(nc.semaphore("es_" + e)) for e in ENGS}
            for i, b in enumerate(self.dma_bufs):
                b.sem = st.enter_context(nc.semaphore("ds%d" % i))
            sigidx = {}
            for e in ENGS:
                c = 0
                arr = []
                for rec in self.ops[e]:
                    if rec.signal:
                        c += 1
                    arr.append(c)
                sigidx[e] = arr
            block = st.enter_context(nc.Block())

            def run(e, engobj):
                for rec in self.ops[e]:
                    for w in rec.waits:
                        if w[0] == 'e':
                            engobj.wait_ge(esem[w[1]], sigidx[w[1]][w[2]])
                        else:
                            engobj.wait_ge(w[1].sem, w[2])
                    if rec.fn is None:
                        continue
                    ins = rec.fn(engobj)
                    if rec.dma is not None:
                        ins.then_inc(rec.dma.sem, 16)
                    elif rec.signal:
                        ins.then_inc(esem[e], 1)

            @block.tensor
            def _(eng):
                run("pe", eng)

            @block.scalar
            def _(eng):
                run("act", eng)

            @block.vector
            def _(eng):
                run("dve", eng)

            @block.gpsimd
            def _(eng):
                run("pool", eng)

            @block.sync
            def _(eng):
                run("sp", eng)


class Carver:
    def __init__(self, ap):
        self.ap = ap
        self.off = 0
        self.hi = 0

    def bf(self, n):
        v = self.ap[:, self.off:self.off + n]
        self.off += n + (n & 1)
        self.hi = max(self.hi, self.off)
        assert self.off <= self.ap.shape[1], ("arena overflow", self.off, self.ap.shape)
        return v

    def f32(self, n):
        return self.bf(2 * n).bitcast(F32)

    def i32(self, n):
        return self.bf(2 * n).bitcast(I32)


def build(nlayers=4, dbg=False):
    nc = bass.Bass("TRN2", target_bir_lowering=False)
    P = Prog(nc)

    def din(name, shape, dt=F32):
        return nc.dram_tensor(name, shape, dt, kind="ExternalInput").ap()

    x_d = din("x", [S, D])
    pT_d = din("pT", [4, 256, S])
    posr_d = din("pos_row", [1, S], I32)
    posc_d = din("pos_col", [128, NT], I32)
    ropecol_d = din("ropecol", [128, 2])
    inv2_d = din("inv2", [128, 128])
    ph2_d = din("ph2", [128, 128])
    mla_w_in_d = din("mla_w_in", [2, 1024, 832])
    mla_qn_d = din("mla_q_norm", [2, 512])
    mla_kvn_d = din("mla_kv_norm", [2, 256])
    wqn_d = din("wq_n", [2, 512, 2048])
    wqr_d = din("wq_r", [2, 512, 2048])
    wuk_d = din("mla_w_uk", [2, 256, 2048])
    wuv_d = din("mla_w_uv", [2, 256, 2048])
    mla_wo_d = din("mla_w_o", [2, 2048, 1024])
    gla_w_in_d = din("gla_w_in", [2, 1024, 3088])
    gla_wa2x_d = din("gla_wa2x", [2, 17, 512])
    gla_on_d = din("gla_o_norm", [2, 1024])
    gla_wo_d = din("gla_w_o", [2, 1024, 1024])
    ln1g_d = din("ln1_g", [4, 1024])
    ln1b_d = din("ln1_b", [4, 1024])
    ln2g_d = din("ln2_g", [4, 1024])
    ln2b_d = din("ln2_b", [4, 1024])
    wup_d = din("ffn_w_up", [4, 1024, 5632])
    convp_d = din("convp", [4, 128, 176])
    wdn_d = din("ffn_w_down", [4, 2816, 1024])
    wpp_d = din("ple_w_proj", [4, 256, 1024])
    wpg_d = din("ple_w_gate", [4, 1024, 1024])
    bg_d = din("ple_b_gate", [4, 1024])
    out_d = nc.dram_tensor("out", [S, D], F32, kind="ExternalOutput").ap()
    okind = "ExternalOutput" if dbg else "Internal"
    xres_d = nc.dram_tensor("xres", [S, D], F32, kind=okind).ap()
    oT_d = nc.dram_tensor("oT", [2048, S], BF16, kind=okind).ap()
    gq_d = nc.dram_tensor("gq", [512, S], BF16, kind="Internal").ap()
    gk_d = nc.dram_tensor("gk", [S, 512], BF16, kind="Internal").ap()
    gv_d = nc.dram_tensor("gv", [S, 1024], BF16, kind="Internal").ap()
    gsr_d = nc.dram_tensor("gsr", [S, 1024], F32, kind="Internal").ap()
    B_xres, B_oT, B_gq, B_gk, B_gv, B_gsr, B_out = (Buf(n) for n in ("xres", "oT", "gq", "gk", "gv", "gsr", "out"))

    st = contextlib.ExitStack()

    def sbt(name, cols, dt):
        return st.enter_context(nc.sbuf_tensor(name, [128, cols], dt))[:]

    ident_f = sbt("ident_f", 128, F32); B_identf = Buf()
    ident_b = sbt("ident_b", 128, BF16); B_identb = Buf()
    ones_f = sbt("ones_f", 128, F32); B_ones = Buf()
    tri_f = sbt("tri_f", 128, F32); B_tri = Buf()
    ind_f = sbt("ind_f", 2, F32); B_ind = Buf()
    T2 = sbt("T2", S, BF16); B_T2 = Buf()
    MASK = sbt("MASK", S, BF16); B_MASK = Buf()
    cqT = sbt("cqT", 4 * S, BF16)
    ckvT = sbt("ckvT", 2 * S, BF16)
    kr2T = sbt("kr2T", S, BF16)
    B_cq = [Buf() for _ in range(NBLK)]
    B_ckv = [Buf() for _ in range(NBLK)]
    B_kr = [Buf() for _ in range(NBLK)]
    posc_i = sbt("posc_i", NT, I32); B_posci = Buf()
    posc_f = sbt("posc_f", NT, F32); B_poscf = Buf()
    cidc_f = sbt("cidc_f", NT, F32); B_cidc = Buf()
    ropecol = sbt("ropecol_s", 2, F32); B_ropecol = Buf()
    inv2 = sbt("inv2_s", 128, F32); B_inv2 = Buf()
    ph2 = sbt("ph2_s", 128, F32); B_ph2 = Buf()
    convp = sbt("convp_s", 176, F32); B_convp = Buf()
    gq_b = sbt("gq_b", 512, F32); B_gqb = Buf()
    gkv_b = sbt("gkv_b", 256, F32); B_gkvb = Buf()
    DECAY = sbt("DECAY", 256, F32); B_decay = Buf()
    wa2x = sbt("wa2x", 512, BF16); B_wa2x = Buf()
    AT = sbt("AT", 512, BF16); B_AT = Buf()
    HALO = [sbt("HALO%d" % i, 88, F32) for i in range(2)]; B_halo = [Buf(), Buf()]
    CORR = sbt("CORR", 88, F32); B_corr = Buf()
    smalls = sbt("smalls", 64, F32)
    wslots = [(sbt("w%d" % i, 512, BF16), Buf("w%d" % i)) for i in range(NW)]
    wctr = [0]
    arena_cols = (nc.sbuf_bytes_remaining - 1024) // 2
    arena_cols -= arena_cols % 64
    arena = sbt("arena", arena_cols, BF16)
    banks = [st.enter_context(nc.psum_tensor("bank%d" % i, [128, 512], F32))[:] for i in range(8)]
    banks_bf = [b.bitcast(BF16) for b in banks]
    B_bank = [Buf("bank%d" % i) for i in range(8)]

    def mm(out, lhsT, rhs, start, stop, R, W):
        P.op("pe", lambda e: e.matmul(out, lhsT=lhsT, rhs=rhs, start=start, stop=stop), R, W)

    def tr(out, in_, ident, R, W):
        P.op("pe", lambda e: e.transpose(out, in_, ident), R, W)

    def act(out, in_, func, R, W, scale=1.0, bias=0.0, accum=None):
        if accum is None:
            P.op("act", lambda e: e.activation(out=out, in_=in_, func=func, bias=bias, scale=scale), R, W)
        else:
            P.op("act", lambda e: e.activation(out=out, in_=in_, func=func, bias=bias, scale=scale, accum_out=accum), R, W)

    def ts(eng, out, in0, s1, s2, op0, op1, R, W):
        if s2 is None:
            P.op(eng, lambda e: e.tensor_scalar(out=out, in0=in0, scalar1=s1, scalar2=None, op0=op0), R, W)
        else:
            P.op(eng, lambda e: e.tensor_scalar(out=out, in0=in0, scalar1=s1, scalar2=s2, op0=op0, op1=op1), R, W)

    def tt(eng, out, in0, in1, op, R, W):
        P.op(eng, lambda e: e.tensor_tensor(out=out, in0=in0, in1=in1, op=op), R, W)

    def stt(out, in0, scalar, in1, op0, op1, R, W):
        P.op("dve", lambda e: e.scalar_tensor_tensor(out=out, in0=in0, scalar=scalar, in1=in1, op0=op0, op1=op1), R, W)

    def cp(eng, out, in_, R, W):
        P.op(eng, lambda e: e.tensor_copy(out=out, in_=in_), R, W)

    def memset(eng, ap, val, W):
        P.op(eng, lambda e: e.memset(ap, val), (), W)

    def dma(eng, out, in_, R, W, dbuf):
        P.op(eng, lambda e: e.dma_start(out=out, in_=in_), R, W, dma=dbuf)

    def wt(src, rows=128, cols=512):
        ap, b = wslots[wctr[0] % NW]
        wctr[0] += 1
        v = ap[0:rows, 0:cols]
        dma("pool", v, src, (), [b], b)
        return v, b

    def v3(ap, a):
        return ap.rearrange("p (a b) -> p a b", a=a)

    memset("pool", ident_f, 1.0, [B_identf])
    P.op("pool", lambda e: e.affine_select(out=ident_f, in_=ident_f, pattern=[[-1, 128]], compare_op=ALU.is_equal,
                                           fill=0.0, base=0, channel_multiplier=1), [B_identf], [B_identf])
    cp("dve", ident_b, ident_f, [B_identf], [B_identb])
    memset("pool", ones_f, 1.0, [B_ones])
    memset("pool", tri_f, 1.0 / 16.0, [B_tri])
    P.op("pool", lambda e: e.affine_select(out=tri_f, in_=tri_f, pattern=[[-1, 128]], compare_op=ALU.is_gt,
                                           fill=0.0, base=0, channel_multiplier=1), [B_tri], [B_tri])
    memset("pool", tri_f[64:128, 0:64], 0.0, [B_tri])
    memset("pool", ind_f, 0.0, [B_ind])
    memset("pool", ind_f[0:64, 0:1], 1.0 / 16.0, [B_ind])
    memset("pool", ind_f[64:128, 1:2], 1.0 / 16.0, [B_ind])
    memset("pool", AT[0:32, :], 1.0, [B_AT])
    dma("sp", posc_i, posc_d, (), [B_posci], B_posci)
    dma("sp", ropecol, ropecol_d, (), [B_ropecol], B_ropecol)
    dma("sp", inv2, inv2_d, (), [B_inv2], B_inv2)
    dma("sp", ph2, ph2_d, (), [B_ph2], B_ph2)
    cp("dve", posc_f, posc_i, [B_posci], [B_poscf])
    cv = Carver(arena)
    R0 = cv.bf(2 * S); R1 = cv.bf(2 * S); R2 = cv.bf(2 * S); R3 = cv.bf(2 * S)
    B_R = [Buf() for _ in range(4)]
    tmpi = smalls[:, 0:32].bitcast(I32); B_tmpi = Buf()
    ts("dve", tmpi, posc_i, 6, None, ALU.arith_shift_right, None, [B_posci], [B_tmpi])
    cp("dve", cidc_f, tmpi, [B_tmpi], [B_cidc])
    dma("sp", R0.bitcast(I32), posr_d.partition_broadcast(128), (), [B_R[0]], B_R[0])
    cp("dve", R1.bitcast(F32), R0.bitcast(I32), [B_R[0]], [B_R[1]])
    ts("dve", R2.bitcast(I32), R0.bitcast(I32), 6, None, ALU.arith_shift_right, None, [B_R[0]], [B_R[2]])
    cp("dve", R0.bitcast(F32), R2.bitcast(I32), [B_R[2]], [B_R[0]])
    for b in range(NT):
        ts("dve", MASK[:, b * 128:(b + 1) * 128], R0.bitcast(F32)[:, b * 128:(b + 1) * 128], cidc_f[:, b:b + 1], None,
           ALU.is_ge, None, [B_R[0], B_cidc], [B_MASK])
    ts("dve", R2.bitcast(F32), R1.bitcast(F32), ropecol[:, 0:1], ropecol[:, 1:2], ALU.mult, ALU.add, [B_R[1], B_ropecol], [B_R[2]])
    cp("dve", R1.bitcast(I32), R2.bitcast(F32), [B_R[2]], [B_R[1]])
    cp("dve", R3.bitcast(F32), R1.bitcast(I32), [B_R[1]], [B_R[3]])
    tt("dve", R2.bitcast(F32), R2.bitcast(F32), R3.bitcast(F32), ALU.subtract, [B_R[2], B_R[3]], [B_R[2]])
    stt(R2.bitcast(F32), R2.bitcast(F32), 0.5, R2.bitcast(F32), ALU.is_gt, ALU.subtract, [B_R[2]], [B_R[2]])
    act(T2, R2.bitcast(F32), AF.Sin, [B_R[2]], [B_T2], scale=-TWO_PI)
    P.barrier()

    def tail_layout(with_cq_live):
        cv = Carver(arena)
        L = {}
        L["VEC"] = [cv.f32(1024) for _ in range(2)]
        L["XA"] = cv.f32(4 * 1024)
        L["XBF"] = cv.bf(1024)
        L["XT"] = cv.bf(8 * 512)
        L["OTB"] = cv.bf(16 * 512)
        L["ACTT"] = cv.bf(22 * 512)
        L["PTB"] = cv.bf(2 * 512)
        L["TMP"] = cv.f32(1024)
        L["FU"] = cv.f32(512)
        L["FG"] = cv.f32(512)
        L["SQJ"] = cv.bf(512)
        L["CQN"] = cv.bf(768)
        L["KR"] = cv.bf(128)
        L["CS"] = cv.f32(512)
        L["CSI"] = cv.i32(512)
        L["CSF"] = cv.f32(512)
        L["XC"] = cv.f32(64)
        L["XS"] = cv.f32(64)
        L["ST"] = cv.f32(64)
        if not with_cq_live:
            cv = Carver(cqT)
            L["LA"] = cv.f32(512)
            L["AZ"] = cv.f32(512)
            L["DEC"] = cv.f32(512)
            L["KD"] = cv.bf(512)
            L["VB"] = cv.bf(1024)
            L["SR"] = cv.f32(1024)
            L["QTS"] = cv.bf(512)
        for k in list(L.keys()):
            if k == "VEC":
                L["B_VEC"] = [Buf() for _ in range(2)]
            else:
                L["B_" + k] = Buf(k)
        L["B_XAt"] = [Buf() for _ in range(4)]
        L["B_XTt"] = [Buf() for _ in range(4)]
        return L

    def to_featmajor(L, tt_i, tbank):
        XAt = L["XA"][:, tt_i * 1024:(tt_i + 1) * 1024]
        act(L["XBF"], XAt, AF.Copy, [L["B_XAt"][tt_i]], [L["B_XBF"]])
        for kc in range(8):
            tr(banks_bf[tbank][:, kc * 128:(kc + 1) * 128], L["XBF"][:, kc * 128:(kc + 1) * 128], ident_b,
               [L["B_XBF"], B_identb], [B_bank[tbank]])
        cp("dve", v3(L["XT"], 8)[:, :, tt_i * 128:(tt_i + 1) * 128], v3(banks_bf[tbank], 8),
           [B_bank[tbank]], [L["B_XTt"][tt_i]])

    def project_mla(L, j, blk):
        XT = L["XT"]
        RX = L["B_XTt"]
        for kc in range(8):
            w, wb = wt(mla_w_in_d[j, kc * 128:(kc + 1) * 128, 0:512])
            for t4 in range(4):
                mm(banks[t4], XT[:, kc * 512 + t4 * 128: kc * 512 + (t4 + 1) * 128], w, kc == 0, kc == 7,
                   [RX[t4], wb], [B_bank[t4]])
        for kc in range(8):
            w, wb = wt(mla_w_in_d[j, kc * 128:(kc + 1) * 128, 512:832], cols=320)
            for t4 in range(4):
                mm(banks[4 + t4][:, 0:320], XT[:, kc * 512 + t4 * 128: kc * 512 + (t4 + 1) * 128], w, kc == 0, kc == 7,
                   [RX[t4], wb], [B_bank[4 + t4]])
        CS, CSI, CSF = L["CS"], L["CSI"], L["CSF"]
        for t4 in range(4):
            t = blk * 4 + t4
            stt(CS[:, t4 * 128:(t4 + 1) * 128], inv2, posc_f[:, t:t + 1], ph2, ALU.mult, ALU.add,
                [B_inv2, B_poscf, B_ph2], [L["B_CS"]])
        cp("dve", CSI, CS, [L["B_CS"]], [L["B_CSI"]])
        cp("dve", CSF, CSI, [L["B_CSI"]], [L["B_CSF"]])
        tt("dve", CS, CS, CSF, ALU.subtract, [L["B_CS"], L["B_CSF"]], [L["B_CS"]])
        stt(CS, CS, 0.5, CS, ALU.is_gt, ALU.subtract, [L["B_CS"]], [L["B_CS"]])
        act(CS, CS, AF.Sin, [L["B_CS"]], [L["B_CS"]], scale=-TWO_PI)
        ST = L["ST"]
        for t4 in range(4):
            t = blk * 4 + t4
            bq, bk = banks[t4], banks[4 + t4]
            act(L["SQJ"], bq, AF.Square, [B_bank[t4]], [L["B_SQJ"], L["B_ST"]], scale=512.0 ** -0.5, accum=ST[:, 0:1])
            act(L["SQJ"][:, 0:256], bk[:, 0:256], AF.Square, [B_bank[4 + t4]], [L["B_SQJ"], L["B_ST"]], scale=256.0 ** -0.5,
                accum=ST[:, 1:2])
            ts("dve", ST[:, 2:4], ST[:, 0:2], EPS, None, ALU.add, None, [L["B_ST"]], [L["B_ST"]])
            act(ST[:, 4:6], ST[:, 2:4], AF.Sqrt, [L["B_ST"]], [L["B_ST"]])
            P.op("dve", lambda e: e.reciprocal(out=ST[:, 6:8], in_=ST[:, 4:6]), [L["B_ST"]], [L["B_ST"]])
            CQN = L["CQN"]
            stt(CQN[:, 0:512], bq, ST[:, 6:7], gq_b, ALU.mult, ALU.mult, [B_bank[t4], L["B_ST"], B_gqb], [L["B_CQN"]])
            stt(CQN[:, 512:768], bk[:, 0:256], ST[:, 7:8], gkv_b, ALU.mult, ALU.mult, [B_bank[4 + t4], L["B_ST"], B_gkvb], [L["B_CQN"]])
            XC, XS, KR = L["XC"], L["XS"], L["KR"]
            tt("dve", XC, bk[:, 256:320], CS[:, t4 * 128: t4 * 128 + 64], ALU.mult, [B_bank[4 + t4], L["B_CS"]], [L["B_XC"]])
            tt("dve", XS, bk[:, 256:320], CS[:, t4 * 128 + 64: t4 * 128 + 128], ALU.mult, [B_bank[4 + t4], L["B_CS"]], [L["B_XS"]])
            tt("dve", KR[:, 0:32], XC[:, 0:32], XS[:, 32:64], ALU.subtract, [L["B_XC"], L["B_XS"]], [L["B_KR"]])
            tt("dve", KR[:, 32:64], XS[:, 0:32], XC[:, 32:64], ALU.add, [L["B_XC"], L["B_XS"]], [L["B_KR"]])
            cp("dve", KR[:, 64:128], KR[:, 0:64], [L["B_KR"]], [L["B_KR"]])
            tb_ = banks_bf[t4]
            for kc in range(6):
                tr(tb_[:, kc * 128:(kc + 1) * 128], CQN[:, kc * 128:(kc + 1) * 128], ident_b, [L["B_CQN"], B_identb], [B_bank[t4]])
            tr(tb_[:, 768:896], KR, ident_b, [L["B_KR"], B_identb], [B_bank[t4]])
            act(v3(cqT, 4)[:, :, t * 128:(t + 1) * 128], v3(tb_[:, 0:512], 4), AF.Copy, [B_bank[t4]], [B_cq[blk]])
            act(v3(ckvT, 2)[:, :, t * 128:(t + 1) * 128], v3(tb_[:, 512:768], 2), AF.Copy, [B_bank[t4]], [B_ckv[blk]])
            act(kr2T[:, t * 128:(t + 1) * 128], tb_[:, 768:896], AF.Copy, [B_bank[t4]], [B_kr[blk]])

    def project_gla(L, j, blk):
        XT = L["XT"]
        RX = L["B_XTt"]
        win = gla_w_in_d[j]
        for kc in range(8):
            w, wb = wt(win[kc * 128:(kc + 1) * 128, 0:512])
            for h in range(4):
                mm(banks[h], w[:, h * 128:(h + 1) * 128], XT[:, kc * 512:(kc + 1) * 512], kc == 0, kc == 7, RX + [wb], [B_bank[h]])
        for h in range(4):
            act(L["QTS"], banks[h], AF.Copy, [B_bank[h]], [L["B_QTS"]], scale=GLA_QS)
            dma("sp", gq_d[h * 128:(h + 1) * 128, blk * 512:(blk + 1) * 512], L["QTS"], [L["B_QTS"]], [B_gq], B_gq)
        for kc in range(8):
            w, wb = wt(win[kc * 128:(kc + 1) * 128, 3072:3088], cols=16)
            mm(banks[4][0:16, :], w, XT[:, kc * 512:(kc + 1) * 512], kc == 0, kc == 7, RX + [wb], [B_bank[4]])
        act(AT[0:16, :], banks[4][0:16, :], AF.Copy, [B_bank[4]], [B_AT])
        for kc in range(8):
            w, wb = wt(win[kc * 128:(kc + 1) * 128, 512:1024])
            for t4 in range(4):
                mm(banks[t4], XT[:, kc * 512 + t4 * 128: kc * 512 + (t4 + 1) * 128], w, kc == 0, kc == 7, [RX[t4], wb], [B_bank[t4]])
        LA, AZ, DEC, KD = L["LA"], L["AZ"], L["DEC"], L["KD"]
        for t4 in range(4):
            t = blk * 4 + t4
            zb = 5
            mm(banks[zb], AT[0:17, t4 * 128:(t4 + 1) * 128], wa2x[0:17, :], True, True, [B_AT, B_wa2x], [B_bank[zb]])
            act(DEC, banks[zb], AF.Copy, [B_bank[zb]], [L["B_DEC"]])
            stt(AZ, DEC, -1.0, DEC, ALU.mult, ALU.max, [L["B_DEC"]], [L["B_AZ"]])
            act(AZ, AZ, AF.Exp, [L["B_AZ"]], [L["B_AZ"]], scale=-1.0)
            act(AZ, AZ, AF.Ln, [L["B_AZ"]], [L["B_AZ"]], bias=1.0)
            stt(LA, DEC, 0.0, AZ, ALU.min, ALU.subtract, [L["B_DEC"], L["B_AZ"]], [L["B_LA"]])
            mm(banks[6], tri_f, LA, True, True, [B_tri, L["B_LA"]], [B_bank[6]])
            for h in range(4):
                mm(banks[7][:, 2 * h:2 * h + 2], LA[:, h * 128:(h + 1) * 128], ind_f, True, True, [L["B_LA"], B_ind], [B_bank[7]])
            act(DEC, banks[6], AF.Exp, [B_bank[6]], [L["B_DEC"]])
            act(v3(DECAY, 4)[:, :, 2 * t:2 * t + 2], v3(banks[7][:, 0:8], 4), AF.Exp, [B_bank[7]], [B_decay])
            tt("dve", KD, banks[t4], DEC, ALU.mult, [B_bank[t4], L["B_DEC"]], [L["B_KD"]])
            dma("sp", gk_d[t * 128:(t + 1) * 128, :], KD, [L["B_KD"]], [B_gk], B_gk)
        for cg in range(4):
            c0 = 1024 + cg * 512
            bs = 4 * (cg % 2)
            for kc in range(8):
                w, wb = wt(win[kc * 128:(kc + 1) * 128, c0:c0 + 512])
                for t4 in range(4):
                    mm(banks[bs + t4], XT[:, kc * 512 + t4 * 128: kc * 512 + (t4 + 1) * 128], w, kc == 0, kc == 7,
                       [RX[t4], wb], [B_bank[bs + t4]])
            for t4 in range(4):
                t = blk * 4 + t4
                if cg < 2:
                    act(L["VB"][:, 0:512], banks[bs + t4], AF.Copy, [B_bank[bs + t4]], [L["B_VB"]])
                    dma("sp", gv_d[t * 128:(t + 1) * 128, cg * 512:(cg + 1) * 512], L["VB"][:, 0:512], [L["B_VB"]], [B_gv], B_gv)
                else:
                    act(L["SR"][:, 0:512], banks[bs + t4], AF.Silu, [B_bank[bs + t4]], [L["B_SR"]])
                    dma("sp", gsr_d[t * 128:(t + 1) * 128, (cg - 2) * 512:(cg - 1) * 512], L["SR"][:, 0:512], [L["B_SR"]], [B_gsr], B_gsr)

    def load_mla_vecs(j):
        dma("sp", gq_b, mla_qn_d[j:j + 1, :].partition_broadcast(128), (), [B_gqb], B_gqb)
        dma("sp", gkv_b, mla_kvn_d[j:j + 1, :].partition_broadcast(128), (), [B_gkvb], B_gkvb)

    def load_gla_vecs(j):
        dma("pool", wa2x[0:17, :], gla_wa2x_d[j], (), [B_wa2x], B_wa2x)

    def layer_norm(L, t4, gi, bi):
        XAt = L["XA"][:, t4 * 1024:(t4 + 1) * 1024]
        BX = L["B_XAt"][t4]
        ST = L["ST"]
        BS = L["B_ST"]
        P.op("dve", lambda e: e.bn_stats(out=ST[:, 8:14], in_=XAt[:, 0:512]), [BX], [BS])
        P.op("dve", lambda e: e.bn_stats(out=ST[:, 14:20], in_=XAt[:, 512:1024]), [BX], [BS])
        P.op("dve", lambda e: e.bn_aggr(out=ST[:, 20:22], in_=ST[:, 8:20]), [BS], [BS])
        ts("dve", ST[:, 22:23], ST[:, 21:22], EPS, None, ALU.add, None, [BS], [BS])
        act(ST[:, 23:24], ST[:, 22:23], AF.Sqrt, [BS], [BS])
        P.op("dve", lambda e: e.reciprocal(out=ST[:, 24:25], in_=ST[:, 23:24]), [BS], [BS])
        ts("dve", XAt, XAt, ST[:, 20:21], ST[:, 24:25], ALU.subtract, ALU.mult, [BX, BS], [BX])
        tt("pool", XAt, XAt, L["VEC"][gi], ALU.mult, [BX, L["B_VEC"][gi]], [BX])
        tt("pool", XAt, XAt, L["VEC"][bi], ALU.add, [BX, L["B_VEC"][bi]], [BX])

    def tail(i, nk, wo_d, nxt):
        L = tail_layout(not (nxt is not None and nxt[0] == 'gla'))
        XA, XT, OTB, ACTT, PTB, TMP, FU, FG = (L[k] for k in ("XA", "XT", "OTB", "ACTT", "PTB", "TMP", "FU", "FG"))
        def load_vec(slot, src):
            dma("sp", L["VEC"][slot], src[i:i + 1, :].partition_broadcast(128), (), [L["B_VEC"][slot]], L["B_VEC"][slot])
        dma("sp", convp, convp_d[i], (), [B_convp], B_convp)
        memset("dve", HALO[0], 0.0, [B_halo[0]])
        if nxt is not None and nxt[0] == 'mla':
            load_mla_vecs(nxt[1])
        if nxt is not None and nxt[0] == 'gla':
            load_gla_vecs(nxt[1])
        xsrc = x_d if i == 0 else xres_d
        cw = v3(convp, 44)
        for blk in range(NBLK):
            hp, hn = HALO[blk % 2], HALO[(blk + 1) % 2]
            Bhp, Bhn = B_halo[blk % 2], B_halo[(blk + 1) % 2]
            t0 = blk * 4
            dma("sp", v3(OTB[:, 0:nk * 512], nk), oT_d[0:nk * 128, blk * 512:(blk + 1) * 512].rearrange("(k p) s -> p k s", p=128),
                [B_oT], [L["B_OTB"]], L["B_OTB"])
            for t4 in range(4):
                dma("sp", XA[:, t4 * 1024:(t4 + 1) * 1024], xsrc[(t0 + t4) * 128:(t0 + t4 + 1) * 128, :],
                    [B_xres] if i > 0 else [], [L["B_XAt"][t4]], L["B_XAt"][t4])
            dma("pool", v3(PTB, 2), pT_d[i, :, blk * 512:(blk + 1) * 512].rearrange("(k p) s -> p k s", p=128), (), [L["B_PTB"]], L["B_PTB"])
            for half in range(2):
                for kc in range(nk):
                    w, wb = wt(wo_d[kc * 128:(kc + 1) * 128, half * 512:(half + 1) * 512])
                    for t4 in range(4):
                        mm(banks[4 * half + t4], OTB[:, kc * 512 + t4 * 128: kc * 512 + (t4 + 1) * 128], w, kc == 0, kc == nk - 1,
                           [L["B_OTB"], wb], [B_bank[4 * half + t4]])
            load_vec(0, ln1g_d)
            load_vec(1, ln1b_d)
            for t4 in range(4):
                for half in range(2):
                    xs = XA[:, t4 * 1024 + half * 512: t4 * 1024 + (half + 1) * 512]
                    stt(xs, xs, ALPHA, banks[4 * half + t4], ALU.mult, ALU.add, [L["B_XAt"][t4], B_bank[4 * half + t4]], [L["B_XAt"][t4]])
                layer_norm(L, t4, 0, 1)
                to_featmajor(L, t4, t4)
            hv = v3(hp, 44)
            cr = v3(CORR, 44)
            tmpa = L["ST"][:, 0:44]
            ta = FU[:, 0:44]
            tb2 = FU[:, 64:108]
            tt("dve", ta, hv[:, :, 1], cw[:, :, 1], ALU.mult, [Bhp, B_convp], [L["B_FU"]])
            tt("dve", tb2, hv[:, :, 0], cw[:, :, 0], ALU.mult, [Bhp, B_convp], [L["B_FU"]])
            tt("dve", cr[:, :, 0], ta, tb2, ALU.add, [L["B_FU"]], [B_corr])
            tt("dve", cr[:, :, 1], hv[:, :, 1], cw[:, :, 0], ALU.mult, [Bhp, B_convp], [B_corr])
            for g4 in range(6):
                ncg = 4 if g4 < 5 else 2
                wu = [wt(wup_d[i, kc * 128:(kc + 1) * 128, g4 * 512: g4 * 512 + ncg * 128], cols=ncg * 128) for kc in range(8)]
                wg = [wt(wup_d[i, kc * 128:(kc + 1) * 128, DFF + g4 * 512: DFF + g4 * 512 + ncg * 128], cols=ncg * 128) for kc in range(8)]
                for cc in range(ncg):
                    c = g4 * 4 + cc
                    bu, bg = (0, 1) if c % 2 == 0 else (2, 3)
                    for kc in range(8):
                        mm(banks[bu], wu[kc][0][:, cc * 128:(cc + 1) * 128], XT[:, kc * 512:(kc + 1) * 512], kc == 0, kc == 7,
                           L["B_XTt"] + [wu[kc][1]], [B_bank[bu]])
                    for kc in range(8):
                        mm(banks[bg], wg[kc][0][:, cc * 128:(cc + 1) * 128], XT[:, kc * 512:(kc + 1) * 512], kc == 0, kc == 7,
                           L["B_XTt"] + [wg[kc][1]], [B_bank[bg]])
                    for (bk_, ci, Fo, BFo) in ((bu, c, FU, L["B_FU"]), (bg, 22 + c, FG, L["B_FG"])):
                        pb = banks[bk_]
                        act(Fo, pb, AF.Identity, [B_bank[bk_], B_convp], [BFo], scale=cw[:, ci, 2:3], bias=cw[:, ci, 3:4])
                        stt(Fo[:, 1:512], pb[:, 0:511], cw[:, ci, 1:2], Fo[:, 1:512], ALU.mult, ALU.add, [B_bank[bk_], B_convp, BFo], [BFo])
                        stt(Fo[:, 2:512], pb[:, 0:510], cw[:, ci, 0:1], Fo[:, 2:512], ALU.mult, ALU.add, [B_bank[bk_], B_convp, BFo], [BFo])
                        tt("dve", Fo[:, 0:2], Fo[:, 0:2], cr[:, ci, :], ALU.add, [BFo, B_corr], [BFo])
                        act(v3(hn, 44)[:, ci, :], pb[:, 510:512], AF.Copy, [B_bank[bk_]], [Bhn])
                    act(FG, FG, AF.Gelu, [L["B_FG"]], [L["B_FG"]])
                    tt("pool", ACTT[:, c * 512:(c + 1) * 512], FU, FG, ALU.mult, [L["B_FU"], L["B_FG"]], [L["B_ACTT"]])
            for half in range(2):
                bs = 4 if half == 0 else 0
                for c in range(NFC):
                    w, wb = wt(wdn_d[i, c * 128:(c + 1) * 128, half * 512:(half + 1) * 512])
                    for t4 in range(4):
                        mm(banks[bs + t4], ACTT[:, c * 512 + t4 * 128: c * 512 + (t4 + 1) * 128], w, c == 0, c == NFC - 1,
                           [L["B_ACTT"], wb], [B_bank[bs + t4]])
            load_vec(0, ln2g_d)
            load_vec(1, ln2b_d)
            for t4 in range(4):
                for half in range(2):
                    bs = 4 if half == 0 else 0
                    xs = XA[:, t4 * 1024 + half * 512: t4 * 1024 + (half + 1) * 512]
                    stt(xs, xs, ALPHA, banks[bs + t4], ALU.mult, ALU.add, [L["B_XAt"][t4], B_bank[bs + t4]], [L["B_XAt"][t4]])
                layer_norm(L, t4, 0, 1)
                to_featmajor(L, t4, t4)
            load_vec(0, bg_d)
            for half in range(2):
                for kc in range(8):
                    w, wb = wt(wpg_d[i, kc * 128:(kc + 1) * 128, half * 512:(half + 1) * 512])
                    for t4 in range(4):
                        mm(banks[t4], XT[:, kc * 512 + t4 * 128: kc * 512 + (t4 + 1) * 128], w, kc == 0, kc == 7,
                           [L["B_XTt"][t4], wb], [B_bank[t4]])
                for kc in range(2):
                    w, wb = wt(wpp_d[i, kc * 128:(kc + 1) * 128, half * 512:(half + 1) * 512])
                    for t4 in range(4):
                        mm(banks[4 + t4], PTB[:, kc * 512 + t4 * 128: kc * 512 + (t4 + 1) * 128], w, kc == 0, kc == 1,
                           [L["B_PTB"], wb], [B_bank[4 + t4]])
                for t4 in range(4):
                    G = TMP[:, 0:512]
                    tt("dve", G, banks[t4], L["VEC"][0][:, half * 512:(half + 1) * 512], ALU.add, [B_bank[t4], L["B_VEC"][0]], [L["B_TMP"]])
                    act(G, G, AF.Sigmoid, [L["B_TMP"]], [L["B_TMP"]])
                    tt("dve", G, G, banks[4 + t4], ALU.mult, [L["B_TMP"], B_bank[4 + t4]], [L["B_TMP"]])
                    xs = XA[:, t4 * 1024 + half * 512: t4 * 1024 + (half + 1) * 512]
                    tt("pool", xs, xs, G, ALU.add, [L["B_XAt"][t4], L["B_TMP"]], [L["B_XAt"][t4]])
            dst = out_d if nxt is None else xres_d
            Bd = B_out if nxt is None else B_xres
            for t4 in range(4):
                dma("sp", dst[(t0 + t4) * 128:(t0 + t4 + 1) * 128, :], XA[:, t4 * 1024:(t4 + 1) * 1024], [L["B_XAt"][t4]], [Bd], Bd)
            if nxt is not None:
                for t4 in range(4):
                    to_featmajor(L, t4, t4)
                if nxt[0] == 'mla':
                    project_mla(L, nxt[1], blk)
                else:
                    project_gla(L, nxt[1], blk)
        P.barrier()

    def prologue():
        L = tail_layout(True)
        load_mla_vecs(0)
        for blk in range(NBLK):
            for t4 in range(4):
                t = blk * 4 + t4
                dma("sp", L["XA"][:, t4 * 1024:(t4 + 1) * 1024], x_d[t * 128:(t + 1) * 128, :], (), [L["B_XAt"][t4]], L["B_XAt"][t4])
            for t4 in range(4):
                to_featmajor(L, t4, t4)
            project_mla(L, 0, blk)
        P.barrier()

    def mla_attention(j):
        cv = Carver(arena)
        QN = [cv.bf(S) for _ in range(2)]; QR = [cv.bf(S) for _ in range(2)]; KN = [cv.bf(S) for _ in range(2)]
        B_QN = [[Buf() for _ in range(NBLK)] for _ in range(2)]
        B_QR = [[Buf() for _ in range(NBLK)] for _ in range(2)]
        B_KN = [[Buf() for _ in range(NBLK)] for _ in range(2)]
        V4 = cv.bf(NT * 512); B_V4 = [Buf() for _ in range(NT)]
        PT = [cv.bf(512) for _ in range(4)]; B_PT = [Buf() for _ in range(4)]
        ACCp = [cv.f32(512) for _ in range(2)]; B_ACCp = [Buf(), Buf()]
        ACCd = [cv.f32(512) for _ in range(2)]; B_ACCd = [Buf(), Buf()]
        RS = cv.f32(512); B_RS = Buf()
        OTs = [cv.bf(512) for _ in range(2)]; B_OTs = [Buf(), Buf()]
        ptc = 0
        sbc = 0
        for g in range(4):
            cs = slice(g * 512, (g + 1) * 512)
            WQN = [wt(wqn_d[j, kc * 128:(kc + 1) * 128, cs]) for kc in range(4)]
            WQR = [wt(wqr_d[j, kc * 128:(kc + 1) * 128, cs]) for kc in range(4)]
            WUK = [wt(wuk_d[j, kc * 128:(kc + 1) * 128, cs]) for kc in range(2)]
            WUV = [wt(wuv_d[j, kc * 128:(kc + 1) * 128, cs]) for kc in range(2)]
            for t in range(NT):
                pb = 6 + (t % 2)
                for kc in range(2):
                    mm(banks[pb], ckvT[:, kc * S + t * 128: kc * S + (t + 1) * 128], WUV[kc][0], kc == 0, kc == 1,
                       [B_ckv[t // 4], WUV[kc][1]], [B_bank[pb]])
                act(V4[:, t * 512:(t + 1) * 512], banks[pb], AF.Copy, [B_bank[pb]], [B_V4[t]])
            for hh in range(4):
                h = g * 4 + hh
                par = h % 2
                hs = slice(hh * 128, (hh + 1) * 128)
                for tb in range(NBLK):
                    tsl = slice(tb * 512, (tb + 1) * 512)
                    for kc in range(4):
                        mm(banks[6], WQN[kc][0][:, hs], cqT[:, kc * S + tb * 512: kc * S + (tb + 1) * 512], kc == 0, kc == 3,
                           [B_cq[tb], WQN[kc][1]], [B_bank[6]])
                    act(QN[par][:, tsl], banks[6], AF.Copy, [B_bank[6]], [B_QN[par][tb]])
                    for kc in range(4):
                        mm(banks[7], WQR[kc][0][:, hs], cqT[:, kc * S + tb * 512: kc * S + (tb + 1) * 512], kc == 0, kc == 3,
                           [B_cq[tb], WQR[kc][1]], [B_bank[7]])
                    tt("dve", QR[par][:, tsl], banks[7], T2[:, tsl], ALU.mult, [B_bank[7], B_T2], [B_QR[par][tb]])
                    for kc in range(2):
                        mm(banks[6], WUK[kc][0][:, hs], ckvT[:, kc * S + tb * 512: kc * S + (tb + 1) * 512], kc == 0, kc == 1,
                           [B_ckv[tb], WUK[kc][1]], [B_bank[6]])
                    act(KN[par][:, tsl], banks[6], AF.Copy, [B_bank[6]], [B_KN[par][tb]])
                for qt in range(NBLK):
                    ob = 3 + (qt % 2)
                    ap_ = qt % 2
                    memset("pool", ACCp[ap_], 0.0, [B_ACCp[ap_]])
                    memset("dve", ACCd[ap_], 0.0, [B_ACCd[ap_]])
                    nj = 4 * qt + 4
                    for jk in range(nj):
                        r = jk - 4 * qt
                        q0 = max(r, 0) * 128
                        n = 512 - q0
                        sb = sbc % 3
                        sbc += 1
                        pt = ptc % 4
                        ptc += 1
                        qsl = slice(qt * 512 + q0, qt * 512 + 512)
                        kb = jk // 4
                        mm(banks[sb][:, 0:n], KN[par][:, jk * 128:(jk + 1) * 128], QN[par][:, qsl], True, False,
                           [B_KN[par][kb], B_QN[par][qt]], [B_bank[sb]])
                        mm(banks[sb][:, 0:n], kr2T[:, jk * 128:(jk + 1) * 128], QR[par][:, qsl], False, True,
                           [B_kr[kb], B_QR[par][qt]], [B_bank[sb]])
                        act(PT[pt][:, 0:n], banks[sb][:, 0:n], AF.Exp, [B_bank[sb]], [B_PT[pt]], scale=ATT_SCALE)
                        eng = "pool" if jk % 2 == 0 else "dve"
                        ACC, BACC = (ACCp[ap_], B_ACCp[ap_]) if jk % 2 == 0 else (ACCd[ap_], B_ACCd[ap_])
                        if r >= 0:
                            tt(eng, PT[pt][:, 0:128], PT[pt][:, 0:128], MASK[:, jk * 128:(jk + 1) * 128], ALU.mult,
                               [B_PT[pt], B_MASK], [B_PT[pt]])
                        mm(banks[ob][:, q0:512], V4[:, jk * 512 + hh * 128: jk * 512 + (hh + 1) * 128], PT[pt][:, 0:n],
                           jk == 0, jk == nj - 1, [B_V4[jk], B_PT[pt]], [B_bank[ob]])
                        tt(eng, ACC[:, q0:512], ACC[:, q0:512], PT[pt][:, 0:n], ALU.add, [BACC, B_PT[pt]], [BACC])
                    mm(banks[5], ones_f, ACCp[ap_], True, False, [B_ones, B_ACCp[ap_]], [B_bank[5]])
                    mm(banks[5], ones_f, ACCd[ap_], False, True, [B_ones, B_ACCd[ap_]], [B_bank[5]])
                    P.op("dve", lambda e: e.reciprocal(out=RS, in_=banks[5]), [B_bank[5]], [B_RS])
                    tt("dve", OTs[ap_], banks[ob], RS, ALU.mult, [B_bank[ob], B_RS], [B_OTs[ap_]])
                    dma("sp", oT_d[h * 128:(h + 1) * 128, qt * 512:(qt + 1) * 512], OTs[ap_], [B_OTs[ap_]], [B_oT], B_oT)
        P.barrier()

    def gla_mixer(j):
        cv = Carver(arena)
        St = cv.f32(1024); B_St = Buf()
        Sb = cv.bf(1024); B_Sb = Buf()
        QT = [cv.bf(4 * 512) for _ in range(2)]; B_QT = [Buf(), Buf()]
        KC = [cv.bf(8 * 512) for _ in range(2)]; B_KC = [Buf(), Buf()]
        VC = [cv.bf(8 * 1024) for _ in range(2)]; B_VC = [Buf(), Buf()]
        SRt = [cv.f32(1024) for _ in range(2)]; B_SRt = [Buf(), Buf()]
        ON = cv.f32(1024); B_ON = Buf()
        OG = cv.bf(1024); B_OG = Buf()
        OGT = [cv.bf(1024) for _ in range(2)]; B_OGT = [Buf(), Buf()]
        onb = cv.f32(1024); B_onb = Buf()
        STG = cv.f32(64); B_STG = Buf()
        dma("sp", onb, gla_on_d[j:j + 1, :].partition_broadcast(128), (), [B_onb], B_onb)
        memset("dve", St, 0.0, [B_St])
        dv = v3(DECAY, 4)
        for blk in range(NBLK):
            p2 = blk % 2
            dma("sp", v3(QT[p2], 4), gq_d[:, blk * 512:(blk + 1) * 512].rearrange("(h p) s -> p h s", p=128), [B_gq], [B_QT[p2]], B_QT[p2])
            dma("sp", v3(KC[p2][0:64, :], 8), gk_d[blk * 512:(blk + 1) * 512, :].rearrange("(c p) f -> p c f", p=64), [B_gk], [B_KC[p2]], B_KC[p2])
            dma("sp", v3(VC[p2][0:64, :], 8), gv_d[blk * 512:(blk + 1) * 512, :].rearrange("(c p) f -> p c f", p=64), [B_gv], [B_VC[p2]], B_VC[p2])
            for t4 in range(4):
                t = blk * 4 + t4
                s2 = t % 2
                dma("sp", SRt[s2], gsr_d[t * 128:(t + 1) * 128, :], [B_gsr], [B_SRt[s2]], B_SRt[s2])
                for ch in range(2):
                    c = t4 * 2 + ch
                    n = t * 2 + ch
                    for h in range(4):
                        bk_ = h // 2
                        mm(banks[bk_][:, (h % 2) * 256:(h % 2 + 1) * 256], KC[p2][0:64, c * 512 + h * 128: c * 512 + (h + 1) * 128],
                           VC[p2][0:64, c * 1024 + h * 256: c * 1024 + (h + 1) * 256], True, True, [B_KC[p2], B_VC[p2]], [B_bank[bk_]])
                    for h in range(4):
                        bk_ = h // 2
                        stt(St[:, h * 256:(h + 1) * 256], St[:, h * 256:(h + 1) * 256], dv[:, h, n:n + 1],
                            banks[bk_][:, (h % 2) * 256:(h % 2 + 1) * 256], ALU.mult, ALU.add, [B_St, B_decay, B_bank[bk_]], [B_St])
                    act(Sb, St, AF.Copy, [B_St], [B_Sb])
                    ob = 2 + 2 * ch
                    for h in range(4):
                        bk_ = ob + h // 2
                        mm(banks[bk_][:, (h % 2) * 256:(h % 2 + 1) * 256], QT[p2][:, h * 512 + t4 * 128: h * 512 + (t4 + 1) * 128],
                           Sb[:, h * 256:(h + 1) * 256], True, True, [B_QT[p2], B_Sb], [B_bank[bk_]])
                for ch in range(2):
                    rs_ = slice(ch * 64, (ch + 1) * 64)
                    ob = 2 + 2 * ch
                    for h in range(4):
                        src = banks[ob + h // 2][rs_, (h % 2) * 256:(h % 2 + 1) * 256]
                        P.op("dve", lambda e, src=src, h=h, rs_=rs_: e.bn_stats(out=STG[rs_, h * 6:(h + 1) * 6], in_=src),
                             [B_bank[ob + h // 2]], [B_STG])
                for h in range(4):
                    P.op("dve", lambda e, h=h: e.bn_aggr(out=STG[:, 24 + 2 * h:26 + 2 * h], in_=STG[:, h * 6:(h + 1) * 6]), [B_STG], [B_STG])
                varv = STG[:, 24:32].rearrange("p (h two) -> p h two", two=2)[:, :, 1]
                ts("dve", STG[:, 32:36], varv, EPS, None, ALU.add, None, [B_STG], [B_STG])
                act(STG[:, 36:40], STG[:, 32:36], AF.Sqrt, [B_STG], [B_STG])
                P.op("dve", lambda e: e.reciprocal(out=STG[:, 40:44], in_=STG[:, 36:40]), [B_STG], [B_STG])
                for ch in range(2):
                    rs_ = slice(ch * 64, (ch + 1) * 64)
                    ob = 2 + 2 * ch
                    for h in range(4):
                        src = banks[ob + h // 2][rs_, (h % 2) * 256:(h % 2 + 1) * 256]
                        ts("dve", ON[rs_, h * 256:(h + 1) * 256], src, STG[rs_, 24 + 2 * h:25 + 2 * h], STG[rs_, 40 + h:41 + h],
                           ALU.subtract, ALU.mult, [B_bank[ob + h // 2], B_STG], [B_ON])
                tt("pool", ON, ON, onb, ALU.mult, [B_ON, B_onb], [B_ON])
                tt("pool", OG, ON, SRt[s2], ALU.mult, [B_ON, B_SRt[s2]], [B_OG])
                tb_ = banks_bf[6 + (t % 2)]
                Btb = B_bank[6 + (t % 2)]
                for kc in range(8):
                    tr(tb_[:, kc * 128:(kc + 1) * 128], OG[:, kc * 128:(kc + 1) * 128], ident_b, [B_OG, B_identb], [Btb])
                act(OGT[s2], tb_, AF.Copy, [Btb], [B_OGT[s2]])
                dma("sp", oT_d[0:1024, t * 128:(t + 1) * 128].rearrange("(k p) s -> p k s", p=128), v3(OGT[s2], 8),
                    [B_OGT[s2]], [B_oT], B_oT)
        P.barrier()

    prologue()
    for i in range(nlayers):
        j = i // 2
        last = (i == nlayers - 1)
        if i % 2 == 0:
            mla_attention(j)
            nxt = None if last else ('gla', j)
            tail(i, 16, mla_wo_d[j], nxt)
        else:
            gla_mixer(j)
            nxt = None if last else ('mla', j + 1)
            tail(i, 8, gla_wo_d[j], nxt)
    P.barrier()
    P.emit()
    st.close()
    return nc, P


_CACHE = {}


def _host_layout(inputs, b):
    f32 = np.float32
    d = {}
    d["x"] = np.ascontiguousarray(inputs["x"][b], dtype=f32)
    d["pT"] = np.ascontiguousarray(np.transpose(inputs["p"][:, b], (0, 2, 1)), dtype=f32)
    pos = np.asarray(inputs["positions"][b], dtype=np.int32)
    d["pos_row"] = np.ascontiguousarray(pos.reshape(1, S))
    d["pos_col"] = np.ascontiguousarray(pos.reshape(NT, 128).T)
    return d


def _shared_layout(inputs):
    f32 = np.float32
    d = {}
    inv = (1.0 / (10000.0 ** (np.arange(0, 64, 2, dtype=f32) / f32(64)))).astype(f32)
    invt = (inv.astype(np.float64) / (2 * np.pi)).astype(f32)
    ropecol = np.zeros((128, 2), f32)
    for p in range(128):
        ropecol[p, 0] = invt[p % 32]
        ropecol[p, 1] = 0.25 if p < 64 else (0.5 if p < 96 else 0.0)
    d["ropecol"] = ropecol
    row = np.concatenate([invt, invt, invt, invt]).astype(f32)
    d["inv2"] = np.ascontiguousarray(np.broadcast_to(row, (128, 128)))
    ph = np.concatenate([np.full(64, 0.25, f32), np.zeros(64, f32)])
    d["ph2"] = np.ascontiguousarray(np.broadcast_to(ph, (128, 128)))
    for k in ("mla_w_in", "mla_q_norm", "mla_kv_norm", "mla_w_uk", "mla_w_uv", "mla_w_o", "gla_w_in", "gla_o_norm", "gla_w_o",
              "ln1_g", "ln1_b", "ln2_g", "ln2_b", "ffn_w_up", "ffn_w_down", "ple_w_proj", "ple_w_gate", "ple_b_gate"):
        d[k] = np.ascontiguousarray(inputs[k], dtype=f32)
    wuq = np.asarray(inputs["mla_w_uq"], dtype=f32).reshape(2, 512, 16, 192)
    d["wq_n"] = np.ascontiguousarray(wuq[..., 0:128].reshape(2, 512, 2048))
    wr = wuq[..., 128:192]
    wr2 = np.concatenate([wr, wr[..., 32:64], wr[..., 0:32]], axis=-1)
    d["wq_r"] = np.ascontiguousarray(wr2.reshape(2, 512, 2048))
    d["gla_wa2x"] = np.ascontiguousarray(np.concatenate([np.asarray(inputs["gla_w_a2"], f32),
                                                          np.asarray(inputs["gla_b_a"], f32)[:, None, :]], axis=1))
    cw = np.asarray(inputs["ffn_conv_w"], f32)
    cb = np.asarray(inputs["ffn_conv_b"], f32)
    cat = np.concatenate([cw, cb[:, None, :]], axis=1)
    cat = cat.reshape(4, 4, 44, 128)
    d["convp"] = np.ascontiguousarray(np.transpose(cat, (0, 3, 2, 1)).reshape(4, 128, 176))
    return d


def kernel(**inputs):
    if "nc" not in _CACHE:
        _CACHE["nc"] = build()[0]
    nc = _CACHE["nc"]
    shared = _shared_layout(inputs)
    in_maps = []
    for b in range(8):
        m = dict(shared)
        m.update(_host_layout(inputs, b))
        in_maps.append(m)
    res = run_bass_kernel_spmd(nc, in_maps, core_ids=list(range(8)))
    out = np.stack([np.asarray(r["out"], dtype=np.float32) for r in res.results], axis=0)
    return out
```

```python
import contextlib
import math
import numpy as np
import concourse.bass as bass
import concourse.mybir as mybir
from concourse.bass_utils import run_bass_kernel_spmd

F32 = mybir.dt.float32
BF16 = mybir.dt.bfloat16
I32 = mybir.dt.int32
AF = mybir.ActivationFunctionType
ALU = mybir.AluOpType

ENGS = ("pe", "act", "dve", "pool", "sp")

S = 4096
D = 1024
NT = 32
TB = 512
NBLK = 8
DFF = 2816
NFC = 22
ALPHA = 8.0 ** 0.25
EPS = 1e-5
ATT_SCALE = 192.0 ** -0.5
GLA_QS = 128.0 ** -0.5
NW = 20
TWO_PI = 2.0 * math.pi


class Buf:
    __slots__ = ("name", "lw", "rd", "sem", "cnt")

    def __init__(self, name=""):
        self.name = name
        self.lw = None
        self.rd = {}
        self.sem = None
        self.cnt = 0


class Op:
    __slots__ = ("fn", "waits", "signal", "dma", "clock")

    def __init__(self, fn):
        self.fn = fn
        self.waits = []
        self.signal = False
        self.dma = None
        self.clock = None


class Prog:
    def __init__(self, nc):
        self.nc = nc
        self.ops = {e: [] for e in ENGS}
        self.clock = {e: {} for e in ENGS}
        self.dma_bufs = []

    def _need(self, eng, tok, raw):
        ck = self.clock[eng]
        if tok[0] == 'e':
            _, E, i = tok
            if E == eng and not raw:
                return None
            if ck.get(E, -1) >= i:
                return None
            ck[E] = i
            src = self.ops[E][i]
            src.signal = True
            for k, v in src.clock.items():
                if ck.get(k, -1) < v:
                    ck[k] = v
            return tok
        _, b, c = tok
        if ck.get(b, -1) >= c:
            return None
        ck[b] = c
        return tok

    def op(self, eng, fn, reads=(), writes=(), dma=None):
        rec = Op(fn)
        lst = self.ops[eng]
        idx = len(lst)
        waits = rec.waits
        for b in reads:
            if b.lw is not None:
                w = self._need(eng, b.lw, True)
                if w:
                    waits.append(w)
        for b in writes:
            if b.lw is not None:
                w = self._need(eng, b.lw, False)
                if w:
                    waits.append(w)
            for t in b.rd.values():
                w = self._need(eng, t, False)
                if w:
                    waits.append(w)
        rec.clock = dict(self.clock[eng])
        if dma is not None:
            if dma.sem is None:
                dma.sem = True
                self.dma_bufs.append(dma)
            dma.cnt += 16
            tok = ('d', dma, dma.cnt)
            key = dma
            rec.dma = dma
        else:
            tok = ('e', eng, idx)
            key = eng
        for b in writes:
            b.lw = tok
            b.rd = {}
        for b in reads:
            b.rd[key] = tok
        lst.append(rec)
        return rec

    def barrier(self):
        toks = []
        for e in ENGS:
            lst = self.ops[e]
            for i in range(len(lst) - 1, -1, -1):
                if lst[i].dma is None and lst[i].fn is not None:
                    toks.append(('e', e, i))
                    break
        dtoks = [('d', b, b.cnt) for b in self.dma_bufs if b.cnt > 0]
        for e in ENGS:
            rec = Op(None)
            for t in toks + dtoks:
                w = self._need(e, t, True)
                if w:
                    rec.waits.append(w)
            rec.clock = dict(self.clock[e])
            self.ops[e].append(rec)

    def emit(self):
        nc = self.nc
        with contextlib.ExitStack() as st:
            esem = {e: st.enter_context(nc.semaphore("es_" + e)) for e in ENGS}
            for i, b in enumerate(self.dma_bufs):
                b.sem = st.enter_context(nc.semaphore("ds%d" % i))
            sigidx = {}
            for e in ENGS:
                c = 0
                arr = []
                for rec in self.ops[e]:
                    if rec.signal:
                        c += 1
                    arr.append(c)
                sigidx[e] = arr
            block = st.enter_context(nc.Block())

            def run(e, engobj):
                for rec in self.ops[e]:
                    for w in rec.waits:
                        if w[0] == 'e':
                            engobj.wait_ge(esem[w[1]], sigidx[w[1]][w[2]])
                        else:
                            engobj.wait_ge(w[1].sem, w[2])
                    if rec.fn is None:
                        continue
                    ins = rec.fn(engobj)
                    if rec.dma is not None:
                        ins.then_inc(rec.dma.sem, 16)
                    elif rec.signal:
                        ins.then_inc(esem[e], 1)

            @block.tensor
            def _(eng):
                run("pe", eng)

            @block.scalar
            def _(eng):
                run("act", eng)

            @block.vector
            def _(eng):
                run("dve", eng)

            @block.gpsimd
            def _(eng):
                run("pool", eng)

            @block.sync
            def _(eng):
                run("sp", eng)


class Carver:
    def __init__(self, ap):
        self.ap = ap
        self.off = 0
        self.hi = 0

    def bf(self, n):
        v = self.ap[:, self.off:self.off + n]
        self.off += n + (n & 1)
        self.hi = max(self.hi, self.off)
        assert self.off <= self.ap.shape[1], ("arena overflow", self.off, self.ap.shape)
        return v

    def f32(self, n):
        return self.bf(2 * n).bitcast(F32)

    def i32(self, n):
        return self.bf(2 * n).bitcast(I32)


def build(nlayers=4, dbg=False):
    nc = bass.Bass("TRN2", target_bir_lowering=False)
    P = Prog(nc)

    def din(name, shape, dt=F32):
        return nc.dram_tensor(name, shape, dt, kind="ExternalInput").ap()

    x_d = din("x", [S, D])
    pT_d = din("pT", [4, 256, S])
    posr_d = din("pos_row", [1, S], I32)
    posc_d = din("pos_col", [128, NT], I32)
    ropecol_d = din("ropecol", [128, 2])
    inv2_d = din("inv2", [128, 128])
    ph2_d = din("ph2", [128, 128])
    mla_w_in_d = din("mla_w_in", [2, 1024, 832])
    mla_qn_d = din("mla_q_norm", [2, 512])
    mla_kvn_d = din("mla_kv_norm", [2, 256])
    wqn_d = din("wq_n", [2, 512, 2048])
    wqr_d = din("wq_r", [2, 512, 2048])
    wuk_d = din("mla_w_uk", [2, 256, 2048])
    wuv_d = din("mla_w_uv", [2, 256, 2048])
    mla_wo_d = din("mla_w_o", [2, 2048, 1024])
    gla_w_in_d = din("gla_w_in", [2, 1024, 3088])
    gla_wa2x_d = din("gla_wa2x", [2, 17, 512])
    gla_on_d = din("gla_o_norm", [2, 1024])
    gla_wo_d = din("gla_w_o", [2, 1024, 1024])
    ln1g_d = din("ln1_g", [4, 1024])
    ln1b_d = din("ln1_b", [4, 1024])
    ln2g_d = din("ln2_g", [4, 1024])
    ln2b_d = din("ln2_b", [4, 1024])
    wup_d = din("ffn_w_up", [4, 1024, 5632])
    convp_d = din("convp", [4, 128, 176])
    wdn_d = din("ffn_w_down", [4, 2816, 1024])
    wpp_d = din("ple_w_proj", [4, 256, 1024])
    wpg_d = din("ple_w_gate", [4, 1024, 1024])
    bg_d = din("ple_b_gate", [4, 1024])
    out_d = nc.dram_tensor("out", [S, D], F32, kind="ExternalOutput").ap()
    okind = "ExternalOutput" if dbg else "Internal"
    xres_d = nc.dram_tensor("xres", [S, D], F32, kind=okind).ap()
    oT_d = nc.dram_tensor("oT", [2048, S], BF16, kind=okind).ap()
    gq_d = nc.dram_tensor("gq", [512, S], BF16, kind="Internal").ap()
    gk_d = nc.dram_tensor("gk", [S, 512], BF16, kind="Internal").ap()
    gv_d = nc.dram_tensor("gv", [S, 1024], BF16, kind="Internal").ap()
    gsr_d = nc.dram_tensor("gsr", [S, 1024], F32, kind="Internal").ap()
    B_xres, B_oT, B_gq, B_gk, B_gv, B_gsr, B_out = (Buf(n) for n in ("xres", "oT", "gq", "gk", "gv", "gsr", "out"))

    st = contextlib.ExitStack()

    def sbt(name, cols, dt):
        return st.enter_context(nc.sbuf_tensor(name, [128, cols], dt))[:]

    ident_f = sbt("ident_f", 128, F32); B_identf = Buf()
    ident_b = sbt("ident_b", 128, BF16); B_identb = Buf()
    ones_f = sbt("ones_f", 128, F32); B_ones = Buf()
    tri_f = sbt("tri_f", 128, F32); B_tri = Buf()
    ind_f = sbt("ind_f", 2, F32); B_ind = Buf()
    T2 = sbt("T2", S, BF16); B_T2 = Buf()
    MASK = sbt("MASK", S, BF16); B_MASK = Buf()
    cqT = sbt("cqT", 4 * S, BF16)
    ckvT = sbt("ckvT", 2 * S, BF16)
    kr2T = sbt("kr2T", S, BF16)
    B_cq = [Buf() for _ in range(NBLK)]
    B_ckv = [Buf() for _ in range(NBLK)]
    B_kr = [Buf() for _ in range(NBLK)]
    posc_i = sbt("posc_i", NT, I32); B_posci = Buf()
    posc_f = sbt("posc_f", NT, F32); B_poscf = Buf()
    cidc_f = sbt("cidc_f", NT, F32); B_cidc = Buf()
    ropecol = sbt("ropecol_s", 2, F32); B_ropecol = Buf()
    inv2 = sbt("inv2_s", 128, F32); B_inv2 = Buf()
    ph2 = sbt("ph2_s", 128, F32); B_ph2 = Buf()
    convp = sbt("convp_s", 176, F32); B_convp = Buf()
    gq_b = sbt("gq_b", 512, F32); B_gqb = Buf()
    gkv_b = sbt("gkv_b", 256, F32); B_gkvb = Buf()
    DECAY = sbt("DECAY", 256, F32); B_decay = Buf()
    wa2x = sbt("wa2x", 512, BF16); B_wa2x = Buf()
    AT = sbt("AT", 512, BF16); B_AT = Buf()
    HALO = [sbt("HALO%d" % i, 88, F32) for i in range(2)]; B_halo = [Buf(), Buf()]
    CORR = sbt("CORR", 88, F32); B_corr = Buf()
    smalls = sbt("smalls", 64, F32)
    wslots = [(sbt("w%d" % i, 512, BF16), Buf("w%d" % i)) for i in range(NW)]
    wctr = [0]
    arena_cols = (nc.sbuf_bytes_remaining - 1024) // 2
    arena_cols -= arena_cols % 64
    arena = sbt("arena", arena_cols, BF16)
    banks = [st.enter_context(nc.psum_tensor("bank%d" % i, [128, 512], F32))[:] for i in range(8)]
    banks_bf = [b.bitcast(BF16) for b in banks]
    B_bank = [Buf("bank%d" % i) for i in range(8)]

    def mm(out, lhsT, rhs, start, stop, R, W):
        P.op("pe", lambda e: e.matmul(out, lhsT=lhsT, rhs=rhs, start=start, stop=stop), R, W)

    def tr(out, in_, ident, R, W):
        P.op("pe", lambda e: e.transpose(out, in_, ident), R, W)

    def act(out, in_, func, R, W, scale=1.0, bias=0.0, accum=None):
        if accum is None:
            P.op("act", lambda e: e.activation(out=out, in_=in_, func=func, bias=bias, scale=scale), R, W)
        else:
            P.op("act", lambda e: e.activation(out=out, in_=in_, func=func, bias=bias, scale=scale, accum_out=accum), R, W)

    def ts(eng, out, in0, s1, s2, op0, op1, R, W):
        if s2 is None:
            P.op(eng, lambda e: e.tensor_scalar(out=out, in0=in0, scalar1=s1, scalar2=None, op0=op0), R, W)
        else:
            P.op(eng, lambda e: e.tensor_scalar(out=out, in0=in0, scalar1=s1, scalar2=s2, op0=op0, op1=op1), R, W)

    def tt(eng, out, in0, in1, op, R, W):
        P.op(eng, lambda e: e.tensor_tensor(out=out, in0=in0, in1=in1, op=op), R, W)

    def stt(out, in0, scalar, in1, op0, op1, R, W):
        P.op("dve", lambda e: e.scalar_tensor_tensor(out=out, in0=in0, scalar=scalar, in1=in1, op0=op0, op1=op1), R, W)

    def cp(eng, out, in_, R, W):
        P.op(eng, lambda e: e.tensor_copy(out=out, in_=in_), R, W)

    def memset(eng, ap, val, W):
        P.op(eng, lambda e: e.memset(ap, val), (), W)

    def dma(eng, out, in_, R, W, dbuf):
        P.op(eng, lambda e: e.dma_start(out=out, in_=in_), R, W, dma=dbuf)

    mirror_buf = {}

    def mirror2d(src2d, name):
        return nc.dram_tensor("bf_" + name, list(src2d.shape), BF16, kind="Internal").ap()

    def precast(gbuf, pairs):
        for src, dst in pairs:
            R_, C_ = src.shape
            rows = R_ if R_ <= 128 else max(128, ((1 << 19) // C_) // 128 * 128)
            for r0 in range(0, R_, rows):
                r1 = min(R_, r0 + rows)
                dma("pool", dst[r0:r1, :], src[r0:r1, :], (), (), gbuf)
            mirror_buf[dst.name] = gbuf
        gbuf.lw = ('d', gbuf, gbuf.cnt)

    def wt(src, rows=128, cols=512):
        ap, b = wslots[wctr[0] % NW]
        wctr[0] += 1
        v = ap[0:rows, 0:cols]
        dma("sp", v, src, [mirror_buf[src.name]], [b], b)
        return v, b

    mla_w_in_f, wqn_f, wqr_f, wuk_f, wuv_f, mla_wo_f = mla_w_in_d, wqn_d, wqr_d, wuk_d, wuv_d, mla_wo_d
    gla_w_in_f, gla_wo_f, wup_f, wdn_f, wpp_f, wpg_f, pT_f = gla_w_in_d, gla_wo_d, wup_d, wdn_d, wpp_d, wpg_d, pT_d
    mla_w_in_b = [mirror2d(mla_w_in_f[j], "mla_w_in%d" % j) for j in range(2)]
    wqn_b = [mirror2d(wqn_f[j], "wqn%d" % j) for j in range(2)]
    wqr_b = [mirror2d(wqr_f[j], "wqr%d" % j) for j in range(2)]
    wuk_b = [mirror2d(wuk_f[j], "wuk%d" % j) for j in range(2)]
    wuv_b = [mirror2d(wuv_f[j], "wuv%d" % j) for j in range(2)]
    mla_wo_b = [mirror2d(mla_wo_f[j], "mla_wo%d" % j) for j in range(2)]
    gla_w_in_b = [mirror2d(gla_w_in_f[j], "gla_w_in%d" % j) for j in range(2)]
    gla_wo_b = [mirror2d(gla_wo_f[j], "gla_wo%d" % j) for j in range(2)]
    wup_b = [mirror2d(wup_f[i], "wup%d" % i) for i in range(4)]
    wdn_b = [mirror2d(wdn_f[i], "wdn%d" % i) for i in range(4)]
    wpp_b = [mirror2d(wpp_f[i], "wpp%d" % i) for i in range(4)]
    wpg_b = [mirror2d(wpg_f[i], "wpg%d" % i) for i in range(4)]
    pT_b = [mirror2d(pT_f[i], "pTb%d" % i) for i in range(4)]

    def tail_pairs(i):
        return [(wup_f[i], wup_b[i]), (wdn_f[i], wdn_b[i]), (wpg_f[i], wpg_b[i]), (wpp_f[i], wpp_b[i]), (pT_f[i], pT_b[i])]

    def precast_stage(k):
        if k == 0:
            precast(Buf("g_in0"), [(mla_w_in_f[0], mla_w_in_b[0])])
            precast(Buf("g_mix0"), [(wqn_f[0], wqn_b[0]), (wqr_f[0], wqr_b[0]), (wuk_f[0], wuk_b[0]), (wuv_f[0], wuv_b[0])])
            precast(Buf("g_tail0"), [(mla_wo_f[0], mla_wo_b[0])] + tail_pairs(0) + [(gla_w_in_f[0], gla_w_in_b[0])])
        elif k == 1:
            precast(Buf("g1"), [(gla_wo_f[0], gla_wo_b[0])] + tail_pairs(1) + [(mla_w_in_f[1], mla_w_in_b[1])])
        elif k == 2:
            precast(Buf("g2"), [(wqn_f[1], wqn_b[1]), (wqr_f[1], wqr_b[1]), (wuk_f[1], wuk_b[1]), (wuv_f[1], wuv_b[1]),
                                (mla_wo_f[1], mla_wo_b[1])] + tail_pairs(2) + [(gla_w_in_f[1], gla_w_in_b[1])])
        elif k == 3:
            precast(Buf("g3"), [(gla_wo_f[1], gla_wo_b[1])] + tail_pairs(3))

    def v3(ap, a):
        return ap.rearrange("p (a b) -> p a b", a=a)

    memset("pool", ident_f, 1.0, [B_identf])
    P.op("pool", lambda e: e.affine_select(out=ident_f, in_=ident_f, pattern=[[-1, 128]], compare_op=ALU.is_equal,
                                           fill=0.0, base=0, channel_multiplier=1), [B_identf], [B_identf])
    cp("dve", ident_b, ident_f, [B_identf], [B_identb])
    memset("pool", ones_f, 1.0, [B_ones])
    memset("pool", tri_f, 1.0 / 16.0, [B_tri])
    P.op("pool", lambda e: e.affine_select(out=tri_f, in_=tri_f, pattern=[[-1, 128]], compare_op=ALU.is_gt,
                                           fill=0.0, base=0, channel_multiplier=1), [B_tri], [B_tri])
    memset("pool", tri_f[64:128, 0:64], 0.0, [B_tri])
    memset("pool", ind_f, 0.0, [B_ind])
    memset("pool", ind_f[0:64, 0:1], 1.0 / 16.0, [B_ind])
    memset("pool", ind_f[64:128, 1:2], 1.0 / 16.0, [B_ind])
    memset("pool", AT[0:32, :], 1.0, [B_AT])
    dma("sp", posc_i, posc_d, (), [B_posci], B_posci)
    dma("sp", ropecol, ropecol_d, (), [B_ropecol], B_ropecol)
    dma("sp", inv2, inv2_d, (), [B_inv2], B_inv2)
    dma("sp", ph2, ph2_d, (), [B_ph2], B_ph2)
    cp("dve", posc_f, posc_i, [B_posci], [B_poscf])
    cv = Carver(arena)
    R0 = cv.bf(2 * S); R1 = cv.bf(2 * S); R2 = cv.bf(2 * S); R3 = cv.bf(2 * S)
    B_R = [Buf() for _ in range(4)]
    tmpi = smalls[:, 0:32].bitcast(I32); B_tmpi = Buf()
    ts("dve", tmpi, posc_i, 6, None, ALU.arith_shift_right, None, [B_posci], [B_tmpi])
    cp("dve", cidc_f, tmpi, [B_tmpi], [B_cidc])
    dma("sp", R0.bitcast(I32), posr_d.partition_broadcast(128), (), [B_R[0]], B_R[0])
    cp("dve", R1.bitcast(F32), R0.bitcast(I32), [B_R[0]], [B_R[1]])
    ts("dve", R2.bitcast(I32), R0.bitcast(I32), 6, None, ALU.arith_shift_right, None, [B_R[0]], [B_R[2]])
    cp("dve", R0.bitcast(F32), R2.bitcast(I32), [B_R[2]], [B_R[0]])
    for b in range(NT):
        ts("dve", MASK[:, b * 128:(b + 1) * 128], R0.bitcast(F32)[:, b * 128:(b + 1) * 128], cidc_f[:, b:b + 1], None,
           ALU.is_ge, None, [B_R[0], B_cidc], [B_MASK])
    ts("dve", R2.bitcast(F32), R1.bitcast(F32), ropecol[:, 0:1], ropecol[:, 1:2], ALU.mult, ALU.add, [B_R[1], B_ropecol], [B_R[2]])
    cp("dve", R1.bitcast(I32), R2.bitcast(F32), [B_R[2]], [B_R[1]])
    cp("dve", R3.bitcast(F32), R1.bitcast(I32), [B_R[1]], [B_R[3]])
    tt("dve", R2.bitcast(F32), R2.bitcast(F32), R3.bitcast(F32), ALU.subtract, [B_R[2], B_R[3]], [B_R[2]])
    stt(R2.bitcast(F32), R2.bitcast(F32), 0.5, R2.bitcast(F32), ALU.is_gt, ALU.subtract, [B_R[2]], [B_R[2]])
    act(T2, R2.bitcast(F32), AF.Sin, [B_R[2]], [B_T2], scale=-TWO_PI)
    P.barrier()

    def tail_layout(with_cq_live):
        cv = Carver(arena)
        L = {}
        L["VEC"] = [cv.f32(1024) for _ in range(2)]
        L["XA"] = cv.f32(4 * 1024)
        L["XBF"] = cv.bf(1024)
        L["XT"] = cv.bf(8 * 512)
        L["OTB"] = cv.bf(16 * 512)
        L["ACTT"] = cv.bf(22 * 512)
        L["PTB"] = cv.bf(2 * 512)
        L["TMP"] = cv.f32(1024)
        L["FU"] = cv.f32(512)
        L["FG"] = cv.f32(512)
        L["FU2"] = cv.f32(512)
        L["FG2"] = cv.f32(512)
        L["SQJ"] = cv.bf(512)
        L["CQN"] = cv.bf(768)
        L["KR"] = cv.bf(128)
        L["CS"] = cv.f32(512)
        L["CSI"] = cv.i32(512)
        L["CSF"] = cv.f32(512)
        L["XC"] = cv.f32(64)
        L["XS"] = cv.f32(64)
        L["ST"] = cv.f32(64)
        if not with_cq_live:
            cv = Carver(cqT)
            L["LA"] = cv.f32(512)
            L["AZ"] = cv.f32(512)
            L["DEC"] = cv.f32(512)
            L["KD"] = cv.bf(512)
            L["VB"] = cv.bf(1024)
            L["SR"] = cv.f32(1024)
            L["QTS"] = cv.bf(512)
        for k in list(L.keys()):
            if k == "VEC":
                L["B_VEC"] = [Buf() for _ in range(2)]
            else:
                L["B_" + k] = Buf(k)
        L["B_XAt"] = [Buf() for _ in range(4)]
        L["B_XTt"] = [Buf() for _ in range(4)]
        return L

    def to_featmajor(L, tt_i, tbank):
        XAt = L["XA"][:, tt_i * 1024:(tt_i + 1) * 1024]
        act(L["XBF"], XAt, AF.Copy, [L["B_XAt"][tt_i]], [L["B_XBF"]])
        for kc in range(8):
            tr(banks_bf[tbank][:, kc * 128:(kc + 1) * 128], L["XBF"][:, kc * 128:(kc + 1) * 128], ident_b,
               [L["B_XBF"], B_identb], [B_bank[tbank]])
        cp("dve", v3(L["XT"], 8)[:, :, tt_i * 128:(tt_i + 1) * 128], v3(banks_bf[tbank], 8),
           [B_bank[tbank]], [L["B_XTt"][tt_i]])

    def project_mla(L, j, blk):
        XT = L["XT"]
        RX = L["B_XTt"]
        for kc in range(8):
            w, wb = wt(mla_w_in_b[j][kc * 128:(kc + 1) * 128, 0:512])
            for t4 in range(4):
                mm(banks[t4], XT[:, kc * 512 + t4 * 128: kc * 512 + (t4 + 1) * 128], w, kc == 0, kc == 7,
                   [RX[t4], wb], [B_bank[t4]])
        for kc in range(8):
            w, wb = wt(mla_w_in_b[j][kc * 128:(kc + 1) * 128, 512:832], cols=320)
            for t4 in range(4):
                mm(banks[4 + t4][:, 0:320], XT[:, kc * 512 + t4 * 128: kc * 512 + (t4 + 1) * 128], w, kc == 0, kc == 7,
                   [RX[t4], wb], [B_bank[4 + t4]])
        CS, CSI, CSF = L["CS"], L["CSI"], L["CSF"]
        for t4 in range(4):
            t = blk * 4 + t4
            stt(CS[:, t4 * 128:(t4 + 1) * 128], inv2, posc_f[:, t:t + 1], ph2, ALU.mult, ALU.add,
                [B_inv2, B_poscf, B_ph2], [L["B_CS"]])
        cp("dve", CSI, CS, [L["B_CS"]], [L["B_CSI"]])
        cp("dve", CSF, CSI, [L["B_CSI"]], [L["B_CSF"]])
        tt("dve", CS, CS, CSF, ALU.subtract, [L["B_CS"], L["B_CSF"]], [L["B_CS"]])
        stt(CS, CS, 0.5, CS, ALU.is_gt, ALU.subtract, [L["B_CS"]], [L["B_CS"]])
        act(CS, CS, AF.Sin, [L["B_CS"]], [L["B_CS"]], scale=-TWO_PI)
        ST = L["ST"]
        for t4 in range(4):
            t = blk * 4 + t4
            bq, bk = banks[t4], banks[4 + t4]
            act(L["SQJ"], bq, AF.Square, [B_bank[t4]], [L["B_SQJ"], L["B_ST"]], scale=512.0 ** -0.5, accum=ST[:, 0:1])
            act(L["SQJ"][:, 0:256], bk[:, 0:256], AF.Square, [B_bank[4 + t4]], [L["B_SQJ"], L["B_ST"]], scale=256.0 ** -0.5,
                accum=ST[:, 1:2])
            ts("dve", ST[:, 2:4], ST[:, 0:2], EPS, None, ALU.add, None, [L["B_ST"]], [L["B_ST"]])
            act(ST[:, 4:6], ST[:, 2:4], AF.Sqrt, [L["B_ST"]], [L["B_ST"]])
            P.op("dve", lambda e: e.reciprocal(out=ST[:, 6:8], in_=ST[:, 4:6]), [L["B_ST"]], [L["B_ST"]])
            CQN = L["CQN"]
            stt(CQN[:, 0:512], bq, ST[:, 6:7], gq_b, ALU.mult, ALU.mult, [B_bank[t4], L["B_ST"], B_gqb], [L["B_CQN"]])
            stt(CQN[:, 512:768], bk[:, 0:256], ST[:, 7:8], gkv_b, ALU.mult, ALU.mult, [B_bank[4 + t4], L["B_ST"], B_gkvb], [L["B_CQN"]])
            XC, XS, KR = L["XC"], L["XS"], L["KR"]
            tt("dve", XC, bk[:, 256:320], CS[:, t4 * 128: t4 * 128 + 64], ALU.mult, [B_bank[4 + t4], L["B_CS"]], [L["B_XC"]])
            tt("dve", XS, bk[:, 256:320], CS[:, t4 * 128 + 64: t4 * 128 + 128], ALU.mult, [B_bank[4 + t4], L["B_CS"]], [L["B_XS"]])
            tt("dve", KR[:, 0:32], XC[:, 0:32], XS[:, 32:64], ALU.subtract, [L["B_XC"], L["B_XS"]], [L["B_KR"]])
            tt("dve", KR[:, 32:64], XS[:, 0:32], XC[:, 32:64], ALU.add, [L["B_XC"], L["B_XS"]], [L["B_KR"]])
            cp("dve", KR[:, 64:128], KR[:, 0:64], [L["B_KR"]], [L["B_KR"]])
            tb_ = banks_bf[t4]
            for kc in range(6):
                tr(tb_[:, kc * 128:(kc + 1) * 128], CQN[:, kc * 128:(kc + 1) * 128], ident_b, [L["B_CQN"], B_identb], [B_bank[t4]])
            tr(tb_[:, 768:896], KR, ident_b, [L["B_KR"], B_identb], [B_bank[t4]])
            act(v3(cqT, 4)[:, :, t * 128:(t + 1) * 128], v3(tb_[:, 0:512], 4), AF.Copy, [B_bank[t4]], [B_cq[blk]])
            act(v3(ckvT, 2)[:, :, t * 128:(t + 1) * 128], v3(tb_[:, 512:768], 2), AF.Copy, [B_bank[t4]], [B_ckv[blk]])
            act(kr2T[:, t * 128:(t + 1) * 128], tb_[:, 768:896], AF.Copy, [B_bank[t4]], [B_kr[blk]])

    def project_gla(L, j, blk):
        XT = L["XT"]
        RX = L["B_XTt"]
        win = gla_w_in_b[j]
        for kc in range(8):
            w, wb = wt(win[kc * 128:(kc + 1) * 128, 0:512])
            for h in range(4):
                mm(banks[h], w[:, h * 128:(h + 1) * 128], XT[:, kc * 512:(kc + 1) * 512], kc == 0, kc == 7, RX + [wb], [B_bank[h]])
        for h in range(4):
            act(L["QTS"], banks[h], AF.Copy, [B_bank[h]], [L["B_QTS"]], scale=GLA_QS)
            dma("pool", gq_d[h * 128:(h + 1) * 128, blk * 512:(blk + 1) * 512], L["QTS"], [L["B_QTS"]], [B_gq], B_gq)
        for kc in range(8):
            w, wb = wt(win[kc * 128:(kc + 1) * 128, 3072:3088], cols=16)
            mm(banks[4][0:16, :], w, XT[:, kc * 512:(kc + 1) * 512], kc == 0, kc == 7, RX + [wb], [B_bank[4]])
        act(AT[0:16, :], banks[4][0:16, :], AF.Copy, [B_bank[4]], [B_AT])
        for kc in range(8):
            w, wb = wt(win[kc * 128:(kc + 1) * 128, 512:1024])
            for t4 in range(4):
                mm(banks[t4], XT[:, kc * 512 + t4 * 128: kc * 512 + (t4 + 1) * 128], w, kc == 0, kc == 7, [RX[t4], wb], [B_bank[t4]])
        LA, AZ, DEC, KD = L["LA"], L["AZ"], L["DEC"], L["KD"]
        for t4 in range(4):
            t = blk * 4 + t4
            zb = 5
            mm(banks[zb], AT[0:17, t4 * 128:(t4 + 1) * 128], wa2x[0:17, :], True, True, [B_AT, B_wa2x], [B_bank[zb]])
            act(DEC, banks[zb], AF.Copy, [B_bank[zb]], [L["B_DEC"]])
            stt(AZ, DEC, -1.0, DEC, ALU.mult, ALU.max, [L["B_DEC"]], [L["B_AZ"]])
            act(AZ, AZ, AF.Exp, [L["B_AZ"]], [L["B_AZ"]], scale=-1.0)
            act(AZ, AZ, AF.Ln, [L["B_AZ"]], [L["B_AZ"]], bias=1.0)
            stt(LA, DEC, 0.0, AZ, ALU.min, ALU.subtract, [L["B_DEC"], L["B_AZ"]], [L["B_LA"]])
            mm(banks[6], tri_f, LA, True, True, [B_tri, L["B_LA"]], [B_bank[6]])
            for h in range(4):
                mm(banks[7][:, 2 * h:2 * h + 2], LA[:, h * 128:(h + 1) * 128], ind_f, True, True, [L["B_LA"], B_ind], [B_bank[7]])
            act(DEC, banks[6], AF.Exp, [B_bank[6]], [L["B_DEC"]])
            act(v3(DECAY, 4)[:, :, 2 * t:2 * t + 2], v3(banks[7][:, 0:8], 4), AF.Exp, [B_bank[7]], [B_decay])
            tt("dve", KD, banks[t4], DEC, ALU.mult, [B_bank[t4], L["B_DEC"]], [L["B_KD"]])
            dma("pool", gk_d[t * 128:(t + 1) * 128, :], KD, [L["B_KD"]], [B_gk], B_gk)
        for cg in range(4):
            c0 = 1024 + cg * 512
            bs = 4 * (cg % 2)
            for kc in range(8):
                w, wb = wt(win[kc * 128:(kc + 1) * 128, c0:c0 + 512])
                for t4 in range(4):
                    mm(banks[bs + t4], XT[:, kc * 512 + t4 * 128: kc * 512 + (t4 + 1) * 128], w, kc == 0, kc == 7,
                       [RX[t4], wb], [B_bank[bs + t4]])
            for t4 in range(4):
                t = blk * 4 + t4
                if cg < 2:
                    act(L["VB"][:, 0:512], banks[bs + t4], AF.Copy, [B_bank[bs + t4]], [L["B_VB"]])
                    dma("pool", gv_d[t * 128:(t + 1) * 128, cg * 512:(cg + 1) * 512], L["VB"][:, 0:512], [L["B_VB"]], [B_gv], B_gv)
                else:
                    act(L["SR"][:, 0:512], banks[bs + t4], AF.Silu, [B_bank[bs + t4]], [L["B_SR"]])
                    dma("pool", gsr_d[t * 128:(t + 1) * 128, (cg - 2) * 512:(cg - 1) * 512], L["SR"][:, 0:512], [L["B_SR"]], [B_gsr], B_gsr)

    def load_mla_vecs(j):
        dma("sp", gq_b, mla_qn_d[j:j + 1, :].partition_broadcast(128), (), [B_gqb], B_gqb)
        dma("sp", gkv_b, mla_kvn_d[j:j + 1, :].partition_broadcast(128), (), [B_gkvb], B_gkvb)

    def load_gla_vecs(j):
        dma("pool", wa2x[0:17, :], gla_wa2x_d[j], (), [B_wa2x], B_wa2x)

    def layer_norm(L, t4, gi, bi):
        XAt = L["XA"][:, t4 * 1024:(t4 + 1) * 1024]
        BX = L["B_XAt"][t4]
        ST = L["ST"]
        BS = L["B_ST"]
        P.op("dve", lambda e: e.bn_stats(out=ST[:, 8:14], in_=XAt[:, 0:512]), [BX], [BS])
        P.op("dve", lambda e: e.bn_stats(out=ST[:, 14:20], in_=XAt[:, 512:1024]), [BX], [BS])
        P.op("dve", lambda e: e.bn_aggr(out=ST[:, 20:22], in_=ST[:, 8:20]), [BS], [BS])
        ts("dve", ST[:, 22:23], ST[:, 21:22], EPS, None, ALU.add, None, [BS], [BS])
        act(ST[:, 23:24], ST[:, 22:23], AF.Sqrt, [BS], [BS])
        P.op("dve", lambda e: e.reciprocal(out=ST[:, 24:25], in_=ST[:, 23:24]), [BS], [BS])
        ts("dve", XAt, XAt, ST[:, 20:21], ST[:, 24:25], ALU.subtract, ALU.mult, [BX, BS], [BX])
        tt("pool", XAt, XAt, L["VEC"][gi], ALU.mult, [BX, L["B_VEC"][gi]], [BX])
        tt("pool", XAt, XAt, L["VEC"][bi], ALU.add, [BX, L["B_VEC"][bi]], [BX])

    def tail(i, nk, wo_d, nxt):
        precast_stage(i + 1)
        L = tail_layout(not (nxt is not None and nxt[0] == 'gla'))
        XA, XT, OTB, ACTT, PTB, TMP, FU, FG = (L[k] for k in ("XA", "XT", "OTB", "ACTT", "PTB", "TMP", "FU", "FG"))
        def load_vec(slot, src):
            dma("sp", L["VEC"][slot], src[i:i + 1, :].partition_broadcast(128), (), [L["B_VEC"][slot]], L["B_VEC"][slot])
        dma("sp", convp, convp_d[i], (), [B_convp], B_convp)
        memset("dve", HALO[0], 0.0, [B_halo[0]])
        if nxt is not None and nxt[0] == 'mla':
            load_mla_vecs(nxt[1])
        if nxt is not None and nxt[0] == 'gla':
            load_gla_vecs(nxt[1])
        xsrc = x_d if i == 0 else xres_d
        cw = v3(convp, 44)
        for blk in range(NBLK):
            hp, hn = HALO[blk % 2], HALO[(blk + 1) % 2]
            Bhp, Bhn = B_halo[blk % 2], B_halo[(blk + 1) % 2]
            t0 = blk * 4
            dma("sp", v3(OTB[:, 0:nk * 512], nk), oT_d[0:nk * 128, blk * 512:(blk + 1) * 512].rearrange("(k p) s -> p k s", p=128),
                [B_oT], [L["B_OTB"]], L["B_OTB"])
            for t4 in range(4):
                dma("sp", XA[:, t4 * 1024:(t4 + 1) * 1024], xsrc[(t0 + t4) * 128:(t0 + t4 + 1) * 128, :],
                    [B_xres] if i > 0 else [], [L["B_XAt"][t4]], L["B_XAt"][t4])
            dma("sp", v3(PTB, 2), pT_b[i][:, blk * 512:(blk + 1) * 512].rearrange("(k p) s -> p k s", p=128), [mirror_buf[pT_b[i].name]], [L["B_PTB"]], L["B_PTB"])
            for half in range(2):
                for kc in range(nk):
                    w, wb = wt(wo_d[kc * 128:(kc + 1) * 128, half * 512:(half + 1) * 512])
                    for t4 in range(4):
                        mm(banks[4 * half + t4], OTB[:, kc * 512 + t4 * 128: kc * 512 + (t4 + 1) * 128], w, kc == 0, kc == nk - 1,
                           [L["B_OTB"], wb], [B_bank[4 * half + t4]])
            load_vec(0, ln1g_d)
            load_vec(1, ln1b_d)
            for t4 in range(4):
                for half in range(2):
                    xs = XA[:, t4 * 1024 + half * 512: t4 * 1024 + (half + 1) * 512]
                    stt(xs, xs, ALPHA, banks[4 * half + t4], ALU.mult, ALU.add, [L["B_XAt"][t4], B_bank[4 * half + t4]], [L["B_XAt"][t4]])
                layer_norm(L, t4, 0, 1)
                to_featmajor(L, t4, t4)
            hv = v3(hp, 44)
            cr = v3(CORR, 44)
            tmpa = L["ST"][:, 0:44]
            ta = FU[:, 0:44]
            tb2 = FU[:, 64:108]
            tt("dve", ta, hv[:, :, 1], cw[:, :, 1], ALU.mult, [Bhp, B_convp], [L["B_FU"]])
            tt("dve", tb2, hv[:, :, 0], cw[:, :, 0], ALU.mult, [Bhp, B_convp], [L["B_FU"]])
            tt("dve", cr[:, :, 0], ta, tb2, ALU.add, [L["B_FU"]], [B_corr])
            tt("dve", cr[:, :, 1], hv[:, :, 1], cw[:, :, 0], ALU.mult, [Bhp, B_convp], [B_corr])
            for pr in range(11):
                bs_ = 0 if pr % 2 == 0 else 4
                c0 = 2 * pr
                for kc in range(8):
                    wu_, wub = wt(wup_b[i][kc * 128:(kc + 1) * 128, c0 * 128: c0 * 128 + 256], cols=256)
                    wg_, wgb = wt(wup_b[i][kc * 128:(kc + 1) * 128, DFF + c0 * 128: DFF + c0 * 128 + 256], cols=256)
                    for cc in range(2):
                        mm(banks[bs_ + cc], wu_[:, cc * 128:(cc + 1) * 128], XT[:, kc * 512:(kc + 1) * 512], kc == 0, kc == 7,
                           L["B_XTt"] + [wub], [B_bank[bs_ + cc]])
                        mm(banks[bs_ + 2 + cc], wg_[:, cc * 128:(cc + 1) * 128], XT[:, kc * 512:(kc + 1) * 512], kc == 0, kc == 7,
                           L["B_XTt"] + [wgb], [B_bank[bs_ + 2 + cc]])
                for cc in range(2):
                    c = c0 + cc
                    bu, bg = bs_ + cc, bs_ + 2 + cc
                    FUc, BFUc = (FU, L["B_FU"]) if cc == 0 else (L["FU2"], L["B_FU2"])
                    FGc, BFGc = (FG, L["B_FG"]) if cc == 0 else (L["FG2"], L["B_FG2"])
                    for (bk_, ci, Fo, BFo) in ((bu, c, FUc, BFUc), (bg, 22 + c, FGc, BFGc)):
                        pb = banks[bk_]
                        act(Fo, pb, AF.Identity, [B_bank[bk_], B_convp], [BFo], scale=cw[:, ci, 2:3], bias=cw[:, ci, 3:4])
                        stt(Fo[:, 1:512], pb[:, 0:511], cw[:, ci, 1:2], Fo[:, 1:512], ALU.mult, ALU.add, [B_bank[bk_], B_convp, BFo], [BFo])
                        stt(Fo[:, 2:512], pb[:, 0:510], cw[:, ci, 0:1], Fo[:, 2:512], ALU.mult, ALU.add, [B_bank[bk_], B_convp, BFo], [BFo])
                        tt("dve", Fo[:, 0:2], Fo[:, 0:2], cr[:, ci, :], ALU.add, [BFo, B_corr], [BFo])
                        act(v3(hn, 44)[:, ci, :], pb[:, 510:512], AF.Copy, [B_bank[bk_]], [Bhn])
                    act(FGc, FGc, AF.Gelu, [BFGc], [BFGc])
                    tt("pool", ACTT[:, c * 512:(c + 1) * 512], FUc, FGc, ALU.mult, [BFUc, BFGc], [L["B_ACTT"]])
            for half in range(2):
                bs = 4 if half == 0 else 0
                for c in range(NFC):
                    w, wb = wt(wdn_b[i][c * 128:(c + 1) * 128, half * 512:(half + 1) * 512])
                    for t4 in range(4):
                        mm(banks[bs + t4], ACTT[:, c * 512 + t4 * 128: c * 512 + (t4 + 1) * 128], w, c == 0, c == NFC - 1,
                           [L["B_ACTT"], wb], [B_bank[bs + t4]])
            load_vec(0, ln2g_d)
            load_vec(1, ln2b_d)
            for t4 in range(4):
                for half in range(2):
                    bs = 4 if half == 0 else 0
                    xs = XA[:, t4 * 1024 + half * 512: t4 * 1024 + (half + 1) * 512]
                    stt(xs, xs, ALPHA, banks[bs + t4], ALU.mult, ALU.add, [L["B_XAt"][t4], B_bank[bs + t4]], [L["B_XAt"][t4]])
                layer_norm(L, t4, 0, 1)
                to_featmajor(L, t4, t4)
            load_vec(0, bg_d)
            for half in range(2):
                for kc in range(8):
                    w, wb = wt(wpg_b[i][kc * 128:(kc + 1) * 128, half * 512:(half + 1) * 512])
                    for t4 in range(4):
                        mm(banks[t4], XT[:, kc * 512 + t4 * 128: kc * 512 + (t4 + 1) * 128], w, kc == 0, kc == 7,
                           [L["B_XTt"][t4], wb], [B_bank[t4]])
                for kc in range(2):
                    w, wb = wt(wpp_b[i][kc * 128:(kc + 1) * 128, half * 512:(half + 1) * 512])
                    for t4 in range(4):
                        mm(banks[4 + t4], PTB[:, kc * 512 + t4 * 128: kc * 512 + (t4 + 1) * 128], w, kc == 0, kc == 1,
                           [L["B_PTB"], wb], [B_bank[4 + t4]])
                for t4 in range(4):
                    G = TMP[:, 0:512]
                    tt("dve", G, banks[t4], L["VEC"][0][:, half * 512:(half + 1) * 512], ALU.add, [B_bank[t4], L["B_VEC"][0]], [L["B_TMP"]])
                    act(G, G, AF.Sigmoid, [L["B_TMP"]], [L["B_TMP"]])
                    tt("dve", G, G, banks[4 + t4], ALU.mult, [L["B_TMP"], B_bank[4 + t4]], [L["B_TMP"]])
                    xs = XA[:, t4 * 1024 + half * 512: t4 * 1024 + (half + 1) * 512]
                    tt("pool", xs, xs, G, ALU.add, [L["B_XAt"][t4], L["B_TMP"]], [L["B_XAt"][t4]])
            dst = out_d if nxt is None else xres_d
            Bd = B_out if nxt is None else B_xres
            for t4 in range(4):
                dma("pool", dst[(t0 + t4) * 128:(t0 + t4 + 1) * 128, :], XA[:, t4 * 1024:(t4 + 1) * 1024], [L["B_XAt"][t4]], [Bd], Bd)
            if nxt is not None:
                for t4 in range(4):
                    to_featmajor(L, t4, t4)
                if nxt[0] == 'mla':
                    project_mla(L, nxt[1], blk)
                else:
                    project_gla(L, nxt[1], blk)
        P.barrier()

    def prologue():
        L = tail_layout(True)
        load_mla_vecs(0)
        for blk in range(NBLK):
            for t4 in range(4):
                t = blk * 4 + t4
                dma("sp", L["XA"][:, t4 * 1024:(t4 + 1) * 1024], x_d[t * 128:(t + 1) * 128, :], (), [L["B_XAt"][t4]], L["B_XAt"][t4])
            for t4 in range(4):
                to_featmajor(L, t4, t4)
            project_mla(L, 0, blk)
        P.barrier()

    def mla_attention(j):
        cv = Carver(arena)
        QN = [cv.bf(S) for _ in range(2)]; QR = [cv.bf(S) for _ in range(2)]; KN = [cv.bf(S) for _ in range(2)]
        B_QN = [[Buf() for _ in range(NBLK)] for _ in range(2)]
        B_QR = [[Buf() for _ in range(NBLK)] for _ in range(2)]
        B_KN = [[Buf() for _ in range(NBLK)] for _ in range(2)]
        V4 = cv.bf(NT * 512); B_V4 = [Buf() for _ in range(NT)]
        PT = [cv.bf(512) for _ in range(4)]; B_PT = [Buf() for _ in range(4)]
        ACCp = [cv.f32(512) for _ in range(2)]; B_ACCp = [Buf(), Buf()]
        ACCd = [cv.f32(512) for _ in range(2)]; B_ACCd = [Buf(), Buf()]
        RS = cv.f32(512); B_RS = Buf()
        OTs = [cv.bf(512) for _ in range(2)]; B_OTs = [Buf(), Buf()]
        ptc = [0]
        sbc = [0]
        for g in range(4):
            cs = slice(g * 512, (g + 1) * 512)
            WQN = [wt(wqn_b[j][kc * 128:(kc + 1) * 128, cs]) for kc in range(4)]
            WQR = [wt(wqr_b[j][kc * 128:(kc + 1) * 128, cs]) for kc in range(4)]
            WUK = [wt(wuk_b[j][kc * 128:(kc + 1) * 128, cs]) for kc in range(2)]
            WUV = [wt(wuv_b[j][kc * 128:(kc + 1) * 128, cs]) for kc in range(2)]
            for t in range(NT):
                pb = 6 + (t % 2)
                for kc in range(2):
                    mm(banks[pb], ckvT[:, kc * S + t * 128: kc * S + (t + 1) * 128], WUV[kc][0], kc == 0, kc == 1,
                       [B_ckv[t // 4], WUV[kc][1]], [B_bank[pb]])
                act(V4[:, t * 512:(t + 1) * 512], banks[pb], AF.Copy, [B_bank[pb]], [B_V4[t]])
            for hh in range(4):
                h = g * 4 + hh
                par = h % 2
                hs = slice(hh * 128, (hh + 1) * 128)
                for tb in range(NBLK):
                    tsl = slice(tb * 512, (tb + 1) * 512)
                    for kc in range(4):
                        mm(banks[6], WQN[kc][0][:, hs], cqT[:, kc * S + tb * 512: kc * S + (tb + 1) * 512], kc == 0, kc == 3,
                           [B_cq[tb], WQN[kc][1]], [B_bank[6]])
                    act(QN[par][:, tsl], banks[6], AF.Copy, [B_bank[6]], [B_QN[par][tb]])
                    for kc in range(4):
                        mm(banks[7], WQR[kc][0][:, hs], cqT[:, kc * S + tb * 512: kc * S + (tb + 1) * 512], kc == 0, kc == 3,
                           [B_cq[tb], WQR[kc][1]], [B_bank[7]])
                    tt("dve", QR[par][:, tsl], banks[7], T2[:, tsl], ALU.mult, [B_bank[7], B_T2], [B_QR[par][tb]])
                    for kc in range(2):
                        mm(banks[6], WUK[kc][0][:, hs], ckvT[:, kc * S + tb * 512: kc * S + (tb + 1) * 512], kc == 0, kc == 1,
                           [B_ckv[tb], WUK[kc][1]], [B_bank[6]])
                    act(KN[par][:, tsl], banks[6], AF.Copy, [B_bank[6]], [B_KN[par][tb]])
                items = []
                for qt in range(NBLK):
                    for jk in range(4 * qt + 4):
                        items.append((qt, jk, 4 * qt + 4))

                def emit_scores(it):
                    qt, jk, nj = it
                    r = jk - 4 * qt
                    q0 = max(r, 0) * 128
                    n = 512 - q0
                    sb = sbc[0] % 3
                    sbc[0] += 1
                    qsl = slice(qt * 512 + q0, qt * 512 + 512)
                    kb = jk // 4
                    mm(banks[sb][:, 0:n], KN[par][:, jk * 128:(jk + 1) * 128], QN[par][:, qsl], True, False,
                       [B_KN[par][kb], B_QN[par][qt]], [B_bank[sb]])
                    mm(banks[sb][:, 0:n], kr2T[:, jk * 128:(jk + 1) * 128], QR[par][:, qsl], False, True,
                       [B_kr[kb], B_QR[par][qt]], [B_bank[sb]])
                    return sb

                def epilogue(qt):
                    ob = 3 + (qt % 2)
                    ap_ = qt % 2
                    mm(banks[5], ones_f, ACCp[ap_], True, False, [B_ones, B_ACCp[ap_]], [B_bank[5]])
                    mm(banks[5], ones_f, ACCd[ap_], False, True, [B_ones, B_ACCd[ap_]], [B_bank[5]])
                    P.op("dve", lambda e: e.reciprocal(out=RS, in_=banks[5]), [B_bank[5]], [B_RS])
                    tt("dve", OTs[ap_], banks[ob], RS, ALU.mult, [B_bank[ob], B_RS], [B_OTs[ap_]])
                    dma("sp", oT_d[h * 128:(h + 1) * 128, qt * 512:(qt + 1) * 512], OTs[ap_], [B_OTs[ap_]], [B_oT], B_oT)

                pending = None
                sb_next = emit_scores(items[0])
                for idx, it in enumerate(items):
                    sb = sb_next
                    if idx + 1 < len(items):
                        sb_next = emit_scores(items[idx + 1])
                    qt, jk, nj = it
                    ob = 3 + (qt % 2)
                    ap_ = qt % 2
                    if jk == 0:
                        memset("pool", ACCp[ap_], 0.0, [B_ACCp[ap_]])
                        memset("dve", ACCd[ap_], 0.0, [B_ACCd[ap_]])
                    r = jk - 4 * qt
                    q0 = max(r, 0) * 128
                    n = 512 - q0
                    pt = ptc[0] % 4
                    ptc[0] += 1
                    act(PT[pt][:, 0:n], banks[sb][:, 0:n], AF.Exp, [B_bank[sb]], [B_PT[pt]], scale=ATT_SCALE)
                    eng = "pool" if jk % 2 == 0 else "dve"
                    ACC, BACC = (ACCp[ap_], B_ACCp[ap_]) if jk % 2 == 0 else (ACCd[ap_], B_ACCd[ap_])
                    if r >= 0:
                        tt(eng, PT[pt][:, 0:128], PT[pt][:, 0:128], MASK[:, jk * 128:(jk + 1) * 128], ALU.mult,
                           [B_PT[pt], B_MASK], [B_PT[pt]])
                    mm(banks[ob][:, q0:512], V4[:, jk * 512 + hh * 128: jk * 512 + (hh + 1) * 128], PT[pt][:, 0:n],
                       jk == 0, jk == nj - 1, [B_V4[jk], B_PT[pt]], [B_bank[ob]])
                    tt(eng, ACC[:, q0:512], ACC[:, q0:512], PT[pt][:, 0:n], ALU.add, [BACC, B_PT[pt]], [BACC])
                    if pending is not None:
                        pending[1] -= 1
                        if pending[1] == 0:
                            epilogue(pending[0])
                            pending = None
                    if jk == nj - 1:
                        if pending is not None:
                            epilogue(pending[0])
                        pending = [qt, 2]
                if pending is not None:
                    epilogue(pending[0])
        P.barrier()

    def gla_mixer(j):
        cv = Carver(arena)
        St = cv.f32(1024); B_St = Buf()
        Sb = cv.bf(1024); B_Sb = Buf()
        QT = [cv.bf(4 * 512) for _ in range(2)]; B_QT = [Buf(), Buf()]
        KC = [cv.bf(8 * 512) for _ in range(2)]; B_KC = [Buf(), Buf()]
        VC = [cv.bf(8 * 1024) for _ in range(2)]; B_VC = [Buf(), Buf()]
        SRt = [cv.f32(1024) for _ in range(2)]; B_SRt = [Buf(), Buf()]
        ON = cv.f32(1024); B_ON = Buf()
        OG = cv.bf(1024); B_OG = Buf()
        OGT = [cv.bf(1024) for _ in range(2)]; B_OGT = [Buf(), Buf()]
        onb = cv.f32(1024); B_onb = Buf()
        STG = cv.f32(64); B_STG = Buf()
        dma("sp", onb, gla_on_d[j:j + 1, :].partition_broadcast(128), (), [B_onb], B_onb)
        memset("dve", St, 0.0, [B_St])
        dv = v3(DECAY, 4)
        for blk in range(NBLK):
            p2 = blk % 2
            dma("sp", v3(QT[p2], 4), gq_d[:, blk * 512:(blk + 1) * 512].rearrange("(h p) s -> p h s", p=128), [B_gq], [B_QT[p2]], B_QT[p2])
            dma("sp", v3(KC[p2][0:64, :], 8), gk_d[blk * 512:(blk + 1) * 512, :].rearrange("(c p) f -> p c f", p=64), [B_gk], [B_KC[p2]], B_KC[p2])
            dma("sp", v3(VC[p2][0:64, :], 8), gv_d[blk * 512:(blk + 1) * 512, :].rearrange("(c p) f -> p c f", p=64), [B_gv], [B_VC[p2]], B_VC[p2])
            for t4 in range(4):
                t = blk * 4 + t4
                s2 = t % 2
                dma("sp", SRt[s2], gsr_d[t * 128:(t + 1) * 128, :], [B_gsr], [B_SRt[s2]], B_SRt[s2])
                for ch in range(2):
                    c = t4 * 2 + ch
                    n = t * 2 + ch
                    for h in range(4):
                        bk_ = h // 2
                        mm(banks[bk_][:, (h % 2) * 256:(h % 2 + 1) * 256], KC[p2][0:64, c * 512 + h * 128: c * 512 + (h + 1) * 128],
                           VC[p2][0:64, c * 1024 + h * 256: c * 1024 + (h + 1) * 256], True, True, [B_KC[p2], B_VC[p2]], [B_bank[bk_]])
                    for h in range(4):
                        bk_ = h // 2
                        stt(St[:, h * 256:(h + 1) * 256], St[:, h * 256:(h + 1) * 256], dv[:, h, n:n + 1],
                            banks[bk_][:, (h % 2) * 256:(h % 2 + 1) * 256], ALU.mult, ALU.add, [B_St, B_decay, B_bank[bk_]], [B_St])
                    act(Sb, St, AF.Copy, [B_St], [B_Sb])
                    ob = 2 + 2 * ch
                    for h in range(4):
                        bk_ = ob + h // 2
                        mm(banks[bk_][:, (h % 2) * 256:(h % 2 + 1) * 256], QT[p2][:, h * 512 + t4 * 128: h * 512 + (t4 + 1) * 128],
                           Sb[:, h * 256:(h + 1) * 256], True, True, [B_QT[p2], B_Sb], [B_bank[bk_]])
                for ch in range(2):
                    rs_ = slice(ch * 64, (ch + 1) * 64)
                    ob = 2 + 2 * ch
                    for h in range(4):
                        src = banks[ob + h // 2][rs_, (h % 2) * 256:(h % 2 + 1) * 256]
                        P.op("dve", lambda e, src=src, h=h, rs_=rs_: e.bn_stats(out=STG[rs_, h * 6:(h + 1) * 6], in_=src),
                             [B_bank[ob + h // 2]], [B_STG])
                for h in range(4):
                    P.op("dve", lambda e, h=h: e.bn_aggr(out=STG[:, 24 + 2 * h:26 + 2 * h], in_=STG[:, h * 6:(h + 1) * 6]), [B_STG], [B_STG])
                varv = STG[:, 24:32].rearrange("p (h two) -> p h two", two=2)[:, :, 1]
                ts("dve", STG[:, 32:36], varv, EPS, None, ALU.add, None, [B_STG], [B_STG])
                act(STG[:, 36:40], STG[:, 32:36], AF.Sqrt, [B_STG], [B_STG])
                P.op("dve", lambda e: e.reciprocal(out=STG[:, 40:44], in_=STG[:, 36:40]), [B_STG], [B_STG])
                for ch in range(2):
                    rs_ = slice(ch * 64, (ch + 1) * 64)
                    ob = 2 + 2 * ch
                    for h in range(4):
                        src = banks[ob + h // 2][rs_, (h % 2) * 256:(h % 2 + 1) * 256]
                        ts("dve", ON[rs_, h * 256:(h + 1) * 256], src, STG[rs_, 24 + 2 * h:25 + 2 * h], STG[rs_, 40 + h:41 + h],
                           ALU.subtract, ALU.mult, [B_bank[ob + h // 2], B_STG], [B_ON])
                tt("pool", ON, ON, onb, ALU.mult, [B_ON, B_onb], [B_ON])
                tt("pool", OG, ON, SRt[s2], ALU.mult, [B_ON, B_SRt[s2]], [B_OG])
                tb_ = banks_bf[6 + (t % 2)]
                Btb = B_bank[6 + (t % 2)]
                for kc in range(8):
                    tr(tb_[:, kc * 128:(kc + 1) * 128], OG[:, kc * 128:(kc + 1) * 128], ident_b, [B_OG, B_identb], [Btb])
                act(OGT[s2], tb_, AF.Copy, [Btb], [B_OGT[s2]])
                dma("sp", oT_d[0:1024, t * 128:(t + 1) * 128].rearrange("(k p) s -> p k s", p=128), v3(OGT[s2], 8),
                    [B_OGT[s2]], [B_oT], B_oT)
        P.barrier()

    precast_stage(0)
    prologue()
    for i in range(nlayers):
        j = i // 2
        last = (i == nlayers - 1)
        if i % 2 == 0:
            mla_attention(j)
            nxt = None if last else ('gla', j)
            tail(i, 16, mla_wo_b[j], nxt)
        else:
            gla_mixer(j)
            nxt = None if last else ('mla', j + 1)
            tail(i, 8, gla_wo_b[j], nxt)
    P.barrier()
    P.emit()
    st.close()
    return nc, P


_CACHE = {}


def _host_layout(inputs, b):
    f32 = np.float32
    d = {}
    d["x"] = np.ascontiguousarray(inputs["x"][b], dtype=f32)
    d["pT"] = np.ascontiguousarray(np.transpose(inputs["p"][:, b], (0, 2, 1)), dtype=f32)
    pos = np.asarray(inputs["positions"][b], dtype=np.int32)
    d["pos_row"] = np.ascontiguousarray(pos.reshape(1, S))
    d["pos_col"] = np.ascontiguousarray(pos.reshape(NT, 128).T)
    return d


def _shared_layout(inputs):
    f32 = np.float32
    d = {}
    inv = (1.0 / (10000.0 ** (np.arange(0, 64, 2, dtype=f32) / f32(64)))).astype(f32)
    invt = (inv.astype(np.float64) / (2 * np.pi)).astype(f32)
    ropecol = np.zeros((128, 2), f32)
    for p in range(128):
        ropecol[p, 0] = invt[p % 32]
        ropecol[p, 1] = 0.25 if p < 64 else (0.5 if p < 96 else 0.0)
    d["ropecol"] = ropecol
    row = np.concatenate([invt, invt, invt, invt]).astype(f32)
    d["inv2"] = np.ascontiguousarray(np.broadcast_to(row, (128, 128)))
    ph = np.concatenate([np.full(64, 0.25, f32), np.zeros(64, f32)])
    d["ph2"] = np.ascontiguousarray(np.broadcast_to(ph, (128, 128)))
    for k in ("mla_w_in", "mla_q_norm", "mla_kv_norm", "mla_w_uk", "mla_w_uv", "mla_w_o", "gla_w_in", "gla_o_norm", "gla_w_o",
              "ln1_g", "ln1_b", "ln2_g", "ln2_b", "ffn_w_up", "ffn_w_down", "ple_w_proj", "ple_w_gate", "ple_b_gate"):
        d[k] = np.ascontiguousarray(inputs[k], dtype=f32)
    wuq = np.asarray(inputs["mla_w_uq"], dtype=f32).reshape(2, 512, 16, 192)
    d["wq_n"] = np.ascontiguousarray(wuq[..., 0:128].reshape(2, 512, 2048))
    wr = wuq[..., 128:192]
    wr2 = np.concatenate([wr, wr[..., 32:64], wr[..., 0:32]], axis=-1)
    d["wq_r"] = np.ascontiguousarray(wr2.reshape(2, 512, 2048))
    d["gla_wa2x"] = np.ascontiguousarray(np.concatenate([np.asarray(inputs["gla_w_a2"], f32),
                                                          np.asarray(inputs["gla_b_a"], f32)[:, None, :]], axis=1))
    cw = np.asarray(inputs["ffn_conv_w"], f32)
    cb = np.asarray(inputs["ffn_conv_b"], f32)
    cat = np.concatenate([cw, cb[:, None, :]], axis=1)
    cat = cat.reshape(4, 4, 44, 128)
    d["convp"] = np.ascontiguousarray(np.transpose(cat, (0, 3, 2, 1)).reshape(4, 128, 176))
    return d


def kernel(**inputs):
    if "nc" not in _CACHE:
        _CACHE["nc"] = build()[0]
    nc = _CACHE["nc"]
    shared = _shared_layout(inputs)
    in_maps = []
    for b in range(8):
        m = dict(shared)
        m.update(_host_layout(inputs, b))
        in_maps.append(m)
    res = run_bass_kernel_spmd(nc, in_maps, core_ids=list(range(8)))
    out = np.stack([np.asarray(r["out"], dtype=np.float32) for r in res.results], axis=0)
    return out
```

```python
import contextlib
import math
import numpy as np
import concourse.bass as bass
import concourse.mybir as mybir
from concourse.bass_utils import run_bass_kernel_spmd

F32 = mybir.dt.float32
BF16 = mybir.dt.bfloat16
I32 = mybir.dt.int32
AF = mybir.ActivationFunctionType
ALU = mybir.AluOpType

ENGS = ("pe", "act", "dve", "pool", "sp")

S = 4096
D = 1024
NT = 32
TB = 512
NBLK = 8
DFF = 2816
NFC = 22
ALPHA = 8.0 ** 0.25
EPS = 1e-5
ATT_SCALE = 192.0 ** -0.5
GLA_QS = 128.0 ** -0.5
NW = 20
TWO_PI = 2.0 * math.pi


class Buf:
    __slots__ = ("name", "lw", "rd", "sem", "cnt")

    def __init__(self, name=""):
        self.name = name
        self.lw = None
        self.rd = {}
        self.sem = None
        self.cnt = 0


class Op:
    __slots__ = ("fn", "waits", "signal", "dma", "clock")

    def __init__(self, fn):
        self.fn = fn
        self.waits = []
        self.signal = False
        self.dma = None
        self.clock = None


class Prog:
    def __init__(self, nc):
        self.nc = nc
        self.ops = {e: [] for e in ENGS}
        self.clock = {e: {} for e in ENGS}
        self.dma_bufs = []

    def _need(self, eng, tok, raw):
        ck = self.clock[eng]
        if tok[0] == 'e':
            _, E, i = tok
            if E == eng and not raw:
                return None
            if ck.get(E, -1) >= i:
                return None
            ck[E] = i
            src = self.ops[E][i]
            src.signal = True
            for k, v in src.clock.items():
                if ck.get(k, -1) < v:
                    ck[k] = v
            return tok
        _, b, c = tok
        if ck.get(b, -1) >= c:
            return None
        ck[b] = c
        return tok

    def op(self, eng, fn, reads=(), writes=(), dma=None):
        rec = Op(fn)
        lst = self.ops[eng]
        idx = len(lst)
        waits = rec.waits
        for b in reads:
            if b.lw is not None:
                w = self._need(eng, b.lw, True)
                if w:
                    waits.append(w)
        for b in writes:
            if b.lw is not None:
                w = self._need(eng, b.lw, False)
                if w:
                    waits.append(w)
            for t in b.rd.values():
                w = self._need(eng, t, False)
                if w:
                    waits.append(w)
        rec.clock = dict(self.clock[eng])
        if dma is not None:
            if dma.sem is None:
                dma.sem = True
                self.dma_bufs.append(dma)
            dma.cnt += 16
            tok = ('d', dma, dma.cnt)
            key = dma
            rec.dma = dma
        else:
            tok = ('e', eng, idx)
            key = eng
        for b in writes:
            b.lw = tok
            b.rd = {}
        for b in reads:
            b.rd[key] = tok
        lst.append(rec)
        return rec

    def barrier(self):
        toks = []
        for e in ENGS:
            lst = self.ops[e]
            for i in range(len(lst) - 1, -1, -1):
                if lst[i].dma is None and lst[i].fn is not None:
                    toks.append(('e', e, i))
                    break
        dtoks = [('d', b, b.cnt) for b in self.dma_bufs if b.cnt > 0]
        for e in ENGS:
            rec = Op(None)
            for t in toks + dtoks:
                w = self._need(e, t, True)
                if w:
                    rec.waits.append(w)
            rec.clock = dict(self.clock[e])
            self.ops[e].append(rec)

    def emit(self):
        nc = self.nc
        with contextlib.ExitStack() as st:
            esem = {e: st.enter_context(nc.semaphore("es_" + e)) for e in ENGS}
            for i, b in enumerate(self.dma_bufs):
                b.sem = st.enter_context(nc.semaphore("ds%d" % i))
            sigidx = {}
            for e in ENGS:
                c = 0
                arr = []
                for rec in self.ops[e]:
                    if rec.signal:
                        c += 1
                    arr.append(c)
                sigidx[e] = arr
            block = st.enter_context(nc.Block())

            def run(e, engobj):
                for rec in self.ops[e]:
                    for w in rec.waits:
                        if w[0] == 'e':
                            engobj.wait_ge(esem[w[1]], sigidx[w[1]][w[2]])
                        else:
                            engobj.wait_ge(w[1].sem, w[2])
                    if rec.fn is None:
                        continue
                    ins = rec.fn(engobj)
                    if rec.dma is not None:
                        ins.then_inc(rec.dma.sem, 16)
                    elif rec.signal:
                        ins.then_inc(esem[e], 1)

            @block.tensor
            def _(eng):
                run("pe", eng)

            @block.scalar
            def _(eng):
                run("act", eng)

            @block.vector
            def _(eng):
                run("dve", eng)

            @block.gpsimd
            def _(eng):
                run("pool", eng)

            @block.sync
            def _(eng):
                run("sp", eng)


class Carver:
    def __init__(self, ap):
        self.ap = ap
        self.off = 0
        self.hi = 0

    def bf(self, n):
        v = self.ap[:, self.off:self.off + n]
        self.off += n + (n & 1)
        self.hi = max(self.hi, self.off)
        assert self.off <= self.ap.shape[1], ("arena overflow", self.off, self.ap.shape)
        return v

    def f32(self, n):
        return self.bf(2 * n).bitcast(F32)

    def i32(self, n):
        return self.bf(2 * n).bitcast(I32)


def build(nlayers=4, dbg=False):
    nc = bass.Bass("TRN2", target_bir_lowering=False)
    P = Prog(nc)

    def din(name, shape, dt=F32):
        return nc.dram_tensor(name, shape, dt, kind="ExternalInput").ap()

    x_d = din("x", [S, D])
    pT_d = din("pT", [4, 256, S])
    posr_d = din("pos_row", [1, S], I32)
    posc_d = din("pos_col", [128, NT], I32)
    ropecol_d = din("ropecol", [128, 2])
    inv2_d = din("inv2", [128, 128])
    ph2_d = din("ph2", [128, 128])
    mla_w_in_d = din("mla_w_in", [2, 1024, 832])
    mla_qn_d = din("mla_q_norm", [2, 512])
    mla_kvn_d = din("mla_kv_norm", [2, 256])
    wqn_d = din("wq_n", [2, 512, 2048])
    wqr_d = din("wq_r", [2, 512, 2048])
    wuk_d = din("mla_w_uk", [2, 256, 2048])
    wuv_d = din("mla_w_uv", [2, 256, 2048])
    mla_wo_d = din("mla_w_o", [2, 2048, 1024])
    gla_w_in_d = din("gla_w_in", [2, 1024, 3088])
    gla_wa2x_d = din("gla_wa2x", [2, 17, 512])
    gla_on_d = din("gla_o_norm", [2, 1024])
    gla_wo_d = din("gla_w_o", [2, 1024, 1024])
    ln1g_d = din("ln1_g", [4, 1024])
    ln1b_d = din("ln1_b", [4, 1024])
    ln2g_d = din("ln2_g", [4, 1024])
    ln2b_d = din("ln2_b", [4, 1024])
    wup_d = din("ffn_w_up", [4, 1024, 5632])
    convp_d = din("convp", [4, 128, 176])
    wdn_d = din("ffn_w_down", [4, 2816, 1024])
    wpp_d = din("ple_w_proj", [4, 256, 1024])
    wpg_d = din("ple_w_gate", [4, 1024, 1024])
    bg_d = din("ple_b_gate", [4, 1024])
    out_d = nc.dram_tensor("out", [S, D], F32, kind="ExternalOutput").ap()
    okind = "ExternalOutput" if dbg else "Internal"
    xres_d = nc.dram_tensor("xres", [S, D], F32, kind=okind).ap()
    oT_d = nc.dram_tensor("oT", [2048, S], BF16, kind=okind).ap()
    gq_d = nc.dram_tensor("gq", [512, S], BF16, kind="Internal").ap()
    gk_d = nc.dram_tensor("gk", [S, 512], BF16, kind="Internal").ap()
    gv_d = nc.dram_tensor("gv", [S, 1024], BF16, kind="Internal").ap()
    gsr_d = nc.dram_tensor("gsr", [S, 1024], F32, kind="Internal").ap()
    B_xres, B_oT, B_gq, B_gk, B_gv, B_gsr, B_out = (Buf(n) for n in ("xres", "oT", "gq", "gk", "gv", "gsr", "out"))

    st = contextlib.ExitStack()

    def sbt(name, cols, dt):
        return st.enter_context(nc.sbuf_tensor(name, [128, cols], dt))[:]

    ident_f = sbt("ident_f", 128, F32); B_identf = Buf()
    ident_b = sbt("ident_b", 128, BF16); B_identb = Buf()
    ones_f = sbt("ones_f", 128, F32); B_ones = Buf()
    tri_f = sbt("tri_f", 128, F32); B_tri = Buf()
    ind_f = sbt("ind_f", 2, F32); B_ind = Buf()
    T2 = sbt("T2", S, BF16); B_T2 = Buf()
    MASK = sbt("MASK", S, BF16); B_MASK = Buf()
    cqT = sbt("cqT", 4 * S, BF16)
    ckvT = sbt("ckvT", 2 * S, BF16)
    kr2T = sbt("kr2T", S, BF16)
    B_cq = [Buf() for _ in range(NBLK)]
    B_ckv = [Buf() for _ in range(NBLK)]
    B_kr = [Buf() for _ in range(NBLK)]
    posc_i = sbt("posc_i", NT, I32); B_posci = Buf()
    posc_f = sbt("posc_f", NT, F32); B_poscf = Buf()
    cidc_f = sbt("cidc_f", NT, F32); B_cidc = Buf()
    ropecol = sbt("ropecol_s", 2, F32); B_ropecol = Buf()
    inv2 = sbt("inv2_s", 128, F32); B_inv2 = Buf()
    ph2 = sbt("ph2_s", 128, F32); B_ph2 = Buf()
    convp = sbt("convp_s", 176, F32); B_convp = Buf()
    gq_b = sbt("gq_b", 512, F32); B_gqb = Buf()
    gkv_b = sbt("gkv_b", 256, F32); B_gkvb = Buf()
    DECAY = sbt("DECAY", 256, F32); B_decay = Buf()
    wa2x = sbt("wa2x", 512, BF16); B_wa2x = Buf()
    AT = sbt("AT", 512, BF16); B_AT = Buf()
    HALO = [sbt("HALO%d" % i, 88, F32) for i in range(2)]; B_halo = [Buf(), Buf()]
    CORR = sbt("CORR", 88, F32); B_corr = Buf()
    smalls = sbt("smalls", 64, F32)
    wslots = [(sbt("w%d" % i, 512, BF16), Buf("w%d" % i)) for i in range(NW)]
    wctr = [0]
    arena_cols = (nc.sbuf_bytes_remaining - 1024) // 2
    arena_cols -= arena_cols % 64
    arena = sbt("arena", arena_cols, BF16)
    banks = [st.enter_context(nc.psum_tensor("bank%d" % i, [128, 512], F32))[:] for i in range(8)]
    banks_bf = [b.bitcast(BF16) for b in banks]
    B_bank = [Buf("bank%d" % i) for i in range(8)]

    def mm(out, lhsT, rhs, start, stop, R, W):
        P.op("pe", lambda e: e.matmul(out, lhsT=lhsT, rhs=rhs, start=start, stop=stop), R, W)

    def tr(out, in_, ident, R, W):
        P.op("pe", lambda e: e.transpose(out, in_, ident), R, W)

    def act(out, in_, func, R, W, scale=1.0, bias=0.0, accum=None):
        if accum is None:
            P.op("act", lambda e: e.activation(out=out, in_=in_, func=func, bias=bias, scale=scale), R, W)
        else:
            P.op("act", lambda e: e.activation(out=out, in_=in_, func=func, bias=bias, scale=scale, accum_out=accum), R, W)

    def ts(eng, out, in0, s1, s2, op0, op1, R, W):
        if s2 is None:
            P.op(eng, lambda e: e.tensor_scalar(out=out, in0=in0, scalar1=s1, scalar2=None, op0=op0), R, W)
        else:
            P.op(eng, lambda e: e.tensor_scalar(out=out, in0=in0, scalar1=s1, scalar2=s2, op0=op0, op1=op1), R, W)

    def tt(eng, out, in0, in1, op, R, W):
        P.op(eng, lambda e: e.tensor_tensor(out=out, in0=in0, in1=in1, op=op), R, W)

    def stt(out, in0, scalar, in1, op0, op1, R, W):
        P.op("dve", lambda e: e.scalar_tensor_tensor(out=out, in0=in0, scalar=scalar, in1=in1, op0=op0, op1=op1), R, W)

    def cp(eng, out, in_, R, W):
        P.op(eng, lambda e: e.tensor_copy(out=out, in_=in_), R, W)

    def memset(eng, ap, val, W):
        P.op(eng, lambda e: e.memset(ap, val), (), W)

    def dma(eng, out, in_, R, W, dbuf):
        P.op(eng, lambda e: e.dma_start(out=out, in_=in_), R, W, dma=dbuf)

    mirror_buf = {}

    def mirror2d(src2d, name):
        return nc.dram_tensor("bf_" + name, list(src2d.shape), BF16, kind="Internal").ap()

    def precast(gbuf, pairs):
        for src, dst in pairs:
            R_, C_ = src.shape
            rows = R_ if R_ <= 128 else max(128, ((1 << 19) // C_) // 128 * 128)
            for r0 in range(0, R_, rows):
                r1 = min(R_, r0 + rows)
                dma("pool", dst[r0:r1, :], src[r0:r1, :], (), (), gbuf)
            mirror_buf[dst.name] = gbuf
        gbuf.lw = ('d', gbuf, gbuf.cnt)

    def wt(src, rows=128, cols=512):
        ap, b = wslots[wctr[0] % NW]
        wctr[0] += 1
        v = ap[0:rows, 0:cols]
        dma("sp", v, src, [mirror_buf[src.name]], [b], b)
        return v, b

    mla_w_in_f, wqn_f, wqr_f, wuk_f, wuv_f, mla_wo_f = mla_w_in_d, wqn_d, wqr_d, wuk_d, wuv_d, mla_wo_d
    gla_w_in_f, gla_wo_f, wup_f, wdn_f, wpp_f, wpg_f, pT_f = gla_w_in_d, gla_wo_d, wup_d, wdn_d, wpp_d, wpg_d, pT_d
    mla_w_in_b = [mirror2d(mla_w_in_f[j], "mla_w_in%d" % j) for j in range(2)]
    wqn_b = [mirror2d(wqn_f[j], "wqn%d" % j) for j in range(2)]
    wqr_b = [mirror2d(wqr_f[j], "wqr%d" % j) for j in range(2)]
    wuk_b = [mirror2d(wuk_f[j], "wuk%d" % j) for j in range(2)]
    wuv_b = [mirror2d(wuv_f[j], "wuv%d" % j) for j in range(2)]
    mla_wo_b = [mirror2d(mla_wo_f[j], "mla_wo%d" % j) for j in range(2)]
    gla_w_in_b = [mirror2d(gla_w_in_f[j], "gla_w_in%d" % j) for j in range(2)]
    gla_wo_b = [mirror2d(gla_wo_f[j], "gla_wo%d" % j) for j in range(2)]
    wup_b = [mirror2d(wup_f[i], "wup%d" % i) for i in range(4)]
    wdn_b = [mirror2d(wdn_f[i], "wdn%d" % i) for i in range(4)]
    wpp_b = [mirror2d(wpp_f[i], "wpp%d" % i) for i in range(4)]
    wpg_b = [mirror2d(wpg_f[i], "wpg%d" % i) for i in range(4)]
    pT_b = [mirror2d(pT_f[i], "pTb%d" % i) for i in range(4)]

    def tail_pairs(i):
        return [(wup_f[i], wup_b[i]), (wdn_f[i], wdn_b[i]), (wpg_f[i], wpg_b[i]), (wpp_f[i], wpp_b[i]), (pT_f[i], pT_b[i])]

    def precast_stage(k):
        if k == 0:
            precast(Buf("g_in0"), [(mla_w_in_f[0], mla_w_in_b[0])])
            precast(Buf("g_mix0"), [(wqn_f[0], wqn_b[0]), (wqr_f[0], wqr_b[0]), (wuk_f[0], wuk_b[0]), (wuv_f[0], wuv_b[0])])
            precast(Buf("g_tail0"), [(mla_wo_f[0], mla_wo_b[0])] + tail_pairs(0) + [(gla_w_in_f[0], gla_w_in_b[0])])
        elif k == 1:
            precast(Buf("g1"), [(gla_wo_f[0], gla_wo_b[0])] + tail_pairs(1) + [(mla_w_in_f[1], mla_w_in_b[1])])
        elif k == 2:
            precast(Buf("g2"), [(wqn_f[1], wqn_b[1]), (wqr_f[1], wqr_b[1]), (wuk_f[1], wuk_b[1]), (wuv_f[1], wuv_b[1]),
                                (mla_wo_f[1], mla_wo_b[1])] + tail_pairs(2) + [(gla_w_in_f[1], gla_w_in_b[1])])
        elif k == 3:
            precast(Buf("g3"), [(gla_wo_f[1], gla_wo_b[1])] + tail_pairs(3))

    def v3(ap, a):
        return ap.rearrange("p (a b) -> p a b", a=a)

    memset("pool", ident_f, 1.0, [B_identf])
    P.op("pool", lambda e: e.affine_select(out=ident_f, in_=ident_f, pattern=[[-1, 128]], compare_op=ALU.is_equal,
                                           fill=0.0, base=0, channel_multiplier=1), [B_identf], [B_identf])
    cp("dve", ident_b, ident_f, [B_identf], [B_identb])
    memset("pool", ones_f, 1.0, [B_ones])
    memset("pool", tri_f, 1.0 / 16.0, [B_tri])
    P.op("pool", lambda e: e.affine_select(out=tri_f, in_=tri_f, pattern=[[-1, 128]], compare_op=ALU.is_gt,
                                           fill=0.0, base=0, channel_multiplier=1), [B_tri], [B_tri])
    memset("pool", tri_f[64:128, 0:64], 0.0, [B_tri])
    memset("pool", ind_f, 0.0, [B_ind])
    memset("pool", ind_f[0:64, 0:1], 1.0 / 16.0, [B_ind])
    memset("pool", ind_f[64:128, 1:2], 1.0 / 16.0, [B_ind])
    memset("pool", AT[0:32, :], 1.0, [B_AT])
    dma("sp", posc_i, posc_d, (), [B_posci], B_posci)
    dma("sp", ropecol, ropecol_d, (), [B_ropecol], B_ropecol)
    dma("sp", inv2, inv2_d, (), [B_inv2], B_inv2)
    dma("sp", ph2, ph2_d, (), [B_ph2], B_ph2)
    cp("dve", posc_f, posc_i, [B_posci], [B_poscf])
    cv = Carver(arena)
    R0 = cv.bf(2 * S); R1 = cv.bf(2 * S); R2 = cv.bf(2 * S); R3 = cv.bf(2 * S)
    B_R = [Buf() for _ in range(4)]
    tmpi = smalls[:, 0:32].bitcast(I32); B_tmpi = Buf()
    ts("dve", tmpi, posc_i, 6, None, ALU.arith_shift_right, None, [B_posci], [B_tmpi])
    cp("dve", cidc_f, tmpi, [B_tmpi], [B_cidc])
    dma("sp", R0.bitcast(I32), posr_d.partition_broadcast(128), (), [B_R[0]], B_R[0])
    cp("dve", R1.bitcast(F32), R0.bitcast(I32), [B_R[0]], [B_R[1]])
    ts("dve", R2.bitcast(I32), R0.bitcast(I32), 6, None, ALU.arith_shift_right, None, [B_R[0]], [B_R[2]])
    cp("dve", R0.bitcast(F32), R2.bitcast(I32), [B_R[2]], [B_R[0]])
    for b in range(NT):
        ts("dve", MASK[:, b * 128:(b + 1) * 128], R0.bitcast(F32)[:, b * 128:(b + 1) * 128], cidc_f[:, b:b + 1], None,
           ALU.is_ge, None, [B_R[0], B_cidc], [B_MASK])
    ts("dve", R2.bitcast(F32), R1.bitcast(F32), ropecol[:, 0:1], ropecol[:, 1:2], ALU.mult, ALU.add, [B_R[1], B_ropecol], [B_R[2]])
    cp("dve", R1.bitcast(I32), R2.bitcast(F32), [B_R[2]], [B_R[1]])
    cp("dve", R3.bitcast(F32), R1.bitcast(I32), [B_R[1]], [B_R[3]])
    tt("dve", R2.bitcast(F32), R2.bitcast(F32), R3.bitcast(F32), ALU.subtract, [B_R[2], B_R[3]], [B_R[2]])
    stt(R2.bitcast(F32), R2.bitcast(F32), 0.5, R2.bitcast(F32), ALU.is_gt, ALU.subtract, [B_R[2]], [B_R[2]])
    act(T2, R2.bitcast(F32), AF.Sin, [B_R[2]], [B_T2], scale=-TWO_PI)
    P.barrier()

    TBUFS = {}

    def tail_layout(with_cq_live):
        cv = Carver(arena)
        L = {}
        L["VEC"] = [cv.f32(1024) for _ in range(3)]
        L["XA"] = cv.f32(4 * 1024)
        L["XBF"] = cv.bf(1024)
        L["XT"] = cv.bf(8 * 512)
        L["OTB"] = cv.bf(16 * 512)
        L["ACTT"] = cv.bf(22 * 512)
        L["PTB"] = cv.bf(2 * 512)
        L["TMP"] = cv.f32(1024)
        L["FU"] = cv.f32(512)
        L["FG"] = cv.f32(512)
        L["FU2"] = cv.f32(512)
        L["FG2"] = cv.f32(512)
        L["SQJ"] = cv.bf(512)
        L["CQN"] = cv.bf(768)
        L["KR"] = cv.bf(128)
        L["CS"] = cv.f32(512)
        L["CSI"] = cv.i32(512)
        L["CSF"] = cv.f32(512)
        L["XC"] = cv.f32(64)
        L["XS"] = cv.f32(64)
        L["ST"] = cv.f32(64)
        if not with_cq_live:
            cv = Carver(cqT)
            L["LA"] = cv.f32(512)
            L["AZ"] = cv.f32(512)
            L["DEC"] = cv.f32(512)
            L["KD"] = cv.bf(512)
            L["VB"] = cv.bf(1024)
            L["SR"] = cv.f32(1024)
            L["QTS"] = cv.bf(512)
        for k in list(L.keys()):
            if k == "VEC":
                L["B_VEC"] = TBUFS.setdefault("VEC", [Buf() for _ in range(3)])
            else:
                L["B_" + k] = TBUFS.setdefault(k, Buf(k))
        L["B_XAt"] = TBUFS.setdefault("XAt", [Buf() for _ in range(4)])
        L["B_XTt"] = TBUFS.setdefault("XTt", [Buf() for _ in range(4)])
        return L

    def to_featmajor(L, tt_i, tbank):
        XAt = L["XA"][:, tt_i * 1024:(tt_i + 1) * 1024]
        act(L["XBF"], XAt, AF.Copy, [L["B_XAt"][tt_i]], [L["B_XBF"]])
        for kc in range(8):
            tr(banks_bf[tbank][:, kc * 128:(kc + 1) * 128], L["XBF"][:, kc * 128:(kc + 1) * 128], ident_b,
               [L["B_XBF"], B_identb], [B_bank[tbank]])
        cp("dve", v3(L["XT"], 8)[:, :, tt_i * 128:(tt_i + 1) * 128], v3(banks_bf[tbank], 8),
           [B_bank[tbank]], [L["B_XTt"][tt_i]])

    def project_mla(L, j, blk):
        XT = L["XT"]
        RX = L["B_XTt"]
        for kc in range(8):
            w, wb = wt(mla_w_in_b[j][kc * 128:(kc + 1) * 128, 0:512])
            for t4 in range(4):
                mm(banks[t4], XT[:, kc * 512 + t4 * 128: kc * 512 + (t4 + 1) * 128], w, kc == 0, kc == 7,
                   [RX[t4], wb], [B_bank[t4]])
        for kc in range(8):
            w, wb = wt(mla_w_in_b[j][kc * 128:(kc + 1) * 128, 512:832], cols=320)
            for t4 in range(4):
                mm(banks[4 + t4][:, 0:320], XT[:, kc * 512 + t4 * 128: kc * 512 + (t4 + 1) * 128], w, kc == 0, kc == 7,
                   [RX[t4], wb], [B_bank[4 + t4]])
        CS, CSI, CSF = L["CS"], L["CSI"], L["CSF"]
        for t4 in range(4):
            t = blk * 4 + t4
            stt(CS[:, t4 * 128:(t4 + 1) * 128], inv2, posc_f[:, t:t + 1], ph2, ALU.mult, ALU.add,
                [B_inv2, B_poscf, B_ph2], [L["B_CS"]])
        cp("dve", CSI, CS, [L["B_CS"]], [L["B_CSI"]])
        cp("dve", CSF, CSI, [L["B_CSI"]], [L["B_CSF"]])
        tt("dve", CS, CS, CSF, ALU.subtract, [L["B_CS"], L["B_CSF"]], [L["B_CS"]])
        stt(CS, CS, 0.5, CS, ALU.is_gt, ALU.subtract, [L["B_CS"]], [L["B_CS"]])
        act(CS, CS, AF.Sin, [L["B_CS"]], [L["B_CS"]], scale=-TWO_PI)
        ST = L["ST"]
        for t4 in range(4):
            t = blk * 4 + t4
            bq, bk = banks[t4], banks[4 + t4]
            act(L["SQJ"], bq, AF.Square, [B_bank[t4]], [L["B_SQJ"], L["B_ST"]], scale=512.0 ** -0.5, accum=ST[:, 0:1])
            act(L["SQJ"][:, 0:256], bk[:, 0:256], AF.Square, [B_bank[4 + t4]], [L["B_SQJ"], L["B_ST"]], scale=256.0 ** -0.5,
                accum=ST[:, 1:2])
            ts("dve", ST[:, 2:4], ST[:, 0:2], EPS, None, ALU.add, None, [L["B_ST"]], [L["B_ST"]])
            act(ST[:, 4:6], ST[:, 2:4], AF.Sqrt, [L["B_ST"]], [L["B_ST"]])
            P.op("dve", lambda e: e.reciprocal(out=ST[:, 6:8], in_=ST[:, 4:6]), [L["B_ST"]], [L["B_ST"]])
            CQN = L["CQN"]
            stt(CQN[:, 0:512], bq, ST[:, 6:7], gq_b, ALU.mult, ALU.mult, [B_bank[t4], L["B_ST"], B_gqb], [L["B_CQN"]])
            stt(CQN[:, 512:768], bk[:, 0:256], ST[:, 7:8], gkv_b, ALU.mult, ALU.mult, [B_bank[4 + t4], L["B_ST"], B_gkvb], [L["B_CQN"]])
            XC, XS, KR = L["XC"], L["XS"], L["KR"]
            tt("dve", XC, bk[:, 256:320], CS[:, t4 * 128: t4 * 128 + 64], ALU.mult, [B_bank[4 + t4], L["B_CS"]], [L["B_XC"]])
            tt("dve", XS, bk[:, 256:320], CS[:, t4 * 128 + 64: t4 * 128 + 128], ALU.mult, [B_bank[4 + t4], L["B_CS"]], [L["B_XS"]])
            tt("dve", KR[:, 0:32], XC[:, 0:32], XS[:, 32:64], ALU.subtract, [L["B_XC"], L["B_XS"]], [L["B_KR"]])
            tt("dve", KR[:, 32:64], XS[:, 0:32], XC[:, 32:64], ALU.add, [L["B_XC"], L["B_XS"]], [L["B_KR"]])
            cp("dve", KR[:, 64:128], KR[:, 0:64], [L["B_KR"]], [L["B_KR"]])
            tb_ = banks_bf[t4]
            for kc in range(6):
                tr(tb_[:, kc * 128:(kc + 1) * 128], CQN[:, kc * 128:(kc + 1) * 128], ident_b, [L["B_CQN"], B_identb], [B_bank[t4]])
            tr(tb_[:, 768:896], KR, ident_b, [L["B_KR"], B_identb], [B_bank[t4]])
            act(v3(cqT, 4)[:, :, t * 128:(t + 1) * 128], v3(tb_[:, 0:512], 4), AF.Copy, [B_bank[t4]], [B_cq[blk]])
            act(v3(ckvT, 2)[:, :, t * 128:(t + 1) * 128], v3(tb_[:, 512:768], 2), AF.Copy, [B_bank[t4]], [B_ckv[blk]])
            act(kr2T[:, t * 128:(t + 1) * 128], tb_[:, 768:896], AF.Copy, [B_bank[t4]], [B_kr[blk]])

    def project_gla(L, j, blk):
        XT = L["XT"]
        RX = L["B_XTt"]
        win = gla_w_in_b[j]
        for kc in range(8):
            w, wb = wt(win[kc * 128:(kc + 1) * 128, 0:512])
            for h in range(4):
                mm(banks[h], w[:, h * 128:(h + 1) * 128], XT[:, kc * 512:(kc + 1) * 512], kc == 0, kc == 7, RX + [wb], [B_bank[h]])
        for h in range(4):
            act(L["QTS"], banks[h], AF.Copy, [B_bank[h]], [L["B_QTS"]], scale=GLA_QS)
            dma("pool", gq_d[h * 128:(h + 1) * 128, blk * 512:(blk + 1) * 512], L["QTS"], [L["B_QTS"]], [B_gq], B_gq)
        for kc in range(8):
            w, wb = wt(win[kc * 128:(kc + 1) * 128, 3072:3088], cols=16)
            mm(banks[4][0:16, :], w, XT[:, kc * 512:(kc + 1) * 512], kc == 0, kc == 7, RX + [wb], [B_bank[4]])
        act(AT[0:16, :], banks[4][0:16, :], AF.Copy, [B_bank[4]], [B_AT])
        for kc in range(8):
            w, wb = wt(win[kc * 128:(kc + 1) * 128, 512:1024])
            for t4 in range(4):
                mm(banks[t4], XT[:, kc * 512 + t4 * 128: kc * 512 + (t4 + 1) * 128], w, kc == 0, kc == 7, [RX[t4], wb], [B_bank[t4]])
        LA, AZ, DEC, KD = L["LA"], L["AZ"], L["DEC"], L["KD"]
        for t4 in range(4):
            t = blk * 4 + t4
            zb = 5
            mm(banks[zb], AT[0:17, t4 * 128:(t4 + 1) * 128], wa2x[0:17, :], True, True, [B_AT, B_wa2x], [B_bank[zb]])
            act(DEC, banks[zb], AF.Copy, [B_bank[zb]], [L["B_DEC"]])
            stt(AZ, DEC, -1.0, DEC, ALU.mult, ALU.max, [L["B_DEC"]], [L["B_AZ"]])
            act(AZ, AZ, AF.Exp, [L["B_AZ"]], [L["B_AZ"]], scale=-1.0)
            act(AZ, AZ, AF.Ln, [L["B_AZ"]], [L["B_AZ"]], bias=1.0)
            stt(LA, DEC, 0.0, AZ, ALU.min, ALU.subtract, [L["B_DEC"], L["B_AZ"]], [L["B_LA"]])
            mm(banks[6], tri_f, LA, True, True, [B_tri, L["B_LA"]], [B_bank[6]])
            for h in range(4):
                mm(banks[7][:, 2 * h:2 * h + 2], LA[:, h * 128:(h + 1) * 128], ind_f, True, True, [L["B_LA"], B_ind], [B_bank[7]])
            act(DEC, banks[6], AF.Exp, [B_bank[6]], [L["B_DEC"]])
            act(v3(DECAY, 4)[:, :, 2 * t:2 * t + 2], v3(banks[7][:, 0:8], 4), AF.Exp, [B_bank[7]], [B_decay])
            tt("dve", KD, banks[t4], DEC, ALU.mult, [B_bank[t4], L["B_DEC"]], [L["B_KD"]])
            dma("pool", gk_d[t * 128:(t + 1) * 128, :], KD, [L["B_KD"]], [B_gk], B_gk)
        for cg in range(4):
            c0 = 1024 + cg * 512
            bs = 4 * (cg % 2)
            for kc in range(8):
                w, wb = wt(win[kc * 128:(kc + 1) * 128, c0:c0 + 512])
                for t4 in range(4):
                    mm(banks[bs + t4], XT[:, kc * 512 + t4 * 128: kc * 512 + (t4 + 1) * 128], w, kc == 0, kc == 7,
                       [RX[t4], wb], [B_bank[bs + t4]])
            for t4 in range(4):
                t = blk * 4 + t4
                if cg < 2:
                    act(L["VB"][:, 0:512], banks[bs + t4], AF.Copy, [B_bank[bs + t4]], [L["B_VB"]])
                    dma("pool", gv_d[t * 128:(t + 1) * 128, cg * 512:(cg + 1) * 512], L["VB"][:, 0:512], [L["B_VB"]], [B_gv], B_gv)
                else:
                    act(L["SR"][:, 0:512], banks[bs + t4], AF.Silu, [B_bank[bs + t4]], [L["B_SR"]])
                    dma("pool", gsr_d[t * 128:(t + 1) * 128, (cg - 2) * 512:(cg - 1) * 512], L["SR"][:, 0:512], [L["B_SR"]], [B_gsr], B_gsr)

    def load_mla_vecs(j):
        dma("sp", gq_b, mla_qn_d[j:j + 1, :].partition_broadcast(128), (), [B_gqb], B_gqb)
        dma("sp", gkv_b, mla_kvn_d[j:j + 1, :].partition_broadcast(128), (), [B_gkvb], B_gkvb)

    def load_gla_vecs(j):
        dma("pool", wa2x[0:17, :], gla_wa2x_d[j], (), [B_wa2x], B_wa2x)

    def layer_norm(L, t4, gi, bi):
        XAt = L["XA"][:, t4 * 1024:(t4 + 1) * 1024]
        BX = L["B_XAt"][t4]
        ST = L["ST"]
        BS = L["B_ST"]
        P.op("dve", lambda e: e.bn_stats(out=ST[:, 8:14], in_=XAt[:, 0:512]), [BX], [BS])
        P.op("dve", lambda e: e.bn_stats(out=ST[:, 14:20], in_=XAt[:, 512:1024]), [BX], [BS])
        P.op("dve", lambda e: e.bn_aggr(out=ST[:, 20:22], in_=ST[:, 8:20]), [BS], [BS])
        ts("dve", ST[:, 22:23], ST[:, 21:22], EPS, None, ALU.add, None, [BS], [BS])
        act(ST[:, 23:24], ST[:, 22:23], AF.Sqrt, [BS], [BS])
        P.op("dve", lambda e: e.reciprocal(out=ST[:, 24:25], in_=ST[:, 23:24]), [BS], [BS])
        ts("dve", XAt, XAt, ST[:, 20:21], ST[:, 24:25], ALU.subtract, ALU.mult, [BX, BS], [BX])
        tt("pool", XAt, XAt, L["VEC"][gi], ALU.mult, [BX, L["B_VEC"][gi]], [BX])
        tt("pool", XAt, XAt, L["VEC"][bi], ALU.add, [BX, L["B_VEC"][bi]], [BX])

    def tail(i, nk, wo_d, nxt):
        precast_stage(i + 1)
        L = tail_layout(not (nxt is not None and nxt[0] == 'gla'))
        XA, XT, OTB, ACTT, PTB, TMP, FU, FG = (L[k] for k in ("XA", "XT", "OTB", "ACTT", "PTB", "TMP", "FU", "FG"))
        vrot = [0]

        def load_vec(src):
            slot = vrot[0] % 3
            vrot[0] += 1
            dma("sp", L["VEC"][slot], src[i:i + 1, :].partition_broadcast(128), (), [L["B_VEC"][slot]], L["B_VEC"][slot])
            return slot
        dma("sp", convp, convp_d[i], (), [B_convp], B_convp)
        memset("dve", HALO[0], 0.0, [B_halo[0]])
        if nxt is not None and nxt[0] == 'mla':
            load_mla_vecs(nxt[1])
        if nxt is not None and nxt[0] == 'gla':
            load_gla_vecs(nxt[1])
        xsrc = x_d if i == 0 else xres_d
        BG_ = [Buf(), Buf()]
        cw = v3(convp, 44)
        for blk in range(NBLK):
            hp, hn = HALO[blk % 2], HALO[(blk + 1) % 2]
            Bhp, Bhn = B_halo[blk % 2], B_halo[(blk + 1) % 2]
            t0 = blk * 4
            dma("sp", v3(OTB[:, 0:nk * 512], nk), oT_d[0:nk * 128, blk * 512:(blk + 1) * 512].rearrange("(k p) s -> p k s", p=128),
                [B_oT], [L["B_OTB"]], L["B_OTB"])
            for t4 in range(4):
                dma("pool", XA[:, t4 * 1024:(t4 + 1) * 1024], xsrc[(t0 + t4) * 128:(t0 + t4 + 1) * 128, :],
                    [B_xres] if i > 0 else [], [L["B_XAt"][t4]], L["B_XAt"][t4])
            dma("sp", v3(PTB, 2), pT_b[i][:, blk * 512:(blk + 1) * 512].rearrange("(k p) s -> p k s", p=128), [mirror_buf[pT_b[i].name]], [L["B_PTB"]], L["B_PTB"])
            for half in range(2):
                for kc in range(nk):
                    w, wb = wt(wo_d[kc * 128:(kc + 1) * 128, half * 512:(half + 1) * 512])
                    for t4 in range(4):
                        mm(banks[4 * half + t4], OTB[:, kc * 512 + t4 * 128: kc * 512 + (t4 + 1) * 128], w, kc == 0, kc == nk - 1,
                           [L["B_OTB"], wb], [B_bank[4 * half + t4]])
            vg = load_vec(ln1g_d)
            vb = load_vec(ln1b_d)
            for t4 in range(4):
                for half in range(2):
                    xs = XA[:, t4 * 1024 + half * 512: t4 * 1024 + (half + 1) * 512]
                    stt(xs, xs, ALPHA, banks[4 * half + t4], ALU.mult, ALU.add, [L["B_XAt"][t4], B_bank[4 * half + t4]], [L["B_XAt"][t4]])
                layer_norm(L, t4, vg, vb)
                to_featmajor(L, t4, t4)
            hv = v3(hp, 44)
            cr = v3(CORR, 44)
            tmpa = L["ST"][:, 0:44]
            ta = FU[:, 0:44]
            tb2 = FU[:, 64:108]
            tt("dve", ta, hv[:, :, 1], cw[:, :, 1], ALU.mult, [Bhp, B_convp], [L["B_FU"]])
            tt("dve", tb2, hv[:, :, 0], cw[:, :, 0], ALU.mult, [Bhp, B_convp], [L["B_FU"]])
            tt("dve", cr[:, :, 0], ta, tb2, ALU.add, [L["B_FU"]], [B_corr])
            tt("dve", cr[:, :, 1], hv[:, :, 1], cw[:, :, 0], ALU.mult, [Bhp, B_convp], [B_corr])
            for pr in range(11):
                bs_ = 0 if pr % 2 == 0 else 4
                c0 = 2 * pr
                for kc in range(8):
                    wu_, wub = wt(wup_b[i][kc * 128:(kc + 1) * 128, c0 * 128: c0 * 128 + 256], cols=256)
                    wg_, wgb = wt(wup_b[i][kc * 128:(kc + 1) * 128, DFF + c0 * 128: DFF + c0 * 128 + 256], cols=256)
                    for cc in range(2):
                        mm(banks[bs_ + cc], wu_[:, cc * 128:(cc + 1) * 128], XT[:, kc * 512:(kc + 1) * 512], kc == 0, kc == 7,
                           L["B_XTt"] + [wub], [B_bank[bs_ + cc]])
                        mm(banks[bs_ + 2 + cc], wg_[:, cc * 128:(cc + 1) * 128], XT[:, kc * 512:(kc + 1) * 512], kc == 0, kc == 7,
                           L["B_XTt"] + [wgb], [B_bank[bs_ + 2 + cc]])
                for cc in range(2):
                    c = c0 + cc
                    bu, bg = bs_ + cc, bs_ + 2 + cc
                    FUc, BFUc = (FU, L["B_FU"]) if cc == 0 else (L["FU2"], L["B_FU2"])
                    FGc, BFGc = (FG, L["B_FG"]) if cc == 0 else (L["FG2"], L["B_FG2"])
                    for (bk_, ci, Fo, BFo) in ((bu, c, FUc, BFUc), (bg, 22 + c, FGc, BFGc)):
                        pb = banks[bk_]
                        act(Fo, pb, AF.Identity, [B_bank[bk_], B_convp], [BFo], scale=cw[:, ci, 2:3], bias=cw[:, ci, 3:4])
                        stt(Fo[:, 1:512], pb[:, 0:511], cw[:, ci, 1:2], Fo[:, 1:512], ALU.mult, ALU.add, [B_bank[bk_], B_convp, BFo], [BFo])
                        stt(Fo[:, 2:512], pb[:, 0:510], cw[:, ci, 0:1], Fo[:, 2:512], ALU.mult, ALU.add, [B_bank[bk_], B_convp, BFo], [BFo])
                        tt("dve", Fo[:, 0:2], Fo[:, 0:2], cr[:, ci, :], ALU.add, [BFo, B_corr], [BFo])
                        act(v3(hn, 44)[:, ci, :], pb[:, 510:512], AF.Copy, [B_bank[bk_]], [Bhn])
                    act(FGc, FGc, AF.Gelu, [BFGc], [BFGc])
                    tt("pool", ACTT[:, c * 512:(c + 1) * 512], FUc, FGc, ALU.mult, [BFUc, BFGc], [L["B_ACTT"]])
            for half in range(2):
                bs = 4 if half == 0 else 0
                for c in range(NFC):
                    w, wb = wt(wdn_b[i][c * 128:(c + 1) * 128, half * 512:(half + 1) * 512])
                    for t4 in range(4):
                        mm(banks[bs + t4], ACTT[:, c * 512 + t4 * 128: c * 512 + (t4 + 1) * 128], w, c == 0, c == NFC - 1,
                           [L["B_ACTT"], wb], [B_bank[bs + t4]])
            vg = load_vec(ln2g_d)
            vb = load_vec(ln2b_d)
            for t4 in range(4):
                for half in range(2):
                    bs = 4 if half == 0 else 0
                    xs = XA[:, t4 * 1024 + half * 512: t4 * 1024 + (half + 1) * 512]
                    stt(xs, xs, ALPHA, banks[bs + t4], ALU.mult, ALU.add, [L["B_XAt"][t4], B_bank[bs + t4]], [L["B_XAt"][t4]])
                layer_norm(L, t4, vg, vb)
                to_featmajor(L, t4, t4)
            vbg = load_vec(bg_d)
            for half in range(2):
                for kc in range(8):
                    w, wb = wt(wpg_b[i][kc * 128:(kc + 1) * 128, half * 512:(half + 1) * 512])
                    for t4 in range(4):
                        mm(banks[t4], XT[:, kc * 512 + t4 * 128: kc * 512 + (t4 + 1) * 128], w, kc == 0, kc == 7,
                           [L["B_XTt"][t4], wb], [B_bank[t4]])
                for kc in range(2):
                    w, wb = wt(wpp_b[i][kc * 128:(kc + 1) * 128, half * 512:(half + 1) * 512])
                    for t4 in range(4):
                        mm(banks[4 + t4], PTB[:, kc * 512 + t4 * 128: kc * 512 + (t4 + 1) * 128], w, kc == 0, kc == 1,
                           [L["B_PTB"], wb], [B_bank[4 + t4]])
                for t4 in range(4):
                    G = TMP[:, (t4 % 2) * 512:(t4 % 2 + 1) * 512]
                    tt("dve", G, banks[t4], L["VEC"][vbg][:, half * 512:(half + 1) * 512], ALU.add, [B_bank[t4], L["B_VEC"][vbg]], [BG_[t4 % 2]])
                    act(G, G, AF.Sigmoid, [BG_[t4 % 2]], [BG_[t4 % 2]])
                    tt("dve", G, G, banks[4 + t4], ALU.mult, [BG_[t4 % 2], B_bank[4 + t4]], [BG_[t4 % 2]])
                    xs = XA[:, t4 * 1024 + half * 512: t4 * 1024 + (half + 1) * 512]
                    tt("dve", xs, xs, G, ALU.add, [L["B_XAt"][t4], BG_[t4 % 2]], [L["B_XAt"][t4]])
            dst = out_d if nxt is None else xres_d
            Bd = B_out if nxt is None else B_xres
            for t4 in range(4):
                dma("pool", dst[(t0 + t4) * 128:(t0 + t4 + 1) * 128, :], XA[:, t4 * 1024:(t4 + 1) * 1024], [L["B_XAt"][t4]], [Bd], Bd)
            if nxt is not None:
                for t4 in range(4):
                    to_featmajor(L, t4, t4)
                if nxt[0] == 'mla':
                    project_mla(L, nxt[1], blk)
                else:
                    project_gla(L, nxt[1], blk)
        P.barrier()

    def prologue():
        L = tail_layout(True)
        load_mla_vecs(0)
        for blk in range(NBLK):
            for t4 in range(4):
                t = blk * 4 + t4
                dma("sp", L["XA"][:, t4 * 1024:(t4 + 1) * 1024], x_d[t * 128:(t + 1) * 128, :], (), [L["B_XAt"][t4]], L["B_XAt"][t4])
            for t4 in range(4):
                to_featmajor(L, t4, t4)
            project_mla(L, 0, blk)
        P.barrier()

    def mla_attention(j):
        cv = Carver(arena)
        QN = [cv.bf(S) for _ in range(2)]; QR = [cv.bf(S) for _ in range(2)]; KN = [cv.bf(S) for _ in range(2)]
        B_QN = [[Buf() for _ in range(NBLK)] for _ in range(2)]
        B_QR = [[Buf() for _ in range(NBLK)] for _ in range(2)]
        B_KN = [[Buf() for _ in range(NBLK)] for _ in range(2)]
        V4 = cv.bf(NT * 512); B_V4 = [Buf() for _ in range(NT)]
        PT = [cv.bf(512) for _ in range(4)]; B_PT = [Buf() for _ in range(4)]
        ACCp = [cv.f32(512) for _ in range(2)]; B_ACCp = [Buf(), Buf()]
        ACCd = [cv.f32(512) for _ in range(2)]; B_ACCd = [Buf(), Buf()]
        RS = cv.f32(512); B_RS = Buf()
        OTs = [cv.bf(512) for _ in range(2)]; B_OTs = [Buf(), Buf()]
        ptc = [0]
        sbc = [0]
        for g in range(4):
            cs = slice(g * 512, (g + 1) * 512)
            WQN = [wt(wqn_b[j][kc * 128:(kc + 1) * 128, cs]) for kc in range(4)]
            WQR = [wt(wqr_b[j][kc * 128:(kc + 1) * 128, cs]) for kc in range(4)]
            WUK = [wt(wuk_b[j][kc * 128:(kc + 1) * 128, cs]) for kc in range(2)]
            WUV = [wt(wuv_b[j][kc * 128:(kc + 1) * 128, cs]) for kc in range(2)]
            for t in range(NT):
                pb = 6 + (t % 2)
                for kc in range(2):
                    mm(banks[pb], ckvT[:, kc * S + t * 128: kc * S + (t + 1) * 128], WUV[kc][0], kc == 0, kc == 1,
                       [B_ckv[t // 4], WUV[kc][1]], [B_bank[pb]])
                act(V4[:, t * 512:(t + 1) * 512], banks[pb], AF.Copy, [B_bank[pb]], [B_V4[t]])
            for hh in range(4):
                h = g * 4 + hh
                par = h % 2
                hs = slice(hh * 128, (hh + 1) * 128)
                for tb in range(NBLK):
                    tsl = slice(tb * 512, (tb + 1) * 512)
                    for kc in range(4):
                        mm(banks[6], WQN[kc][0][:, hs], cqT[:, kc * S + tb * 512: kc * S + (tb + 1) * 512], kc == 0, kc == 3,
                           [B_cq[tb], WQN[kc][1]], [B_bank[6]])
                    act(QN[par][:, tsl], banks[6], AF.Copy, [B_bank[6]], [B_QN[par][tb]])
                    for kc in range(4):
                        mm(banks[7], WQR[kc][0][:, hs], cqT[:, kc * S + tb * 512: kc * S + (tb + 1) * 512], kc == 0, kc == 3,
                           [B_cq[tb], WQR[kc][1]], [B_bank[7]])
                    tt("dve", QR[par][:, tsl], banks[7], T2[:, tsl], ALU.mult, [B_bank[7], B_T2], [B_QR[par][tb]])
                    for kc in range(2):
                        mm(banks[6], WUK[kc][0][:, hs], ckvT[:, kc * S + tb * 512: kc * S + (tb + 1) * 512], kc == 0, kc == 1,
                           [B_ckv[tb], WUK[kc][1]], [B_bank[6]])
                    act(KN[par][:, tsl], banks[6], AF.Copy, [B_bank[6]], [B_KN[par][tb]])
                items = []
                for qt in range(NBLK):
                    for jk in range(4 * qt + 4):
                        items.append((qt, jk, 4 * qt + 4))

                def emit_scores(it):
                    qt, jk, nj = it
                    r = jk - 4 * qt
                    q0 = max(r, 0) * 128
                    n = 512 - q0
                    sb = sbc[0] % 4
                    sbc[0] += 1
                    qsl = slice(qt * 512 + q0, qt * 512 + 512)
                    kb = jk // 4
                    mm(banks[sb][:, 0:n], KN[par][:, jk * 128:(jk + 1) * 128], QN[par][:, qsl], True, False,
                       [B_KN[par][kb], B_QN[par][qt]], [B_bank[sb]])
                    mm(banks[sb][:, 0:n], kr2T[:, jk * 128:(jk + 1) * 128], QR[par][:, qsl], False, True,
                       [B_kr[kb], B_QR[par][qt]], [B_bank[sb]])
                    return sb

                def epilogue(qt):
                    ob = 4 + (qt % 2)
                    ap_ = qt % 2
                    mm(banks[6], ones_f, ACCp[ap_], True, False, [B_ones, B_ACCp[ap_]], [B_bank[6]])
                    mm(banks[6], ones_f, ACCd[ap_], False, True, [B_ones, B_ACCd[ap_]], [B_bank[6]])
                    act(RS, banks[6], AF.Ln, [B_bank[6]], [B_RS])
                    act(RS, RS, AF.Exp, [B_RS], [B_RS], scale=-1.0)
                    tt("dve", OTs[ap_], banks[ob], RS, ALU.mult, [B_bank[ob], B_RS], [B_OTs[ap_]])
                    dma("sp", oT_d[h * 128:(h + 1) * 128, qt * 512:(qt + 1) * 512], OTs[ap_], [B_OTs[ap_]], [B_oT], B_oT)

                pending = None
                sbq = [emit_scores(items[0]), emit_scores(items[1])]
                for idx, it in enumerate(items):
                    sb = sbq.pop(0)
                    if idx + 2 < len(items):
                        sbq.append(emit_scores(items[idx + 2]))
                    qt, jk, nj = it
                    ob = 4 + (qt % 2)
                    ap_ = qt % 2
                    if jk == 0:
                        memset("pool", ACCp[ap_], 0.0, [B_ACCp[ap_]])
                        memset("dve", ACCd[ap_], 0.0, [B_ACCd[ap_]])
                    r = jk - 4 * qt
                    q0 = max(r, 0) * 128
                    n = 512 - q0
                    pt = ptc[0] % 4
                    ptc[0] += 1
                    act(PT[pt][:, 0:n], banks[sb][:, 0:n], AF.Exp, [B_bank[sb]], [B_PT[pt]], scale=ATT_SCALE)
                    eng = "pool" if jk % 3 == 0 else "dve"
                    ACC, BACC = (ACCp[ap_], B_ACCp[ap_]) if jk % 3 == 0 else (ACCd[ap_], B_ACCd[ap_])
                    if r >= 0:
                        tt(eng, PT[pt][:, 0:128], PT[pt][:, 0:128], MASK[:, jk * 128:(jk + 1) * 128], ALU.mult,
                           [B_PT[pt], B_MASK], [B_PT[pt]])
                    mm(banks[ob][:, q0:512], V4[:, jk * 512 + hh * 128: jk * 512 + (hh + 1) * 128], PT[pt][:, 0:n],
                       jk == 0, jk == nj - 1, [B_V4[jk], B_PT[pt]], [B_bank[ob]])
                    tt(eng, ACC[:, q0:512], ACC[:, q0:512], PT[pt][:, 0:n], ALU.add, [BACC, B_PT[pt]], [BACC])
                    if pending is not None:
                        pending[1] -= 1
                        if pending[1] == 0:
                            epilogue(pending[0])
                            pending = None
                    if jk == nj - 1:
                        if pending is not None:
                            epilogue(pending[0])
                        pending = [qt, 2]
                if pending is not None:
                    epilogue(pending[0])
        P.barrier()

    def gla_mixer(j):
        cv = Carver(arena)
        St = cv.f32(1024); B_St = Buf()
        Sb = cv.bf(1024); B_Sb = Buf()
        QT = [cv.bf(4 * 512) for _ in range(2)]; B_QT = [Buf(), Buf()]
        KC = [cv.bf(8 * 512) for _ in range(2)]; B_KC = [Buf(), Buf()]
        VC = [cv.bf(8 * 1024) for _ in range(2)]; B_VC = [Buf(), Buf()]
        SRt = [cv.f32(1024) for _ in range(2)]; B_SRt = [Buf(), Buf()]
        ONs = [cv.f32(1024) for _ in range(2)]; B_ONs = [Buf(), Buf()]
        OG = cv.bf(1024); B_OG = Buf()
        OGT = [cv.bf(1024) for _ in range(2)]; B_OGT = [Buf(), Buf()]
        onb = cv.f32(1024); B_onb = Buf()
        STG = cv.f32(64); B_STG = Buf()
        dma("sp", onb, gla_on_d[j:j + 1, :].partition_broadcast(128), (), [B_onb], B_onb)
        memset("dve", St, 0.0, [B_St])
        dv = v3(DECAY, 4)
        for blk in range(NBLK):
            p2 = blk % 2
            dma("sp", v3(QT[p2], 4), gq_d[:, blk * 512:(blk + 1) * 512].rearrange("(h p) s -> p h s", p=128), [B_gq], [B_QT[p2]], B_QT[p2])
            dma("sp", v3(KC[p2][0:64, :], 8), gk_d[blk * 512:(blk + 1) * 512, :].rearrange("(c p) f -> p c f", p=64), [B_gk], [B_KC[p2]], B_KC[p2])
            dma("sp", v3(VC[p2][0:64, :], 8), gv_d[blk * 512:(blk + 1) * 512, :].rearrange("(c p) f -> p c f", p=64), [B_gv], [B_VC[p2]], B_VC[p2])
            for t4 in range(4):
                t = blk * 4 + t4
                s2 = t % 2
                dma("sp", SRt[s2], gsr_d[t * 128:(t + 1) * 128, :], [B_gsr], [B_SRt[s2]], B_SRt[s2])
                ON, B_ON = ONs[t % 2], B_ONs[t % 2]
                for ch in range(2):
                    c = t4 * 2 + ch
                    n = t * 2 + ch
                    for h in range(4):
                        bk_ = h // 2
                        mm(banks[bk_][:, (h % 2) * 256:(h % 2 + 1) * 256], KC[p2][0:64, c * 512 + h * 128: c * 512 + (h + 1) * 128],
                           VC[p2][0:64, c * 1024 + h * 256: c * 1024 + (h + 1) * 256], True, True, [B_KC[p2], B_VC[p2]], [B_bank[bk_]])
                    for h in range(4):
                        bk_ = h // 2
                        stt(St[:, h * 256:(h + 1) * 256], St[:, h * 256:(h + 1) * 256], dv[:, h, n:n + 1],
                            banks[bk_][:, (h % 2) * 256:(h % 2 + 1) * 256], ALU.mult, ALU.add, [B_St, B_decay, B_bank[bk_]], [B_St])
                    act(Sb, St, AF.Copy, [B_St], [B_Sb])
                    ob = 2 + 2 * ch
                    for h in range(4):
                        bk_ = ob + h // 2
                        mm(banks[bk_][:, (h % 2) * 256:(h % 2 + 1) * 256], QT[p2][:, h * 512 + t4 * 128: h * 512 + (t4 + 1) * 128],
                           Sb[:, h * 256:(h + 1) * 256], True, True, [B_QT[p2], B_Sb], [B_bank[bk_]])
                    rs_ = slice(ch * 64, (ch + 1) * 64)
                    for hb in range(2):
                        act(ON[rs_, hb * 512:(hb + 1) * 512], banks[ob + hb][rs_, :], AF.Copy, [B_bank[ob + hb]], [B_ON])
                for h in range(4):
                    P.op("dve", lambda e, h=h, ON=ON: e.bn_stats(out=STG[:, h * 6:(h + 1) * 6], in_=ON[:, h * 256:(h + 1) * 256]), [B_ON], [B_STG])
                for h in range(4):
                    P.op("dve", lambda e, h=h: e.bn_aggr(out=STG[:, 24 + 2 * h:26 + 2 * h], in_=STG[:, h * 6:(h + 1) * 6]), [B_STG], [B_STG])
                varv = STG[:, 24:32].rearrange("p (h two) -> p h two", two=2)[:, :, 1]
                ts("dve", STG[:, 32:36], varv, EPS, None, ALU.add, None, [B_STG], [B_STG])
                act(STG[:, 36:40], STG[:, 32:36], AF.Sqrt, [B_STG], [B_STG])
                P.op("dve", lambda e: e.reciprocal(out=STG[:, 40:44], in_=STG[:, 36:40]), [B_STG], [B_STG])
                for h in range(4):
                    ts("dve", ON[:, h * 256:(h + 1) * 256], ON[:, h * 256:(h + 1) * 256], STG[:, 24 + 2 * h:25 + 2 * h], STG[:, 40 + h:41 + h],
                       ALU.subtract, ALU.mult, [B_ON, B_STG], [B_ON])
                tt("pool", ON, ON, onb, ALU.mult, [B_ON, B_onb], [B_ON])
                tt("pool", OG, ON, SRt[s2], ALU.mult, [B_ON, B_SRt[s2]], [B_OG])
                tb_ = banks_bf[6 + (t % 2)]
                Btb = B_bank[6 + (t % 2)]
                for kc in range(8):
                    tr(tb_[:, kc * 128:(kc + 1) * 128], OG[:, kc * 128:(kc + 1) * 128], ident_b, [B_OG, B_identb], [Btb])
                act(OGT[s2], tb_, AF.Copy, [Btb], [B_OGT[s2]])
                dma("sp", oT_d[0:1024, t * 128:(t + 1) * 128].rearrange("(k p) s -> p k s", p=128), v3(OGT[s2], 8),
                    [B_OGT[s2]], [B_oT], B_oT)
        P.barrier()

    precast_stage(0)
    prologue()
    for i in range(nlayers):
        j = i // 2
        last = (i == nlayers - 1)
        if i % 2 == 0:
            mla_attention(j)
            nxt = None if last else ('gla', j)
            tail(i, 16, mla_wo_b[j], nxt)
        else:
            gla_mixer(j)
            nxt = None if last else ('mla', j + 1)
            tail(i, 8, gla_wo_b[j], nxt)
    P.barrier()
    P.emit()
    st.close()
    return nc, P


_CACHE = {}


def _host_layout(inputs, b):
    f32 = np.float32
    d = {}
    d["x"] = np.ascontiguousarray(inputs["x"][b], dtype=f32)
    d["pT"] = np.ascontiguousarray(np.transpose(inputs["p"][:, b], (0, 2, 1)), dtype=f32)
    pos = np.asarray(inputs["positions"][b], dtype=np.int32)
    d["pos_row"] = np.ascontiguousarray(pos.reshape(1, S))
    d["pos_col"] = np.ascontiguousarray(pos.reshape(NT, 128).T)
    return d


def _shared_layout(inputs):
    f32 = np.float32
    d = {}
    inv = (1.0 / (10000.0 ** (np.arange(0, 64, 2, dtype=f32) / f32(64)))).astype(f32)
    invt = (inv.astype(np.float64) / (2 * np.pi)).astype(f32)
    ropecol = np.zeros((128, 2), f32)
    for p in range(128):
        ropecol[p, 0] = invt[p % 32]
        ropecol[p, 1] = 0.25 if p < 64 else (0.5 if p < 96 else 0.0)
    d["ropecol"] = ropecol
    row = np.concatenate([invt, invt, invt, invt]).astype(f32)
    d["inv2"] = np.ascontiguousarray(np.broadcast_to(row, (128, 128)))
    ph = np.concatenate([np.full(64, 0.25, f32), np.zeros(64, f32)])
    d["ph2"] = np.ascontiguousarray(np.broadcast_to(ph, (128, 128)))
    for k in ("mla_w_in", "mla_q_norm", "mla_kv_norm", "mla_w_uk", "mla_w_uv", "mla_w_o", "gla_w_in", "gla_o_norm", "gla_w_o",
              "ln1_g", "ln1_b", "ln2_g", "ln2_b", "ffn_w_up", "ffn_w_down", "ple_w_proj", "ple_w_gate", "ple_b_gate"):
        d[k] = np.ascontiguousarray(inputs[k], dtype=f32)
    wuq = np.asarray(inputs["mla_w_uq"], dtype=f32).reshape(2, 512, 16, 192)
    d["wq_n"] = np.ascontiguousarray(wuq[..., 0:128].reshape(2, 512, 2048))
    wr = wuq[..., 128:192]
    wr2 = np.concatenate([wr, wr[..., 32:64], wr[..., 0:32]], axis=-1)
    d["wq_r"] = np.ascontiguousarray(wr2.reshape(2, 512, 2048))
    d["gla_wa2x"] = np.ascontiguousarray(np.concatenate([np.asarray(inputs["gla_w_a2"], f32),
                                                          np.asarray(inputs["gla_b_a"], f32)[:, None, :]], axis=1))
    cw = np.asarray(inputs["ffn_conv_w"], f32)
    cb = np.asarray(inputs["ffn_conv_b"], f32)
    cat = np.concatenate([cw, cb[:, None, :]], axis=1)
    cat = cat.reshape(4, 4, 44, 128)
    d["convp"] = np.ascontiguousarray(np.transpose(cat, (0, 3, 2, 1)).reshape(4, 128, 176))
    return d


def kernel(**inputs):
    if "nc" not in _CACHE:
        _CACHE["nc"] = build()[0]
    nc = _CACHE["nc"]
    shared = _shared_layout(inputs)
    in_maps = []
    for b in range(8):
        m = dict(shared)
        m.update(_host_layout(inputs, b))
        in_maps.append(m)
    res = run_bass_kernel_spmd(nc, in_maps, core_ids=list(range(8)))
    out = np.stack([np.asarray(r["out"], dtype=np.float32) for r in res.results], axis=0)
    return out
```

```python
import contextlib
import math
import numpy as np
import concourse.bass as bass
import concourse.mybir as mybir
from concourse.bass_utils import run_bass_kernel_spmd

F32 = mybir.dt.float32
BF16 = mybir.dt.bfloat16
I32 = mybir.dt.int32
AF = mybir.ActivationFunctionType
ALU = mybir.AluOpType

ENGS = ("pe", "act", "dve", "pool", "sp")

S = 4096
D = 1024
NT = 32
TB = 512
NBLK = 8
DFF = 2816
NFC = 22
ALPHA = 8.0 ** 0.25
EPS = 1e-5
ATT_SCALE = 192.0 ** -0.5
GLA_QS = 128.0 ** -0.5
NW = 20
TWO_PI = 2.0 * math.pi


class Buf:
    __slots__ = ("name", "lw", "rd", "sem", "cnt")

    def __init__(self, name=""):
        self.name = name
        self.lw = None
        self.rd = {}
        self.sem = None
        self.cnt = 0


class Op:
    __slots__ = ("fn", "waits", "signal", "dma", "clock")

    def __init__(self, fn):
        self.fn = fn
        self.waits = []
        self.signal = False
        self.dma = None
        self.clock = None


class Prog:
    def __init__(self, nc):
        self.nc = nc
        self.ops = {e: [] for e in ENGS}
        self.clock = {e: {} for e in ENGS}
        self.dma_bufs = []

    def _need(self, eng, tok, raw):
        ck = self.clock[eng]
        if tok[0] == 'e':
            _, E, i = tok
            if E == eng and not raw:
                return None
            if ck.get(E, -1) >= i:
                return None
            ck[E] = i
            src = self.ops[E][i]
            src.signal = True
            for k, v in src.clock.items():
                if ck.get(k, -1) < v:
                    ck[k] = v
            return tok
        _, b, c = tok
        if ck.get(b, -1) >= c:
            return None
        ck[b] = c
        return tok

    def op(self, eng, fn, reads=(), writes=(), dma=None):
        rec = Op(fn)
        lst = self.ops[eng]
        idx = len(lst)
        waits = rec.waits
        for b in reads:
            if b.lw is not None:
                w = self._need(eng, b.lw, True)
                if w:
                    waits.append(w)
        for b in writes:
            if b.lw is not None:
                w = self._need(eng, b.lw, False)
                if w:
                    waits.append(w)
            for t in b.rd.values():
                w = self._need(eng, t, False)
                if w:
                    waits.append(w)
        rec.clock = dict(self.clock[eng])
        if dma is not None:
            if dma.sem is None:
                dma.sem = True
                self.dma_bufs.append(dma)
            dma.cnt += 16
            tok = ('d', dma, dma.cnt)
            key = dma
            rec.dma = dma
        else:
            tok = ('e', eng, idx)
            key = eng
        for b in writes:
            b.lw = tok
            b.rd = {}
        for b in reads:
            b.rd[key] = tok
        lst.append(rec)
        return rec

    def barrier(self):
        toks = []
        for e in ENGS:
            lst = self.ops[e]
            for i in range(len(lst) - 1, -1, -1):
                if lst[i].dma is None and lst[i].fn is not None:
                    toks.append(('e', e, i))
                    break
        dtoks = [('d', b, b.cnt) for b in self.dma_bufs if b.cnt > 0]
        for e in ENGS:
            rec = Op(None)
            for t in toks + dtoks:
                w = self._need(e, t, True)
                if w:
                    rec.waits.append(w)
            rec.clock = dict(self.clock[e])
            self.ops[e].append(rec)

    def emit(self):
        nc = self.nc
        with contextlib.ExitStack() as st:
            esem = {e: st.enter_context(nc.semaphore("es_" + e)) for e in ENGS}
            for i, b in enumerate(self.dma_bufs):
                b.sem = st.enter_context(nc.semaphore("ds%d" % i))
            sigidx = {}
            for e in ENGS:
                c = 0
                arr = []
                for rec in self.ops[e]:
                    if rec.signal:
                        c += 1
                    arr.append(c)
                sigidx[e] = arr
            block = st.enter_context(nc.Block())

            def run(e, engobj):
                for rec in self.ops[e]:
                    for w in rec.waits:
                        if w[0] == 'e':
                            engobj.wait_ge(esem[w[1]], sigidx[w[1]][w[2]])
                        else:
                            engobj.wait_ge(w[1].sem, w[2])
                    if rec.fn is None:
                        continue
                    ins = rec.fn(engobj)
                    if rec.dma is not None:
                        ins.then_inc(rec.dma.sem, 16)
                    elif rec.signal:
                        ins.then_inc(esem[e], 1)

            @block.tensor
            def _(eng):
                run("pe", eng)

            @block.scalar
            def _(eng):
                run("act", eng)

            @block.vector
            def _(eng):
                run("dve", eng)

            @block.gpsimd
            def _(eng):
                run("pool", eng)

            @block.sync
            def _(eng):
                run("sp", eng)


class Carver:
    def __init__(self, ap):
        self.ap = ap
        self.off = 0
        self.hi = 0

    def bf(self, n):
        v = self.ap[:, self.off:self.off + n]
        self.off += n + (n & 1)
        self.hi = max(self.hi, self.off)
        assert self.off <= self.ap.shape[1], ("arena overflow", self.off, self.ap.shape)
        return v

    def f32(self, n):
        return self.bf(2 * n).bitcast(F32)

    def i32(self, n):
        return self.bf(2 * n).bitcast(I32)


def build(nlayers=4, dbg=False):
    nc = bass.Bass("TRN2", target_bir_lowering=False)
    P = Prog(nc)

    def din(name, shape, dt=F32):
        return nc.dram_tensor(name, shape, dt, kind="ExternalInput").ap()

    x_d = din("x", [S, D])
    pT_d = din("pT", [4, 256, S])
    posr_d = din("pos_row", [1, S], I32)
    posc_d = din("pos_col", [128, NT], I32)
    ropecol_d = din("ropecol", [128, 2])
    inv2_d = din("inv2", [128, 128])
    ph2_d = din("ph2", [128, 128])
    mla_w_in_d = din("mla_w_in", [2, 1024, 832])
    mla_qn_d = din("mla_q_norm", [2, 512])
    mla_kvn_d = din("mla_kv_norm", [2, 256])
    wqn_d = din("wq_n", [2, 512, 2048])
    wqr_d = din("wq_r", [2, 512, 2048])
    wuk_d = din("mla_w_uk", [2, 256, 2048])
    wuv_d = din("mla_w_uv", [2, 256, 2048])
    mla_wo_d = din("mla_w_o", [2, 2048, 1024])
    gla_w_in_d = din("gla_w_in", [2, 1024, 3088])
    gla_wa2x_d = din("gla_wa2x", [2, 17, 512])
    gla_on_d = din("gla_o_norm", [2, 1024])
    gla_wo_d = din("gla_w_o", [2, 1024, 1024])
    ln1g_d = din("ln1_g", [4, 1024])
    ln1b_d = din("ln1_b", [4, 1024])
    ln2g_d = din("ln2_g", [4, 1024])
    ln2b_d = din("ln2_b", [4, 1024])
    wup_d = din("ffn_w_up", [4, 1024, 5632])
    convp_d = din("convp", [4, 128, 176])
    wdn_d = din("ffn_w_down", [4, 2816, 1024])
    wpp_d = din("ple_w_proj", [4, 256, 1024])
    wpg_d = din("ple_w_gate", [4, 1024, 1024])
    bg_d = din("ple_b_gate", [4, 1024])
    out_d = nc.dram_tensor("out", [S, D], F32, kind="ExternalOutput").ap()
    okind = "ExternalOutput" if dbg else "Internal"
    xres_d = nc.dram_tensor("xres", [S, D], F32, kind=okind).ap()
    oT_d = nc.dram_tensor("oT", [2048, S], BF16, kind=okind).ap()
    gq_d = nc.dram_tensor("gq", [512, S], BF16, kind="Internal").ap()
    gk_d = nc.dram_tensor("gk", [S, 512], BF16, kind="Internal").ap()
    gv_d = nc.dram_tensor("gv", [S, 1024], BF16, kind="Internal").ap()
    gsr_d = nc.dram_tensor("gsr", [S, 1024], F32, kind="Internal").ap()
    B_xres, B_oT, B_gq, B_gk, B_gv, B_gsr, B_out = (Buf(n) for n in ("xres", "oT", "gq", "gk", "gv", "gsr", "out"))

    st = contextlib.ExitStack()

    def sbt(name, cols, dt):
        return st.enter_context(nc.sbuf_tensor(name, [128, cols], dt))[:]

    ident_f = sbt("ident_f", 128, F32); B_identf = Buf()
    ident_b = sbt("ident_b", 128, BF16); B_identb = Buf()
    ones_f = sbt("ones_f", 128, F32); B_ones = Buf()
    tri_f = sbt("tri_f", 128, F32); B_tri = Buf()
    ind_f = sbt("ind_f", 2, F32); B_ind = Buf()
    T2 = sbt("T2", S, BF16); B_T2 = Buf()
    MASK = sbt("MASK", S, BF16); B_MASK = Buf()
    cqT = sbt("cqT", 4 * S, BF16)
    ckvT = sbt("ckvT", 2 * S, BF16)
    kr2T = sbt("kr2T", S, BF16)
    B_cq = [Buf() for _ in range(NBLK)]
    B_ckv = [Buf() for _ in range(NBLK)]
    B_kr = [Buf() for _ in range(NBLK)]
    posc_i = sbt("posc_i", NT, I32); B_posci = Buf()
    posc_f = sbt("posc_f", NT, F32); B_poscf = Buf()
    cidc_f = sbt("cidc_f", NT, F32); B_cidc = Buf()
    ropecol = sbt("ropecol_s", 2, F32); B_ropecol = Buf()
    inv2 = sbt("inv2_s", 128, F32); B_inv2 = Buf()
    ph2 = sbt("ph2_s", 128, F32); B_ph2 = Buf()
    convp = sbt("convp_s", 176, F32); B_convp = Buf()
    gq_b = sbt("gq_b", 512, F32); B_gqb = Buf()
    gkv_b = sbt("gkv_b", 256, F32); B_gkvb = Buf()
    DECAY = sbt("DECAY", 256, F32); B_decay = Buf()
    wa2x = sbt("wa2x", 512, BF16); B_wa2x = Buf()
    AT = sbt("AT", 512, BF16); B_AT = Buf()
    HALO = [sbt("HALO%d" % i, 88, F32) for i in range(2)]; B_halo = [Buf(), Buf()]
    CORR = sbt("CORR", 88, F32); B_corr = Buf()
    smalls = sbt("smalls", 64, F32)
    wslots = [(sbt("w%d" % i, 512, BF16), Buf("w%d" % i)) for i in range(NW)]
    wctr = [0]
    arena_cols = (nc.sbuf_bytes_remaining - 1024) // 2
    arena_cols -= arena_cols % 64
    arena = sbt("arena", arena_cols, BF16)
    banks = [st.enter_context(nc.psum_tensor("bank%d" % i, [128, 512], F32))[:] for i in range(8)]
    banks_bf = [b.bitcast(BF16) for b in banks]
    B_bank = [Buf("bank%d" % i) for i in range(8)]

    def mm(out, lhsT, rhs, start, stop, R, W):
        P.op("pe", lambda e: e.matmul(out, lhsT=lhsT, rhs=rhs, start=start, stop=stop), R, W)

    def tr(out, in_, ident, R, W):
        P.op("pe", lambda e: e.transpose(out, in_, ident), R, W)

    def act(out, in_, func, R, W, scale=1.0, bias=0.0, accum=None):
        if accum is None:
            P.op("act", lambda e: e.activation(out=out, in_=in_, func=func, bias=bias, scale=scale), R, W)
        else:
            P.op("act", lambda e: e.activation(out=out, in_=in_, func=func, bias=bias, scale=scale, accum_out=accum), R, W)

    def ts(eng, out, in0, s1, s2, op0, op1, R, W):
        if s2 is None:
            P.op(eng, lambda e: e.tensor_scalar(out=out, in0=in0, scalar1=s1, scalar2=None, op0=op0), R, W)
        else:
            P.op(eng, lambda e: e.tensor_scalar(out=out, in0=in0, scalar1=s1, scalar2=s2, op0=op0, op1=op1), R, W)

    def tt(eng, out, in0, in1, op, R, W):
        P.op(eng, lambda e: e.tensor_tensor(out=out, in0=in0, in1=in1, op=op), R, W)

    def stt(out, in0, scalar, in1, op0, op1, R, W):
        P.op("dve", lambda e: e.scalar_tensor_tensor(out=out, in0=in0, scalar=scalar, in1=in1, op0=op0, op1=op1), R, W)

    def cp(eng, out, in_, R, W):
        P.op(eng, lambda e: e.tensor_copy(out=out, in_=in_), R, W)

    def memset(eng, ap, val, W):
        P.op(eng, lambda e: e.memset(ap, val), (), W)

    def dma(eng, out, in_, R, W, dbuf):
        P.op(eng, lambda e: e.dma_start(out=out, in_=in_), R, W, dma=dbuf)

    mirror_buf = {}

    def mirror2d(src2d, name):
        return nc.dram_tensor("bf_" + name, list(src2d.shape), BF16, kind="Internal").ap()

    def precast(gbuf, pairs):
        for src, dst in pairs:
            R_, C_ = src.shape
            rows = R_ if R_ <= 128 else max(128, ((1 << 19) // C_) // 128 * 128)
            for r0 in range(0, R_, rows):
                r1 = min(R_, r0 + rows)
                dma("pool", dst[r0:r1, :], src[r0:r1, :], (), (), gbuf)
            mirror_buf[dst.name] = gbuf
        gbuf.lw = ('d', gbuf, gbuf.cnt)

    def wt(src, rows=128, cols=512):
        ap, b = wslots[wctr[0] % NW]
        wctr[0] += 1
        v = ap[0:rows, 0:cols]
        dma("sp", v, src, [mirror_buf[src.name]], [b], b)
        return v, b

    mla_w_in_f, wqn_f, wqr_f, wuk_f, wuv_f, mla_wo_f = mla_w_in_d, wqn_d, wqr_d, wuk_d, wuv_d, mla_wo_d
    gla_w_in_f, gla_wo_f, wup_f, wdn_f, wpp_f, wpg_f, pT_f = gla_w_in_d, gla_wo_d, wup_d, wdn_d, wpp_d, wpg_d, pT_d
    mla_w_in_b = [mirror2d(mla_w_in_f[j], "mla_w_in%d" % j) for j in range(2)]
    wqn_b = [mirror2d(wqn_f[j], "wqn%d" % j) for j in range(2)]
    wqr_b = [mirror2d(wqr_f[j], "wqr%d" % j) for j in range(2)]
    wuk_b = [mirror2d(wuk_f[j], "wuk%d" % j) for j in range(2)]
    wuv_b = [mirror2d(wuv_f[j], "wuv%d" % j) for j in range(2)]
    mla_wo_b = [mirror2d(mla_wo_f[j], "mla_wo%d" % j) for j in range(2)]
    gla_w_in_b = [mirror2d(gla_w_in_f[j], "gla_w_in%d" % j) for j in range(2)]
    gla_wo_b = [mirror2d(gla_wo_f[j], "gla_wo%d" % j) for j in range(2)]
    wup_b = [mirror2d(wup_f[i], "wup%d" % i) for i in range(4)]
    wdn_b = [mirror2d(wdn_f[i], "wdn%d" % i) for i in range(4)]
    wpp_b = [mirror2d(wpp_f[i], "wpp%d" % i) for i in range(4)]
    wpg_b = [mirror2d(wpg_f[i], "wpg%d" % i) for i in range(4)]
    pT_b = [mirror2d(pT_f[i], "pTb%d" % i) for i in range(4)]

    def tail_pairs(i):
        return [(wup_f[i], wup_b[i]), (wdn_f[i], wdn_b[i]), (wpg_f[i], wpg_b[i]), (wpp_f[i], wpp_b[i]), (pT_f[i], pT_b[i])]

    def precast_stage(k):
        if k == 0:
            precast(Buf("g_in0"), [(mla_w_in_f[0], mla_w_in_b[0])])
            precast(Buf("g_mix0"), [(wqn_f[0], wqn_b[0]), (wqr_f[0], wqr_b[0]), (wuk_f[0], wuk_b[0]), (wuv_f[0], wuv_b[0])])
            precast(Buf("g_tail0"), [(mla_wo_f[0], mla_wo_b[0])] + tail_pairs(0) + [(gla_w_in_f[0], gla_w_in_b[0])])
        elif k == 1:
            precast(Buf("g1"), [(gla_wo_f[0], gla_wo_b[0])] + tail_pairs(1) + [(mla_w_in_f[1], mla_w_in_b[1])])
        elif k == 2:
            precast(Buf("g2"), [(wqn_f[1], wqn_b[1]), (wqr_f[1], wqr_b[1]), (wuk_f[1], wuk_b[1]), (wuv_f[1], wuv_b[1]),
                                (mla_wo_f[1], mla_wo_b[1])] + tail_pairs(2) + [(gla_w_in_f[1], gla_w_in_b[1])])
        elif k == 3:
            precast(Buf("g3"), [(gla_wo_f[1], gla_wo_b[1])] + tail_pairs(3))

    def v3(ap, a):
        return ap.rearrange("p (a b) -> p a b", a=a)

    memset("pool", ident_f, 1.0, [B_identf])
    P.op("pool", lambda e: e.affine_select(out=ident_f, in_=ident_f, pattern=[[-1, 128]], compare_op=ALU.is_equal,
                                           fill=0.0, base=0, channel_multiplier=1), [B_identf], [B_identf])
    cp("dve", ident_b, ident_f, [B_identf], [B_identb])
    memset("pool", ones_f, 1.0, [B_ones])
    memset("pool", tri_f, 1.0 / 16.0, [B_tri])
    P.op("pool", lambda e: e.affine_select(out=tri_f, in_=tri_f, pattern=[[-1, 128]], compare_op=ALU.is_gt,
                                           fill=0.0, base=0, channel_multiplier=1), [B_tri], [B_tri])
    memset("pool", tri_f[64:128, 0:64], 0.0, [B_tri])
    memset("pool", ind_f, 0.0, [B_ind])
    memset("pool", ind_f[0:64, 0:1], 1.0 / 16.0, [B_ind])
    memset("pool", ind_f[64:128, 1:2], 1.0 / 16.0, [B_ind])
    memset("pool", AT[0:32, :], 1.0, [B_AT])
    dma("sp", posc_i, posc_d, (), [B_posci], B_posci)
    dma("sp", ropecol, ropecol_d, (), [B_ropecol], B_ropecol)
    dma("sp", inv2, inv2_d, (), [B_inv2], B_inv2)
    dma("sp", ph2, ph2_d, (), [B_ph2], B_ph2)
    cp("dve", posc_f, posc_i, [B_posci], [B_poscf])
    cv = Carver(arena)
    R0 = cv.bf(2 * S); R1 = cv.bf(2 * S); R2 = cv.bf(2 * S); R3 = cv.bf(2 * S)
    B_R = [Buf() for _ in range(4)]
    tmpi = smalls[:, 0:32].bitcast(I32); B_tmpi = Buf()
    ts("dve", tmpi, posc_i, 6, None, ALU.arith_shift_right, None, [B_posci], [B_tmpi])
    cp("dve", cidc_f, tmpi, [B_tmpi], [B_cidc])
    dma("sp", R0.bitcast(I32), posr_d.partition_broadcast(128), (), [B_R[0]], B_R[0])
    cp("dve", R1.bitcast(F32), R0.bitcast(I32), [B_R[0]], [B_R[1]])
    ts("dve", R2.bitcast(I32), R0.bitcast(I32), 6, None, ALU.arith_shift_right, None, [B_R[0]], [B_R[2]])
    cp("dve", R0.bitcast(F32), R2.bitcast(I32), [B_R[2]], [B_R[0]])
    for b in range(NT):
        ts("dve", MASK[:, b * 128:(b + 1) * 128], R0.bitcast(F32)[:, b * 128:(b + 1) * 128], cidc_f[:, b:b + 1], None,
           ALU.is_ge, None, [B_R[0], B_cidc], [B_MASK])
    ts("dve", R2.bitcast(F32), R1.bitcast(F32), ropecol[:, 0:1], ropecol[:, 1:2], ALU.mult, ALU.add, [B_R[1], B_ropecol], [B_R[2]])
    cp("dve", R1.bitcast(I32), R2.bitcast(F32), [B_R[2]], [B_R[1]])
    cp("dve", R3.bitcast(F32), R1.bitcast(I32), [B_R[1]], [B_R[3]])
    tt("dve", R2.bitcast(F32), R2.bitcast(F32), R3.bitcast(F32), ALU.subtract, [B_R[2], B_R[3]], [B_R[2]])
    stt(R2.bitcast(F32), R2.bitcast(F32), 0.5, R2.bitcast(F32), ALU.is_gt, ALU.subtract, [B_R[2]], [B_R[2]])
    act(T2, R2.bitcast(F32), AF.Sin, [B_R[2]], [B_T2], scale=-TWO_PI)
    P.barrier()

    TBUFS = {}

    def tail_layout(with_cq_live):
        cv = Carver(arena)
        L = {}
        L["VEC"] = [cv.f32(1024) for _ in range(3)]
        L["XA"] = cv.f32(4 * 1024)
        L["XBF"] = cv.bf(1024)
        L["XBF2"] = cv.bf(1024)
        L["XT"] = cv.bf(8 * 512)
        L["OTB"] = cv.bf(16 * 512)
        L["ACTT"] = cv.bf(22 * 512)
        L["PTB"] = cv.bf(2 * 512)
        L["TMP"] = cv.f32(1024)
        L["FU"] = cv.f32(512)
        L["FG"] = cv.f32(512)
        L["FU2"] = cv.f32(512)
        L["FG2"] = cv.f32(512)
        L["SQJ"] = cv.bf(512)
        L["CQN"] = cv.bf(768)
        L["KR"] = cv.bf(128)
        L["CS"] = cv.f32(512)
        L["CSI"] = cv.i32(512)
        L["CSF"] = cv.f32(512)
        L["XC"] = cv.f32(64)
        L["XS"] = cv.f32(64)
        L["ST"] = cv.f32(64)
        if not with_cq_live:
            cv = Carver(cqT)
            L["LA"] = cv.f32(512)
            L["AZ"] = cv.f32(512)
            L["DEC"] = cv.f32(512)
            L["KD"] = cv.bf(512)
            L["VB"] = cv.bf(1024)
            L["SR"] = cv.f32(1024)
            L["QTS"] = cv.bf(512)
        for k in list(L.keys()):
            if k == "VEC":
                L["B_VEC"] = TBUFS.setdefault("VEC", [Buf() for _ in range(3)])
            else:
                L["B_" + k] = TBUFS.setdefault(k, Buf(k))
        L["B_XAt"] = TBUFS.setdefault("XAt", [Buf() for _ in range(4)])
        L["B_XTt"] = TBUFS.setdefault("XTt", [Buf() for _ in range(4)])
        return L

    def to_featmajor(L, tt_i, tbank):
        XAt = L["XA"][:, tt_i * 1024:(tt_i + 1) * 1024]
        XBF, BXBF = (L["XBF"], L["B_XBF"]) if tt_i % 2 == 0 else (L["XBF2"], L["B_XBF2"])
        act(XBF, XAt, AF.Copy, [L["B_XAt"][tt_i]], [BXBF])
        for kc in range(8):
            tr(banks_bf[tbank][:, kc * 128:(kc + 1) * 128], XBF[:, kc * 128:(kc + 1) * 128], ident_b,
               [BXBF, B_identb], [B_bank[tbank]])
        cp("dve", v3(L["XT"], 8)[:, :, tt_i * 128:(tt_i + 1) * 128], v3(banks_bf[tbank], 8),
           [B_bank[tbank]], [L["B_XTt"][tt_i]])

    def project_mla(L, j, blk):
        XT = L["XT"]
        RX = L["B_XTt"]
        for kc in range(8):
            w, wb = wt(mla_w_in_b[j][kc * 128:(kc + 1) * 128, 0:512])
            for t4 in range(4):
                mm(banks[t4], XT[:, kc * 512 + t4 * 128: kc * 512 + (t4 + 1) * 128], w, kc == 0, kc == 7,
                   [RX[t4], wb], [B_bank[t4]])
        for kc in range(8):
            w, wb = wt(mla_w_in_b[j][kc * 128:(kc + 1) * 128, 512:832], cols=320)
            for t4 in range(4):
                mm(banks[4 + t4][:, 0:320], XT[:, kc * 512 + t4 * 128: kc * 512 + (t4 + 1) * 128], w, kc == 0, kc == 7,
                   [RX[t4], wb], [B_bank[4 + t4]])
        CS, CSI, CSF = L["CS"], L["CSI"], L["CSF"]
        for t4 in range(4):
            t = blk * 4 + t4
            stt(CS[:, t4 * 128:(t4 + 1) * 128], inv2, posc_f[:, t:t + 1], ph2, ALU.mult, ALU.add,
                [B_inv2, B_poscf, B_ph2], [L["B_CS"]])
        cp("dve", CSI, CS, [L["B_CS"]], [L["B_CSI"]])
        cp("dve", CSF, CSI, [L["B_CSI"]], [L["B_CSF"]])
        tt("dve", CS, CS, CSF, ALU.subtract, [L["B_CS"], L["B_CSF"]], [L["B_CS"]])
        stt(CS, CS, 0.5, CS, ALU.is_gt, ALU.subtract, [L["B_CS"]], [L["B_CS"]])
        act(CS, CS, AF.Sin, [L["B_CS"]], [L["B_CS"]], scale=-TWO_PI)
        ST = L["ST"]
        for t4 in range(4):
            t = blk * 4 + t4
            bq, bk = banks[t4], banks[4 + t4]
            act(L["SQJ"], bq, AF.Square, [B_bank[t4]], [L["B_SQJ"], L["B_ST"]], scale=512.0 ** -0.5, accum=ST[:, 0:1])
            act(L["SQJ"][:, 0:256], bk[:, 0:256], AF.Square, [B_bank[4 + t4]], [L["B_SQJ"], L["B_ST"]], scale=256.0 ** -0.5,
                accum=ST[:, 1:2])
            ts("dve", ST[:, 2:4], ST[:, 0:2], EPS, None, ALU.add, None, [L["B_ST"]], [L["B_ST"]])
            act(ST[:, 4:6], ST[:, 2:4], AF.Sqrt, [L["B_ST"]], [L["B_ST"]])
            P.op("dve", lambda e: e.reciprocal(out=ST[:, 6:8], in_=ST[:, 4:6]), [L["B_ST"]], [L["B_ST"]])
            CQN = L["CQN"]
            stt(CQN[:, 0:512], bq, ST[:, 6:7], gq_b, ALU.mult, ALU.mult, [B_bank[t4], L["B_ST"], B_gqb], [L["B_CQN"]])
            stt(CQN[:, 512:768], bk[:, 0:256], ST[:, 7:8], gkv_b, ALU.mult, ALU.mult, [B_bank[4 + t4], L["B_ST"], B_gkvb], [L["B_CQN"]])
            XC, XS, KR = L["XC"], L["XS"], L["KR"]
            tt("dve", XC, bk[:, 256:320], CS[:, t4 * 128: t4 * 128 + 64], ALU.mult, [B_bank[4 + t4], L["B_CS"]], [L["B_XC"]])
            tt("dve", XS, bk[:, 256:320], CS[:, t4 * 128 + 64: t4 * 128 + 128], ALU.mult, [B_bank[4 + t4], L["B_CS"]], [L["B_XS"]])
            tt("dve", KR[:, 0:32], XC[:, 0:32], XS[:, 32:64], ALU.subtract, [L["B_XC"], L["B_XS"]], [L["B_KR"]])
            tt("dve", KR[:, 32:64], XS[:, 0:32], XC[:, 32:64], ALU.add, [L["B_XC"], L["B_XS"]], [L["B_KR"]])
            cp("dve", KR[:, 64:128], KR[:, 0:64], [L["B_KR"]], [L["B_KR"]])
            tb_ = banks_bf[t4]
            for kc in range(6):
                tr(tb_[:, kc * 128:(kc + 1) * 128], CQN[:, kc * 128:(kc + 1) * 128], ident_b, [L["B_CQN"], B_identb], [B_bank[t4]])
            tr(tb_[:, 768:896], KR, ident_b, [L["B_KR"], B_identb], [B_bank[t4]])
            act(v3(cqT, 4)[:, :, t * 128:(t + 1) * 128], v3(tb_[:, 0:512], 4), AF.Copy, [B_bank[t4]], [B_cq[blk]])
            act(v3(ckvT, 2)[:, :, t * 128:(t + 1) * 128], v3(tb_[:, 512:768], 2), AF.Copy, [B_bank[t4]], [B_ckv[blk]])
            act(kr2T[:, t * 128:(t + 1) * 128], tb_[:, 768:896], AF.Copy, [B_bank[t4]], [B_kr[blk]])

    def project_gla(L, j, blk):
        XT = L["XT"]
        RX = L["B_XTt"]
        win = gla_w_in_b[j]
        for kc in range(8):
            w, wb = wt(win[kc * 128:(kc + 1) * 128, 0:512])
            for h in range(4):
                mm(banks[h], w[:, h * 128:(h + 1) * 128], XT[:, kc * 512:(kc + 1) * 512], kc == 0, kc == 7, RX + [wb], [B_bank[h]])
        for h in range(4):
            act(L["QTS"], banks[h], AF.Copy, [B_bank[h]], [L["B_QTS"]], scale=GLA_QS)
            dma("pool", gq_d[h * 128:(h + 1) * 128, blk * 512:(blk + 1) * 512], L["QTS"], [L["B_QTS"]], [B_gq], B_gq)
        for kc in range(8):
            w, wb = wt(win[kc * 128:(kc + 1) * 128, 3072:3088], cols=16)
            mm(banks[4][0:16, :], w, XT[:, kc * 512:(kc + 1) * 512], kc == 0, kc == 7, RX + [wb], [B_bank[4]])
        act(AT[0:16, :], banks[4][0:16, :], AF.Copy, [B_bank[4]], [B_AT])
        for kc in range(8):
            w, wb = wt(win[kc * 128:(kc + 1) * 128, 512:1024])
            for t4 in range(4):
                mm(banks[t4], XT[:, kc * 512 + t4 * 128: kc * 512 + (t4 + 1) * 128], w, kc == 0, kc == 7, [RX[t4], wb], [B_bank[t4]])
        LA, AZ, DEC, KD = L["LA"], L["AZ"], L["DEC"], L["KD"]
        for t4 in range(4):
            t = blk * 4 + t4
            zb = 5
            mm(banks[zb], AT[0:17, t4 * 128:(t4 + 1) * 128], wa2x[0:17, :], True, True, [B_AT, B_wa2x], [B_bank[zb]])
            act(DEC, banks[zb], AF.Copy, [B_bank[zb]], [L["B_DEC"]])
            stt(AZ, DEC, -1.0, DEC, ALU.mult, ALU.max, [L["B_DEC"]], [L["B_AZ"]])
            act(AZ, AZ, AF.Exp, [L["B_AZ"]], [L["B_AZ"]], scale=-1.0)
            act(AZ, AZ, AF.Ln, [L["B_AZ"]], [L["B_AZ"]], bias=1.0)
            stt(LA, DEC, 0.0, AZ, ALU.min, ALU.subtract, [L["B_DEC"], L["B_AZ"]], [L["B_LA"]])
            mm(banks[6], tri_f, LA, True, True, [B_tri, L["B_LA"]], [B_bank[6]])
            for h in range(4):
                mm(banks[7][:, 2 * h:2 * h + 2], LA[:, h * 128:(h + 1) * 128], ind_f, True, True, [L["B_LA"], B_ind], [B_bank[7]])
            act(DEC, banks[6], AF.Exp, [B_bank[6]], [L["B_DEC"]])
            act(v3(DECAY, 4)[:, :, 2 * t:2 * t + 2], v3(banks[7][:, 0:8], 4), AF.Exp, [B_bank[7]], [B_decay])
            tt("dve", KD, banks[t4], DEC, ALU.mult, [B_bank[t4], L["B_DEC"]], [L["B_KD"]])
            dma("pool", gk_d[t * 128:(t + 1) * 128, :], KD, [L["B_KD"]], [B_gk], B_gk)
        for cg in range(4):
            c0 = 1024 + cg * 512
            bs = 4 * (cg % 2)
            for kc in range(8):
                w, wb = wt(win[kc * 128:(kc + 1) * 128, c0:c0 + 512])
                for t4 in range(4):
                    mm(banks[bs + t4], XT[:, kc * 512 + t4 * 128: kc * 512 + (t4 + 1) * 128], w, kc == 0, kc == 7,
                       [RX[t4], wb], [B_bank[bs + t4]])
            for t4 in range(4):
                t = blk * 4 + t4
                if cg < 2:
                    act(L["VB"][:, 0:512], banks[bs + t4], AF.Copy, [B_bank[bs + t4]], [L["B_VB"]])
                    dma("pool", gv_d[t * 128:(t + 1) * 128, cg * 512:(cg + 1) * 512], L["VB"][:, 0:512], [L["B_VB"]], [B_gv], B_gv)
                else:
                    act(L["SR"][:, 0:512], banks[bs + t4], AF.Silu, [B_bank[bs + t4]], [L["B_SR"]])
                    dma("pool", gsr_d[t * 128:(t + 1) * 128, (cg - 2) * 512:(cg - 1) * 512], L["SR"][:, 0:512], [L["B_SR"]], [B_gsr], B_gsr)

    def load_mla_vecs(j):
        dma("sp", gq_b, mla_qn_d[j:j + 1, :].partition_broadcast(128), (), [B_gqb], B_gqb)
        dma("sp", gkv_b, mla_kvn_d[j:j + 1, :].partition_broadcast(128), (), [B_gkvb], B_gkvb)

    def load_gla_vecs(j):
        dma("pool", wa2x[0:17, :], gla_wa2x_d[j], (), [B_wa2x], B_wa2x)

    def layer_norm(L, t4, gi, bi):
        XAt = L["XA"][:, t4 * 1024:(t4 + 1) * 1024]
        BX = L["B_XAt"][t4]
        ST = L["ST"]
        BS = L["B_ST"]
        P.op("dve", lambda e: e.bn_stats(out=ST[:, 8:14], in_=XAt[:, 0:512]), [BX], [BS])
        P.op("dve", lambda e: e.bn_stats(out=ST[:, 14:20], in_=XAt[:, 512:1024]), [BX], [BS])
        P.op("dve", lambda e: e.bn_aggr(out=ST[:, 20:22], in_=ST[:, 8:20]), [BS], [BS])
        ts("dve", ST[:, 22:23], ST[:, 21:22], EPS, None, ALU.add, None, [BS], [BS])
        act(ST[:, 23:24], ST[:, 22:23], AF.Sqrt, [BS], [BS])
        P.op("dve", lambda e: e.reciprocal(out=ST[:, 24:25], in_=ST[:, 23:24]), [BS], [BS])
        ts("dve", XAt, XAt, ST[:, 20:21], ST[:, 24:25], ALU.subtract, ALU.mult, [BX, BS], [BX])
        tt("pool", XAt, XAt, L["VEC"][gi], ALU.mult, [BX, L["B_VEC"][gi]], [BX])
        tt("pool", XAt, XAt, L["VEC"][bi], ALU.add, [BX, L["B_VEC"][bi]], [BX])

    def tail(i, nk, wo_d, nxt):
        precast_stage(i + 1)
        L = tail_layout(not (nxt is not None and nxt[0] == 'gla'))
        XA, XT, OTB, ACTT, PTB, TMP, FU, FG = (L[k] for k in ("XA", "XT", "OTB", "ACTT", "PTB", "TMP", "FU", "FG"))
        vrot = [0]

        def load_vec(src):
            slot = vrot[0] % 3
            vrot[0] += 1
            dma("sp", L["VEC"][slot], src[i:i + 1, :].partition_broadcast(128), (), [L["B_VEC"][slot]], L["B_VEC"][slot])
            return slot
        dma("sp", convp, convp_d[i], (), [B_convp], B_convp)
        memset("dve", HALO[0], 0.0, [B_halo[0]])
        if nxt is not None and nxt[0] == 'mla':
            load_mla_vecs(nxt[1])
        if nxt is not None and nxt[0] == 'gla':
            load_gla_vecs(nxt[1])
        xsrc = x_d if i == 0 else xres_d
        BG_ = [Buf(), Buf()]
        B_ACTc = [Buf() for _ in range(NFC)]
        cw = v3(convp, 44)
        for blk in range(NBLK):
            hp, hn = HALO[blk % 2], HALO[(blk + 1) % 2]
            Bhp, Bhn = B_halo[blk % 2], B_halo[(blk + 1) % 2]
            t0 = blk * 4
            dma("sp", v3(OTB[:, 0:nk * 512], nk), oT_d[0:nk * 128, blk * 512:(blk + 1) * 512].rearrange("(k p) s -> p k s", p=128),
                [B_oT], [L["B_OTB"]], L["B_OTB"])
            for t4 in range(4):
                dma("pool", XA[:, t4 * 1024:(t4 + 1) * 1024], xsrc[(t0 + t4) * 128:(t0 + t4 + 1) * 128, :],
                    [B_xres] if i > 0 else [], [L["B_XAt"][t4]], L["B_XAt"][t4])
            dma("sp", v3(PTB, 2), pT_b[i][:, blk * 512:(blk + 1) * 512].rearrange("(k p) s -> p k s", p=128), [mirror_buf[pT_b[i].name]], [L["B_PTB"]], L["B_PTB"])
            for half in range(2):
                for kc in range(nk):
                    w, wb = wt(wo_d[kc * 128:(kc + 1) * 128, half * 512:(half + 1) * 512])
                    for t4 in range(4):
                        mm(banks[4 * half + t4], OTB[:, kc * 512 + t4 * 128: kc * 512 + (t4 + 1) * 128], w, kc == 0, kc == nk - 1,
                           [L["B_OTB"], wb], [B_bank[4 * half + t4]])
            vg = load_vec(ln1g_d)
            vb = load_vec(ln1b_d)
            for t4 in range(4):
                for half in range(2):
                    xs = XA[:, t4 * 1024 + half * 512: t4 * 1024 + (half + 1) * 512]
                    stt(xs, xs, ALPHA, banks[4 * half + t4], ALU.mult, ALU.add, [L["B_XAt"][t4], B_bank[4 * half + t4]], [L["B_XAt"][t4]])
                layer_norm(L, t4, vg, vb)
            for t4 in range(4):
                to_featmajor(L, t4, t4)
            hv = v3(hp, 44)
            cr = v3(CORR, 44)
            tmpa = L["ST"][:, 0:44]
            ta = FU[:, 0:44]
            tb2 = FU[:, 64:108]
            tt("dve", ta, hv[:, :, 1], cw[:, :, 1], ALU.mult, [Bhp, B_convp], [L["B_FU"]])
            tt("dve", tb2, hv[:, :, 0], cw[:, :, 0], ALU.mult, [Bhp, B_convp], [L["B_FU"]])
            tt("dve", cr[:, :, 0], ta, tb2, ALU.add, [L["B_FU"]], [B_corr])
            tt("dve", cr[:, :, 1], hv[:, :, 1], cw[:, :, 0], ALU.mult, [Bhp, B_convp], [B_corr])
            for pr in range(11):
                bs_ = 0 if pr % 2 == 0 else 4
                c0 = 2 * pr
                for kc in range(8):
                    wu_, wub = wt(wup_b[i][kc * 128:(kc + 1) * 128, c0 * 128: c0 * 128 + 256], cols=256)
                    wg_, wgb = wt(wup_b[i][kc * 128:(kc + 1) * 128, DFF + c0 * 128: DFF + c0 * 128 + 256], cols=256)
                    for cc in range(2):
                        mm(banks[bs_ + cc], wu_[:, cc * 128:(cc + 1) * 128], XT[:, kc * 512:(kc + 1) * 512], kc == 0, kc == 7,
                           L["B_XTt"] + [wub], [B_bank[bs_ + cc]])
                        mm(banks[bs_ + 2 + cc], wg_[:, cc * 128:(cc + 1) * 128], XT[:, kc * 512:(kc + 1) * 512], kc == 0, kc == 7,
                           L["B_XTt"] + [wgb], [B_bank[bs_ + 2 + cc]])
                for cc in range(2):
                    c = c0 + cc
                    bu, bg = bs_ + cc, bs_ + 2 + cc
                    FUc, BFUc = (FU, L["B_FU"]) if cc == 0 else (L["FU2"], L["B_FU2"])
                    FGc, BFGc = (FG, L["B_FG"]) if cc == 0 else (L["FG2"], L["B_FG2"])
                    for (bk_, ci, Fo, BFo) in ((bu, c, FUc, BFUc), (bg, 22 + c, FGc, BFGc)):
                        pb = banks[bk_]
                        act(Fo, pb, AF.Identity, [B_bank[bk_], B_convp], [BFo], scale=cw[:, ci, 2:3], bias=cw[:, ci, 3:4])
                        stt(Fo[:, 1:512], pb[:, 0:511], cw[:, ci, 1:2], Fo[:, 1:512], ALU.mult, ALU.add, [B_bank[bk_], B_convp, BFo], [BFo])
                        stt(Fo[:, 2:512], pb[:, 0:510], cw[:, ci, 0:1], Fo[:, 2:512], ALU.mult, ALU.add, [B_bank[bk_], B_convp, BFo], [BFo])
                        tt("dve", Fo[:, 0:2], Fo[:, 0:2], cr[:, ci, :], ALU.add, [BFo, B_corr], [BFo])
                        act(v3(hn, 44)[:, ci, :], pb[:, 510:512], AF.Copy, [B_bank[bk_]], [Bhn])
                    act(FGc, FGc, AF.Gelu, [BFGc], [BFGc])
                    tt("pool", ACTT[:, c * 512:(c + 1) * 512], FUc, FGc, ALU.mult, [BFUc, BFGc], [B_ACTc[c]])
            for half in range(2):
                bs = 4 if half == 0 else 0
                for c in range(NFC):
                    w, wb = wt(wdn_b[i][c * 128:(c + 1) * 128, half * 512:(half + 1) * 512])
                    for t4 in range(4):
                        mm(banks[bs + t4], ACTT[:, c * 512 + t4 * 128: c * 512 + (t4 + 1) * 128], w, c == 0, c == NFC - 1,
                           [B_ACTc[c], wb], [B_bank[bs + t4]])
            vg = load_vec(ln2g_d)
            vb = load_vec(ln2b_d)
            for t4 in range(4):
                for half in range(2):
                    bs = 4 if half == 0 else 0
                    xs = XA[:, t4 * 1024 + half * 512: t4 * 1024 + (half + 1) * 512]
                    stt(xs, xs, ALPHA, banks[bs + t4], ALU.mult, ALU.add, [L["B_XAt"][t4], B_bank[bs + t4]], [L["B_XAt"][t4]])
                layer_norm(L, t4, vg, vb)
            for t4 in range(4):
                to_featmajor(L, t4, t4)
            vbg = load_vec(bg_d)
            for half in range(2):
                for kc in range(8):
                    w, wb = wt(wpg_b[i][kc * 128:(kc + 1) * 128, half * 512:(half + 1) * 512])
                    for t4 in range(4):
                        mm(banks[t4], XT[:, kc * 512 + t4 * 128: kc * 512 + (t4 + 1) * 128], w, kc == 0, kc == 7,
                           [L["B_XTt"][t4], wb], [B_bank[t4]])
                for kc in range(2):
                    w, wb = wt(wpp_b[i][kc * 128:(kc + 1) * 128, half * 512:(half + 1) * 512])
                    for t4 in range(4):
                        mm(banks[4 + t4], PTB[:, kc * 512 + t4 * 128: kc * 512 + (t4 + 1) * 128], w, kc == 0, kc == 1,
                           [L["B_PTB"], wb], [B_bank[4 + t4]])
                for t4 in range(4):
                    G = TMP[:, (t4 % 2) * 512:(t4 % 2 + 1) * 512]
                    tt("dve", G, banks[t4], L["VEC"][vbg][:, half * 512:(half + 1) * 512], ALU.add, [B_bank[t4], L["B_VEC"][vbg]], [BG_[t4 % 2]])
                    act(G, G, AF.Sigmoid, [BG_[t4 % 2]], [BG_[t4 % 2]])
                    tt("dve", G, G, banks[4 + t4], ALU.mult, [BG_[t4 % 2], B_bank[4 + t4]], [BG_[t4 % 2]])
                    xs = XA[:, t4 * 1024 + half * 512: t4 * 1024 + (half + 1) * 512]
                    tt("dve", xs, xs, G, ALU.add, [L["B_XAt"][t4], BG_[t4 % 2]], [L["B_XAt"][t4]])
            dst = out_d if nxt is None else xres_d
            Bd = B_out if nxt is None else B_xres
            for t4 in range(4):
                dma("pool", dst[(t0 + t4) * 128:(t0 + t4 + 1) * 128, :], XA[:, t4 * 1024:(t4 + 1) * 1024], [L["B_XAt"][t4]], [Bd], Bd)
            if nxt is not None:
                for t4 in range(4):
                    to_featmajor(L, t4, t4)
                if nxt[0] == 'mla':
                    project_mla(L, nxt[1], blk)
                else:
                    project_gla(L, nxt[1], blk)
        P.barrier()

    def prologue():
        L = tail_layout(True)
        load_mla_vecs(0)
        for blk in range(NBLK):
            for t4 in range(4):
                t = blk * 4 + t4
                dma("sp", L["XA"][:, t4 * 1024:(t4 + 1) * 1024], x_d[t * 128:(t + 1) * 128, :], (), [L["B_XAt"][t4]], L["B_XAt"][t4])
            for t4 in range(4):
                to_featmajor(L, t4, t4)
            project_mla(L, 0, blk)
        P.barrier()

    def mla_attention(j):
        cv = Carver(arena)
        QN = [cv.bf(S) for _ in range(2)]; QR = [cv.bf(S) for _ in range(2)]; KN = [cv.bf(S) for _ in range(2)]
        B_QN = [[Buf() for _ in range(NBLK)] for _ in range(2)]
        B_QR = [[Buf() for _ in range(NBLK)] for _ in range(2)]
        B_KN = [[Buf() for _ in range(NBLK)] for _ in range(2)]
        V4 = cv.bf(NT * 512); B_V4 = [Buf() for _ in range(NT)]
        PT = [cv.bf(512) for _ in range(4)]; B_PT = [Buf() for _ in range(4)]
        ACCp = [cv.f32(512) for _ in range(2)]; B_ACCp = [Buf(), Buf()]
        ACCd = [cv.f32(512) for _ in range(2)]; B_ACCd = [Buf(), Buf()]
        RS = cv.f32(512); B_RS = Buf()
        OTs = [cv.bf(512) for _ in range(2)]; B_OTs = [Buf(), Buf()]
        ptc = [0]
        sbc = [0]
        for g in range(4):
            cs = slice(g * 512, (g + 1) * 512)
            WQN = [wt(wqn_b[j][kc * 128:(kc + 1) * 128, cs]) for kc in range(4)]
            WQR = [wt(wqr_b[j][kc * 128:(kc + 1) * 128, cs]) for kc in range(4)]
            WUK = [wt(wuk_b[j][kc * 128:(kc + 1) * 128, cs]) for kc in range(2)]
            WUV = [wt(wuv_b[j][kc * 128:(kc + 1) * 128, cs]) for kc in range(2)]
            for t in range(NT):
                pb = 6 + (t % 2)
                for kc in range(2):
                    mm(banks[pb], ckvT[:, kc * S + t * 128: kc * S + (t + 1) * 128], WUV[kc][0], kc == 0, kc == 1,
                       [B_ckv[t // 4], WUV[kc][1]], [B_bank[pb]])
                act(V4[:, t * 512:(t + 1) * 512], banks[pb], AF.Copy, [B_bank[pb]], [B_V4[t]])
            for hh in range(4):
                h = g * 4 + hh
                par = h % 2
                hs = slice(hh * 128, (hh + 1) * 128)
                for tb in range(NBLK):
                    tsl = slice(tb * 512, (tb + 1) * 512)
                    for kc in range(4):
                        mm(banks[6], WQN[kc][0][:, hs], cqT[:, kc * S + tb * 512: kc * S + (tb + 1) * 512], kc == 0, kc == 3,
                           [B_cq[tb], WQN[kc][1]], [B_bank[6]])
                    act(QN[par][:, tsl], banks[6], AF.Copy, [B_bank[6]], [B_QN[par][tb]])
                    for kc in range(4):
                        mm(banks[7], WQR[kc][0][:, hs], cqT[:, kc * S + tb * 512: kc * S + (tb + 1) * 512], kc == 0, kc == 3,
                           [B_cq[tb], WQR[kc][1]], [B_bank[7]])
                    tt("dve", QR[par][:, tsl], banks[7], T2[:, tsl], ALU.mult, [B_bank[7], B_T2], [B_QR[par][tb]])
                    for kc in range(2):
                        mm(banks[6], WUK[kc][0][:, hs], ckvT[:, kc * S + tb * 512: kc * S + (tb + 1) * 512], kc == 0, kc == 1,
                           [B_ckv[tb], WUK[kc][1]], [B_bank[6]])
                    act(KN[par][:, tsl], banks[6], AF.Copy, [B_bank[6]], [B_KN[par][tb]])
                items = []
                for qt in range(NBLK):
                    for jk in range(4 * qt + 4):
                        items.append((qt, jk, 4 * qt + 4))

                def emit_scores(it):
                    qt, jk, nj = it
                    r = jk - 4 * qt
                    q0 = max(r, 0) * 128
                    n = 512 - q0
                    sb = sbc[0] % 4
                    sbc[0] += 1
                    qsl = slice(qt * 512 + q0, qt * 512 + 512)
                    kb = jk // 4
                    mm(banks[sb][:, 0:n], KN[par][:, jk * 128:(jk + 1) * 128], QN[par][:, qsl], True, False,
                       [B_KN[par][kb], B_QN[par][qt]], [B_bank[sb]])
                    mm(banks[sb][:, 0:n], kr2T[:, jk * 128:(jk + 1) * 128], QR[par][:, qsl], False, True,
                       [B_kr[kb], B_QR[par][qt]], [B_bank[sb]])
                    return sb

                def epilogue(qt):
                    ob = 4 + (qt % 2)
                    ap_ = qt % 2
                    mm(banks[6], ones_f, ACCp[ap_], True, False, [B_ones, B_ACCp[ap_]], [B_bank[6]])
                    mm(banks[6], ones_f, ACCd[ap_], False, True, [B_ones, B_ACCd[ap_]], [B_bank[6]])
                    act(RS, banks[6], AF.Ln, [B_bank[6]], [B_RS])
                    act(RS, RS, AF.Exp, [B_RS], [B_RS], scale=-1.0)
                    tt("dve", OTs[ap_], banks[ob], RS, ALU.mult, [B_bank[ob], B_RS], [B_OTs[ap_]])
                    dma("sp", oT_d[h * 128:(h + 1) * 128, qt * 512:(qt + 1) * 512], OTs[ap_], [B_OTs[ap_]], [B_oT], B_oT)

                pending = None
                sbq = [emit_scores(items[0]), emit_scores(items[1])]
                for idx, it in enumerate(items):
                    sb = sbq.pop(0)
                    if idx + 2 < len(items):
                        sbq.append(emit_scores(items[idx + 2]))
                    qt, jk, nj = it
                    ob = 4 + (qt % 2)
                    ap_ = qt % 2
                    if jk == 0:
                        memset("pool", ACCp[ap_], 0.0, [B_ACCp[ap_]])
                        memset("dve", ACCd[ap_], 0.0, [B_ACCd[ap_]])
                    r = jk - 4 * qt
                    q0 = max(r, 0) * 128
                    n = 512 - q0
                    pt = ptc[0] % 4
                    ptc[0] += 1
                    act(PT[pt][:, 0:n], banks[sb][:, 0:n], AF.Exp, [B_bank[sb]], [B_PT[pt]], scale=ATT_SCALE)
                    eng = "pool" if jk % 3 == 0 else "dve"
                    ACC, BACC = (ACCp[ap_], B_ACCp[ap_]) if jk % 3 == 0 else (ACCd[ap_], B_ACCd[ap_])
                    if r >= 0:
                        tt(eng, PT[pt][:, 0:128], PT[pt][:, 0:128], MASK[:, jk * 128:(jk + 1) * 128], ALU.mult,
                           [B_PT[pt], B_MASK], [B_PT[pt]])
                    mm(banks[ob][:, q0:512], V4[:, jk * 512 + hh * 128: jk * 512 + (hh + 1) * 128], PT[pt][:, 0:n],
                       jk == 0, jk == nj - 1, [B_V4[jk], B_PT[pt]], [B_bank[ob]])
                    tt(eng, ACC[:, q0:512], ACC[:, q0:512], PT[pt][:, 0:n], ALU.add, [BACC, B_PT[pt]], [BACC])
                    if pending is not None:
                        pending[1] -= 1
                        if pending[1] == 0:
                            epilogue(pending[0])
                            pending = None
                    if jk == nj - 1:
                        if pending is not None:
                            epilogue(pending[0])
                        pending = [qt, 2]
                if pending is not None:
                    epilogue(pending[0])
        P.barrier()

    def gla_mixer(j):
        cv = Carver(arena)
        St = cv.f32(1024); B_St = Buf()
        Sb = cv.bf(1024); B_Sb = Buf()
        QT = [cv.bf(4 * 512) for _ in range(2)]; B_QT = [Buf(), Buf()]
        KC = [cv.bf(8 * 512) for _ in range(2)]; B_KC = [Buf(), Buf()]
        VC = [cv.bf(8 * 1024) for _ in range(2)]; B_VC = [Buf(), Buf()]
        SRt = [cv.f32(1024) for _ in range(2)]; B_SRt = [Buf(), Buf()]
        ONs = [cv.f32(1024) for _ in range(2)]; B_ONs = [Buf(), Buf()]
        OG = cv.bf(1024); B_OG = Buf()
        OGT = [cv.bf(1024) for _ in range(2)]; B_OGT = [Buf(), Buf()]
        onb = cv.f32(1024); B_onb = Buf()
        STG = cv.f32(64); B_STG = Buf()
        dma("sp", onb, gla_on_d[j:j + 1, :].partition_broadcast(128), (), [B_onb], B_onb)
        memset("dve", St, 0.0, [B_St])
        dv = v3(DECAY, 4)
        for blk in range(NBLK):
            p2 = blk % 2
            dma("sp", v3(QT[p2], 4), gq_d[:, blk * 512:(blk + 1) * 512].rearrange("(h p) s -> p h s", p=128), [B_gq], [B_QT[p2]], B_QT[p2])
            dma("sp", v3(KC[p2][0:64, :], 8), gk_d[blk * 512:(blk + 1) * 512, :].rearrange("(c p) f -> p c f", p=64), [B_gk], [B_KC[p2]], B_KC[p2])
            dma("sp", v3(VC[p2][0:64, :], 8), gv_d[blk * 512:(blk + 1) * 512, :].rearrange("(c p) f -> p c f", p=64), [B_gv], [B_VC[p2]], B_VC[p2])
            for t4 in range(4):
                t = blk * 4 + t4
                s2 = t % 2
                dma("sp", SRt[s2], gsr_d[t * 128:(t + 1) * 128, :], [B_gsr], [B_SRt[s2]], B_SRt[s2])
                ON, B_ON = ONs[t % 2], B_ONs[t % 2]
                for ch in range(2):
                    c = t4 * 2 + ch
                    n = t * 2 + ch
                    for h in range(4):
                        bk_ = h // 2
                        mm(banks[bk_][:, (h % 2) * 256:(h % 2 + 1) * 256], KC[p2][0:64, c * 512 + h * 128: c * 512 + (h + 1) * 128],
                           VC[p2][0:64, c * 1024 + h * 256: c * 1024 + (h + 1) * 256], True, True, [B_KC[p2], B_VC[p2]], [B_bank[bk_]])
                    for h in range(4):
                        bk_ = h // 2
                        stt(St[:, h * 256:(h + 1) * 256], St[:, h * 256:(h + 1) * 256], dv[:, h, n:n + 1],
                            banks[bk_][:, (h % 2) * 256:(h % 2 + 1) * 256], ALU.mult, ALU.add, [B_St, B_decay, B_bank[bk_]], [B_St])
                    act(Sb, St, AF.Copy, [B_St], [B_Sb])
                    ob = 2 + 2 * ch
                    for h in range(4):
                        bk_ = ob + h // 2
                        mm(banks[bk_][:, (h % 2) * 256:(h % 2 + 1) * 256], QT[p2][:, h * 512 + t4 * 128: h * 512 + (t4 + 1) * 128],
                           Sb[:, h * 256:(h + 1) * 256], True, True, [B_QT[p2], B_Sb], [B_bank[bk_]])
                    rs_ = slice(ch * 64, (ch + 1) * 64)
                    for hb in range(2):
                        act(ON[rs_, hb * 512:(hb + 1) * 512], banks[ob + hb][rs_, :], AF.Copy, [B_bank[ob + hb]], [B_ON])
                for h in range(4):
                    P.op("dve", lambda e, h=h, ON=ON: e.bn_stats(out=STG[:, h * 6:(h + 1) * 6], in_=ON[:, h * 256:(h + 1) * 256]), [B_ON], [B_STG])
                for h in range(4):
                    P.op("dve", lambda e, h=h: e.bn_aggr(out=STG[:, 24 + 2 * h:26 + 2 * h], in_=STG[:, h * 6:(h + 1) * 6]), [B_STG], [B_STG])
                varv = STG[:, 24:32].rearrange("p (h two) -> p h two", two=2)[:, :, 1]
                ts("dve", STG[:, 32:36], varv, EPS, None, ALU.add, None, [B_STG], [B_STG])
                act(STG[:, 36:40], STG[:, 32:36], AF.Sqrt, [B_STG], [B_STG])
                P.op("dve", lambda e: e.reciprocal(out=STG[:, 40:44], in_=STG[:, 36:40]), [B_STG], [B_STG])
                for h in range(4):
                    ts("dve", ON[:, h * 256:(h + 1) * 256], ON[:, h * 256:(h + 1) * 256], STG[:, 24 + 2 * h:25 + 2 * h], STG[:, 40 + h:41 + h],
                       ALU.subtract, ALU.mult, [B_ON, B_STG], [B_ON])
                tt("pool", ON, ON, onb, ALU.mult, [B_ON, B_onb], [B_ON])
                tt("pool", OG, ON, SRt[s2], ALU.mult, [B_ON, B_SRt[s2]], [B_OG])
                tb_ = banks_bf[6 + (t % 2)]
                Btb = B_bank[6 + (t % 2)]
                for kc in range(8):
                    tr(tb_[:, kc * 128:(kc + 1) * 128], OG[:, kc * 128:(kc + 1) * 128], ident_b, [B_OG, B_identb], [Btb])
                act(OGT[s2], tb_, AF.Copy, [Btb], [B_OGT[s2]])
                dma("sp", oT_d[0:1024, t * 128:(t + 1) * 128].rearrange("(k p) s -> p k s", p=128), v3(OGT[s2], 8),
                    [B_OGT[s2]], [B_oT], B_oT)
        P.barrier()

    precast_stage(0)
    prologue()
    for i in range(nlayers):
        j = i // 2
        last = (i == nlayers - 1)
        if i % 2 == 0:
            mla_attention(j)
            nxt = None if last else ('gla', j)
            tail(i, 16, mla_wo_b[j], nxt)
        else:
            gla_mixer(j)
            nxt = None if last else ('mla', j + 1)
            tail(i, 8, gla_wo_b[j], nxt)
    P.barrier()
    P.emit()
    st.close()
    return nc, P


_CACHE = {}


def _host_layout(inputs, b):
    f32 = np.float32
    d = {}
    d["x"] = np.ascontiguousarray(inputs["x"][b], dtype=f32)
    d["pT"] = np.ascontiguousarray(np.transpose(inputs["p"][:, b], (0, 2, 1)), dtype=f32)
    pos = np.asarray(inputs["positions"][b], dtype=np.int32)
    d["pos_row"] = np.ascontiguousarray(pos.reshape(1, S))
    d["pos_col"] = np.ascontiguousarray(pos.reshape(NT, 128).T)
    return d


def _shared_layout(inputs):
    f32 = np.float32
    d = {}
    inv = (1.0 / (10000.0 ** (np.arange(0, 64, 2, dtype=f32) / f32(64)))).astype(f32)
    invt = (inv.astype(np.float64) / (2 * np.pi)).astype(f32)
    ropecol = np.zeros((128, 2), f32)
    for p in range(128):
        ropecol[p, 0] = invt[p % 32]
        ropecol[p, 1] = 0.25 if p < 64 else (0.5 if p < 96 else 0.0)
    d["ropecol"] = ropecol
    row = np.concatenate([invt, invt, invt, invt]).astype(f32)
    d["inv2"] = np.ascontiguousarray(np.broadcast_to(row, (128, 128)))
    ph = np.concatenate([np.full(64, 0.25, f32), np.zeros(64, f32)])
    d["ph2"] = np.ascontiguousarray(np.broadcast_to(ph, (128, 128)))
    for k in ("mla_w_in", "mla_q_norm", "mla_kv_norm", "mla_w_uk", "mla_w_uv", "mla_w_o", "gla_w_in", "gla_o_norm", "gla_w_o",
              "ln1_g", "ln1_b", "ln2_g", "ln2_b", "ffn_w_up", "ffn_w_down", "ple_w_proj", "ple_w_gate", "ple_b_gate"):
        d[k] = np.ascontiguousarray(inputs[k], dtype=f32)
    wuq = np.asarray(inputs["mla_w_uq"], dtype=f32).reshape(2, 512, 16, 192)
    d["wq_n"] = np.ascontiguousarray(wuq[..., 0:128].reshape(2, 512, 2048))
    wr = wuq[..., 128:192]
    wr2 = np.concatenate([wr, wr[..., 32:64], wr[..., 0:32]], axis=-1)
    d["wq_r"] = np.ascontiguousarray(wr2.reshape(2, 512, 2048))
    d["gla_wa2x"] = np.ascontiguousarray(np.concatenate([np.asarray(inputs["gla_w_a2"], f32),
                                                          np.asarray(inputs["gla_b_a"], f32)[:, None, :]], axis=1))
    cw = np.asarray(inputs["ffn_conv_w"], f32)
    cb = np.asarray(inputs["ffn_conv_b"], f32)
    cat = np.concatenate([cw, cb[:, None, :]], axis=1)
    cat = cat.reshape(4, 4, 44, 128)
    d["convp"] = np.ascontiguousarray(np.transpose(cat, (0, 3, 2, 1)).reshape(4, 128, 176))
    return d


def kernel(**inputs):
    if "nc" not in _CACHE:
        _CACHE["nc"] = build()[0]
    nc = _CACHE["nc"]
    shared = _shared_layout(inputs)
    in_maps = []
    for b in range(8):
        m = dict(shared)
        m.update(_host_layout(inputs, b))
        in_maps.append(m)
    res = run_bass_kernel_spmd(nc, in_maps, core_ids=list(range(8)))
    out = np.stack([np.asarray(r["out"], dtype=np.float32) for r in res.results], axis=0)
    return out
```
